# Optimizing a Trainium2 kernel written in Bass

```python
import jax, jax.numpy as jnp
from jax import lax
import numpy as np

D_MODEL = 1024
BATCH = 8
SEQ = 4096
DEPTH = 4

GRID_W = 64
CTX_LEN = 256
N_EVEN = (DEPTH + 1) // 2
N_ODD = DEPTH // 2
HEAD_DIM_A = 64
E_A = D_MODEL
HEADS_A = E_A // HEAD_DIM_A
E_B = D_MODEL
LORA_W = 64
LORA_A = 64
SHORT_CONV = 3
E_C = 2 * D_MODEL
CONF_CONV = 31
PROJ_EVEN = 4 * E_A + 4 * E_B
EVEN_SPLITS = [E_A, 2 * E_A, 3 * E_A, 4 * E_A, 4 * E_A + E_B, 4 * E_A + 2 * E_B, 4 * E_A + 3 * E_B]
RMS_EPS = 1e-6
LN_EPS = 1e-5
GN_EPS = 64e-5

kernel_name = "hybrid_rwkv7_shortconv_conformer_dit"


def rms_norm(x, g):
    xf = x.astype(jnp.float32)
    y = xf * lax.rsqrt(jnp.mean(jnp.square(xf), axis=-1, keepdims=True) + RMS_EPS)
    return (y * g.astype(jnp.float32)).astype(x.dtype)


def layer_norm(x, g, b):
    xf = x.astype(jnp.float32)
    mu = jnp.mean(xf, axis=-1, keepdims=True)
    var = jnp.mean(jnp.square(xf - mu), axis=-1, keepdims=True)
    return ((xf - mu) * lax.rsqrt(var + LN_EPS) * g + b).astype(x.dtype)


def head_group_norm(o, g, b):
    mu = jnp.mean(o, axis=-1, keepdims=True)
    var = jnp.mean(jnp.square(o - mu), axis=-1, keepdims=True)
    y = (o - mu) * lax.rsqrt(var + GN_EPS)
    return y.reshape(o.shape[:-2] + (-1,)) * g + b


def centred_shift(x):
    p = jnp.pad(x, ((0, 0), (1, 1), (0, 0)))
    return 0.5 * (p[:, :-2] + p[:, 2:])


def to_order(h, rows, col):
    if not col:
        return h
    b, t, ch = h.shape
    return h.reshape(b, rows, GRID_W, ch).swapaxes(1, 2).reshape(b, t, ch)


def from_order(h, rows, col):
    if not col:
        return h
    b, t, ch = h.shape
    return h.reshape(b, GRID_W, rows, ch).swapaxes(1, 2).reshape(b, t, ch)


def dwconv(x, w, seg_len):
    b, t, ch = x.shape
    k = w.shape[0]
    xs = x.reshape(b * (t // seg_len), seg_len, ch)
    y = lax.conv_general_dilated(xs, w[:, None, :].astype(x.dtype), window_strides=(1,),
                                 padding=[(k // 2, k // 2)],
                                 dimension_numbers=("NWC", "WIO", "NWC"),
                                 feature_group_count=ch)
    return y.reshape(b, t, ch)


def wkv_scan(S0, r, decay, k, v, kk, a, reverse):
    def step(S, inp):
        r_t, w_t, k_t, v_t, kk_t, a_t = inp
        sa = jnp.einsum("bhvk,bhk->bhv", S, -kk_t)
        S = (S * w_t[:, :, None, :]
             + sa[..., None] * (kk_t * a_t)[:, :, None, :]
             + v_t[..., None] * k_t[:, :, None, :])
        return S, jnp.einsum("bhvk,bhk->bhv", S, r_t)
    S, out = lax.scan(step, S0, (r, decay, k, v, kk, a), reverse=reverse)
    return out, S


def rwkv_prep(h, r, k, v, mu_rkv, mu_wa, w0, w1, w2, a0, a1, a2, k_k, k_a):
    f32 = jnp.float32
    b, t, _ = h.shape
    dh = centred_shift(h) - h
    xw = h + dh * mu_wa[0]
    xa = h + dh * mu_wa[1]
    r = r + (centred_shift(r) - r) * mu_rkv[0]
    k = k + (centred_shift(k) - k) * mu_rkv[1]
    v = v + (centred_shift(v) - v) * mu_rkv[2]
    lw = jnp.einsum("zbtr,zre->zbte", jnp.tanh(jnp.einsum("btd,zdr->zbtr", xw, w1)), w2)
    w_log = -jax.nn.softplus(-(w0[:, None, None, :] + lw).astype(f32)) - 0.5
    decay = jnp.exp(-jnp.exp(w_log))
    la = jnp.einsum("zbtr,zre->zbte", jnp.einsum("btd,zdr->zbtr", xa, a1), a2)
    a = jax.nn.sigmoid((a0[:, None, None, :] + la).astype(f32))
    k = k.astype(f32)
    kk = (k * k_k).reshape(b, t, HEADS_A, HEAD_DIM_A)
    kk = kk * lax.rsqrt(jnp.maximum(jnp.sum(kk * kk, axis=-1, keepdims=True), 1e-24))
    kd = k[None] * (1.0 + (a - 1.0) * k_a)
    heads = lambda z: z.reshape(z.shape[:-1] + (HEADS_A, HEAD_DIM_A))
    return heads(r.astype(f32)), heads(v.astype(f32)), kk, heads(decay), heads(kd), heads(a)


def rwkv_seq(prep, S0_f, S0_b, r_k, lnx_w, lnx_b, with_output):
    r, v, kk, decay, kd, a = prep
    tm = lambda z: jnp.swapaxes(z, 0, 1)
    rt, vt, kkt = tm(r), tm(v), tm(kk)
    out_f, S_f = wkv_scan(S0_f, rt, tm(decay[0]), tm(kd[0]), vt, kkt, tm(a[0]), False)
    out_b, S_b = wkv_scan(S0_b, rt, tm(decay[1]), tm(kd[1]), vt, kkt, tm(a[1]), True)
    if not with_output:
        return None, S_f, S_b
    o = tm(out_f + out_b)
    bonus = jnp.sum(r[None] * kd * r_k[:, None, None], axis=-1, keepdims=True) * v[None]
    y = head_group_norm(o, lnx_w, lnx_b) + bonus.sum(0).reshape(o.shape[:-2] + (-1,))
    return y, S_f, S_b


def even_mixer(h, hc, seg_len, ctx_out, w_in, w_out, mu_rkv, mu_wa, w0, w1, w2, a0, a1, a2,
               k_k, k_a, r_k, lnx_w, lnx_b, sc_w):
    rw = (mu_rkv, mu_wa, w0, w1, w2, a0, a1, a2, k_k, k_a)
    b = h.shape[0]
    zero = jnp.zeros((b, HEADS_A, HEAD_DIM_A, HEAD_DIM_A), jnp.float32)
    rc, kc, vc, zac, bc, cc, xc, zbc = jnp.split(hc @ w_in, EVEN_SPLITS, axis=-1)
    yac, S_f, S_b = rwkv_seq(rwkv_prep(hc, rc, kc, vc, *rw), zero, zero, r_k, lnx_w, lnx_b, ctx_out)
    r, k, v, za, bg, cg, xb, zb = jnp.split(h @ w_in, EVEN_SPLITS, axis=-1)
    ya, _, _ = rwkv_seq(rwkv_prep(h, r, k, v, *rw), S_f, S_b, r_k, lnx_w, lnx_b, True)
    yb = bg * dwconv(cg * xb, sc_w, seg_len)
    y = jnp.concatenate([ya.astype(h.dtype) * jax.nn.silu(za), yb * jax.nn.silu(zb)], axis=-1) @ w_out
    yc = None
    if ctx_out:
        ybc = bc * dwconv(cc * xc, sc_w, hc.shape[1])
        yc = jnp.concatenate([yac.astype(hc.dtype) * jax.nn.silu(zac), ybc * jax.nn.silu(zbc)], axis=-1) @ w_out
    return y, yc


def odd_mixer(h, seg_len, w_in, dw_w, dw_b, ln_w, ln_b, w_out):
    u, gl, z = jnp.split(h @ w_in, 3, axis=-1)
    a = u * jax.nn.sigmoid(gl)
    a = dwconv(a, dw_w, seg_len) + dw_b
    a = layer_norm(a, ln_w, ln_b)
    return (jax.nn.silu(a) * jax.nn.silu(z)) @ w_out


def setup_inputs(seed: int = 0) -> dict:
    key = jax.random.key(seed)
    ks = iter(jax.random.split(key, 40))
    nrm = lambda shape, s: jax.random.normal(next(ks), shape, jnp.float32) * s
    uni = lambda shape, lo, hi: jax.random.uniform(next(ks), shape, jnp.float32, lo, hi)
    D = D_MODEL
    return {
        "x": nrm((BATCH, SEQ, D), 1.0),
        "c": nrm((BATCH, D), 1.0),
        "ctx": nrm((BATCH, CTX_LEN, D), 1.0),
        "c_ctx": nrm((D,), 1.0),
        "mod_w": nrm((DEPTH, D, 3 * D), 0.5 * D ** -0.5),
        "mod_b": nrm((DEPTH, 3 * D), 0.01),
        "g_pre": 1.0 + nrm((DEPTH, D), 0.02),
        "g_post": 1.0 + nrm((DEPTH, D), 0.02),
        "ev_w_in": nrm((N_EVEN, D, PROJ_EVEN), D ** -0.5),
        "ev_w_out": nrm((N_EVEN, E_A + E_B, D), (E_A + E_B) ** -0.5),
        "ev_mu_rkv": uni((N_EVEN, 3, E_A), 0.0, 1.0),
        "ev_mu_wa": uni((N_EVEN, 2, D), 0.0, 1.0),
        "ev_w0": uni((N_EVEN, 2, E_A), -6.0, -0.5),
        "ev_w1": nrm((N_EVEN, 2, D, LORA_W), D ** -0.5),
        "ev_w2": nrm((N_EVEN, 2, LORA_W, E_A), 0.5 * LORA_W ** -0.5),
        "ev_a0": nrm((N_EVEN, 2, E_A), 0.1),
        "ev_a1": nrm((N_EVEN, 2, D, LORA_A), D ** -0.5),
        "ev_a2": nrm((N_EVEN, 2, LORA_A, E_A), 0.5 * LORA_A ** -0.5),
        "ev_k_k": 0.85 + nrm((N_EVEN, E_A), 0.05),
        "ev_k_a": 1.0 + nrm((N_EVEN, E_A), 0.05),
        "ev_r_k": nrm((N_EVEN, 2, HEADS_A, HEAD_DIM_A), 0.1),
        "ev_lnx_w": 1.0 + nrm((N_EVEN, E_A), 0.02),
        "ev_lnx_b": nrm((N_EVEN, E_A), 0.01),
        "ev_sc_w": nrm((N_EVEN, SHORT_CONV, E_B), SHORT_CONV ** -0.5),
        "od_w_in": nrm((N_ODD, D, 3 * E_C), D ** -0.5),
        "od_dw_w": nrm((N_ODD, CONF_CONV, E_C), CONF_CONV ** -0.5),
        "od_dw_b": nrm((N_ODD, E_C), 0.01),
        "od_ln_w": 1.0 + nrm((N_ODD, E_C), 0.02),
        "od_ln_b": nrm((N_ODD, E_C), 0.01),
        "od_w_out": nrm((N_ODD, E_C, D), E_C ** -0.5),
    }


def reference(x, c, ctx, c_ctx, mod_w, mod_b, g_pre, g_post, ev_w_in, ev_w_out, ev_mu_rkv, ev_mu_wa,
              ev_w0, ev_w1, ev_w2, ev_a0, ev_a1, ev_a2, ev_k_k, ev_k_a, ev_r_k, ev_lnx_w, ev_lnx_b,
              ev_sc_w, od_w_in, od_dw_w, od_dw_b, od_ln_w, od_ln_b, od_w_out):
    b, t, _ = x.shape
    rows = t // GRID_W
    n_ctx = ctx.shape[1]
    s_lat = jax.nn.silu(c)
    s_ctx = jax.nn.silu(c_ctx)
    for i in range(DEPTH):
        j = i // 2
        last = i == DEPTH - 1
        even = i % 2 == 0
        col = j % 2 == 1
        seg = rows if col else GRID_W
        shift, scale, gate = jnp.split(s_lat @ mod_w[i] + mod_b[i], 3, axis=-1)
        h = rms_norm(x, g_pre[i]) * (1.0 + scale[:, None]) + shift[:, None]
        h = to_order(h, rows, col)
        need_ctx = even or not last
        if need_ctx:
            shift_c, scale_c, gate_c = jnp.split(s_ctx @ mod_w[i] + mod_b[i], 3, axis=-1)
            hc = rms_norm(ctx, g_pre[i]) * (1.0 + scale_c) + shift_c
        if even:
            y, yc = even_mixer(h, hc, seg, not last, ev_w_in[j], ev_w_out[j], ev_mu_rkv[j], ev_mu_wa[j],
                               ev_w0[j], ev_w1[j], ev_w2[j], ev_a0[j], ev_a1[j], ev_a2[j], ev_k_k[j],
                               ev_k_a[j], ev_r_k[j], ev_lnx_w[j], ev_lnx_b[j], ev_sc_w[j])
        else:
            y = odd_mixer(h, seg, od_w_in[j], od_dw_w[j], od_dw_b[j], od_ln_w[j], od_ln_b[j], od_w_out[j])
            yc = None
            if not last:
                yc = odd_mixer(hc, n_ctx, od_w_in[j], od_dw_w[j], od_dw_b[j], od_ln_w[j], od_ln_b[j], od_w_out[j])
        x = x + gate[:, None] * rms_norm(from_order(y, rows, col), g_post[i])
        if not last:
            ctx = ctx + gate_c * rms_norm(yc, g_post[i])
    return x
```

```python
import contextlib
import math
import numpy as np
import concourse.bass as bass
import concourse.mybir as mybir
from concourse.bass_utils import run_bass_kernel_spmd

F32 = mybir.dt.float32
F32R = mybir.dt.float32r
USE_R = True
AF = mybir.ActivationFunctionType
ALU = mybir.AluOpType

D = 1024
NT = 256
CH = 64
NDMASEM = 32
SAME_ENGINE_SYNC = True
DECAY_SCALE = -math.exp(-0.5)


class Rec:
    def __init__(self):
        self.calls = []

    def __getattr__(self, name):
        def f(*a, **k):
            self.calls.append((name, a, k))
            return self
        return f


def _replay(calls, engine):
    ins = None
    for name, a, k in calls:
        ins = getattr(engine, name)(*a, **k)
    return ins


class Sched:
    ENG = ("pe", "dve", "act", "pool", "sp")
    OBJ = {"pe": "tensor", "dve": "vector", "act": "scalar", "pool": "gpsimd", "sp": "sync"}

    def __init__(self, nc, stack):
        self.nc = nc
        self.prog = {e: [] for e in self.ENG}
        self.sem = {e: stack.enter_context(nc.semaphore("s_" + e)) for e in self.ENG if e != "sp"}
        self.cnt = {e: 0 for e in self.ENG}
        self.dsem = [stack.enter_context(nc.semaphore("d%d" % i)) for i in range(NDMASEM)]
        self.dcnt = [0] * NDMASEM
        self.dnext = 0
        self.dnext2 = 0
        self.waited = {e: {} for e in self.ENG}
        self.lastw = {}
        self.readers = {}
        self.nops = 0
        import os
        self.kstop = int(os.environ.get("KSTOP", "1000"))
        self.kmax = int(os.environ.get("KMAXOPS", "100000000"))
        self.phase = 0

    def _semobj(self, key):
        return self.sem[key] if isinstance(key, str) else self.dsem[key]

    def _deps(self, eng, reads, writes):
        deps = {}

        def add(k, v):
            if deps.get(k, 0) < v:
                deps[k] = v

        for b in reads:
            w = self.lastw.get(b)
            if w is not None:
                add(*w)
        for b in writes:
            w = self.lastw.get(b)
            if w is not None:
                add(*w)
            for r in self.readers.get(b, ()):
                add(*r)
        out = []
        for k, v in deps.items():
            if k == eng and (eng == "pe" or not SAME_ENGINE_SYNC):
                continue
            if self.waited[eng].get(k, 0) >= v:
                continue
            self.waited[eng][k] = v
            out.append((k, v))
        return out

    def _commit(self, token, reads, writes):
        for b in writes:
            self.lastw[b] = token
            self.readers[b] = []
        for b in reads:
            self.readers.setdefault(b, []).append(token)

    def op(self, eng, fn, reads=(), writes=()):
        if self.phase >= self.kstop or self.nops >= self.kmax:
            return
        waits = self._deps(eng, reads, writes)
        self.cnt[eng] += 1
        rec = Rec()
        fn(rec)
        calls = rec.calls
        self.prog[eng].append((waits, (lambda engine, calls=calls: _replay(calls, engine)), (eng, 1)))
        self._commit((eng, self.cnt[eng]), reads, writes)
        self.nops += 1

    def dma(self, fn, reads=(), writes=(), q="sp"):
        if self.phase >= self.kstop or self.nops >= self.kmax:
            return
        half = NDMASEM // 2
        if q == "sp":
            i = self.dnext
            self.dnext = (self.dnext + 1) % half
        else:
            i = half + self.dnext2
            self.dnext2 = (self.dnext2 + 1) % (NDMASEM - half)
        waits = self._deps(q, reads, writes)
        if self.dcnt[i] > 0 and self.waited[q].get(i, 0) < self.dcnt[i]:
            self.waited[q][i] = self.dcnt[i]
            waits.append((i, self.dcnt[i]))
        self.dcnt[i] += 16
        rec = Rec()
        fn(rec)
        calls = rec.calls
        self.prog[q].append((waits, (lambda engine, calls=calls: _replay(calls, engine)), (i, 16)))
        self._commit((i, self.dcnt[i]), reads, writes)
        self.nops += 1

    def end_phase(self):
        self.phase += 1
        if self.phase > self.kstop:
            return
        print("phase", self.phase, "nops", self.nops)
        waits = [(e, self.cnt[e]) for e in self.ENG if e != "sp" and self.cnt[e] > 0]
        waits += [(i, self.dcnt[i]) for i in range(NDMASEM) if self.dcnt[i] > 0]
        self.prog["sp"].append((waits, None, None))
        nc = self.nc
        with nc.Block() as block:
            for e in self.ENG:
                prog = self.prog[e]
                if not prog:
                    continue

                def body(engine, prog=prog):
                    for waits, fn, inc in prog:
                        for k, v in waits:
                            engine.wait_ge(self._semobj(k), v)
                        if fn is None:
                            continue
                        ins = fn(engine)
                        ins.then_inc(self._semobj(inc[0]), inc[1])

                getattr(block, self.OBJ[e])(body)
        self.prog = {e: [] for e in self.ENG}
        self.lastw = {}
        self.readers = {}


class Rot:
    def __init__(self, alloc, name, shape, n):
        self.tiles = [alloc("%s%d" % (name, i), shape) for i in range(n)]
        self.keys = ["%s%d" % (name, i) for i in range(n)]
        self.i = 0

    def get(self):
        t, k = self.tiles[self.i], self.keys[self.i]
        self.i = (self.i + 1) % len(self.tiles)
        return t, k


def build(T, NCTX, depth):
    TT = NCTX + T
    TTP = TT + 4
    NCH = TT // CH
    nc = bass.Bass("TRN2", target_bir_lowering=False)
    dt = lambda name, shape, kind="ExternalInput": nc.dram_tensor(name, shape, F32, kind=kind).ap()
    n_even = (depth + 1) // 2
    n_odd = depth // 2
    x_in = dt("x", [T, D])
    c_in = dt("ctx", [NCTX, D])
    svec_d = dt("svec", [128, 8, 2])
    masks_d = dt("masks", [128, 6, 128])
    sel_d = dt("sel", [2, 2, 128])
    bones_d = dt("bones", [128, 2, 128])
    rmask_d = dt("rmask", [128, NT])
    modw_d = dt("modw", [depth, 128, 8, 3 * D])
    modb_d = dt("modb", [depth, 2, 3 * D])
    gpre_d = dt("gpre", [depth, 2, D])
    gpost_d = dt("gpost", [depth, 2, D])
    if n_even:
        ewin_d = dt("ewin", [n_even, 8, 128, 8, 8, 128])
        ewout_d = dt("ewout", [n_even, 128, 16, D])
        ew1_d = dt("ew1", [n_even, 2, 128, 8, 128])
        ew2_d = dt("ew2", [n_even, 2, 128, D])
        evec_d = dt("evec", [n_even, 128, 8, 17])
        emuwa_d = dt("emuwa", [n_even, 128, 8, 2])
    if n_odd:
        owin_d = dt("owin", [n_odd, 16, 128, 8, 3, 128])
        owout_d = dt("owout", [n_odd, 128, 16, D])
        odww_d = dt("odww", [n_odd, 128, 16, 31])
        ovec_d = dt("ovec", [n_odd, 128, 16, 3])
    y_out = dt("y", [T, D], "ExternalOutput")
    xs = [dt("xsa", [T, D], "Internal"), dt("xsb", [T, D], "Internal")]
    cs = [dt("csa", [NCTX, D], "Internal"), dt("csb", [NCTX, D], "Internal")]
    if n_even:
        hT_d = dt("hTd", [128, 8, TTP], "Internal")
        scn_d = dt("scn", [2, 8, 128, 6, TT], "Internal")
        vtok_d = dt("vtok", [8, NCH, 128, CH], "Internal")
        o_d = dt("od", [2, 8, 128, TT], "Internal")
        misc_d = dt("miscd", [3, 8, 128, TT], "Internal")

    with contextlib.ExitStack() as gs:
        S = Sched(nc, gs)

        uid = {"n": 0}

        def mk_alloc(stack):
            def alloc(name, shape):
                uid["n"] += 1
                return stack.enter_context(nc.sbuf_tensor("sb%d_%s" % (uid["n"], name), shape, F32))
            return alloc

        galloc = mk_alloc(gs)
        RR = (lambda ap: ap.bitcast(F32R)) if USE_R else (lambda ap: ap)
        WQ = "pool" if USE_R else "sp"
        masks = galloc("masks", [128, 6, 128])
        sel = galloc("sel", [2, 2, 128])
        bones = galloc("bones", [128, 2, 128])
        rmask = galloc("rmask", [128, NT])
        bonesr = galloc("bonesr", [128, 128])
        ssil = galloc("ssil", [128, 8, 2])
        bc = {w: [galloc("bc%s%d" % (w, k), [128, D]) for k in range(3)] for w in ("l", "c")}
        PCt = [None]
        zero = galloc("zero", [128, 8, 2])
        pst = [gs.enter_context(nc.psum_tensor("ps%d" % i, [128, 512], F32)) for i in range(8)]
        ps_state = {"i": 0}

        ps_state["nrot"] = 6

        def ps():
            i = ps_state["i"] % ps_state["nrot"]
            ps_state["i"] = (i + 1) % ps_state["nrot"]
            return pst[i][:, 0:256], "ps%d" % i

        def psbank():
            i = ps_state["i"] % ps_state["nrot"]
            ps_state["i"] = (i + 1) % ps_state["nrot"]
            return pst[i], ("ps%d" % i,)

        def accbank(k):
            return pst[6 + k][:, 0:256], "accbank%d" % k

        S.dma(lambda e: e.dma_start(out=masks[:], in_=masks_d), writes=["masks"])
        S.dma(lambda e: e.dma_start(out=sel[:], in_=sel_d), writes=["sel"])
        S.dma(lambda e: e.dma_start(out=bones[:], in_=bones_d), writes=["bones"])
        S.dma(lambda e: e.dma_start(out=rmask[:], in_=rmask_d), writes=["rmask"])
        S.dma(lambda e: e.dma_start(out=RR(bonesr[:]), in_=bones_d[:, 0, :]), writes=["bonesr"], q=WQ)
        S.dma(lambda e: e.dma_start(out=ssil[:], in_=svec_d), writes=["ssil"])
        S.op("act", lambda e: e.activation(out=ssil[:], in_=ssil[:], func=AF.Silu), reads=["ssil"], writes=["ssil"])
        S.op("dve", lambda e: e.memset(zero[:], 0.0), writes=["zero"])
        if n_even:
            for col in (0, NCTX + 1, NCTX + 2, NCTX + 3 + T):
                S.dma(lambda e, col=col: e.dma_start(out=hT_d[:, :, col:col + 1], in_=zero[:, :, 0:1], allow_slow_non_contiguous=True),
                      reads=["zero"], writes=[("hTdz", col)])
        S.end_phase()

        ident = masks[:, 5, :]
        BD = masks[:, 4, :]

        def seq_tiles(with_ctx):
            tl = []
            if with_ctx:
                for t0 in range(0, NCTX, NT):
                    tl.append(("c", t0))
            for t0 in range(0, T, NT):
                tl.append(("l", t0))
            return tl

        def goff(seq, t0):
            return t0 if seq == "c" else NCTX + t0

        def poff(seq, t0):
            return 1 + t0 if seq == "c" else NCTX + 3 + t0

        def x_rows(src, seq, t0, sub, col):
            p0 = t0 + sub * 128
            if seq == "c" or not col:
                return [((0, 128), src[p0:p0 + 128, :])]
            v = src.rearrange("(r c) d -> c r d", c=64)
            c0 = p0 // 64
            return [((0, 64), v[c0]), ((64, 128), v[c0 + 1])]

        def modulation(i, alloc):
            rows = alloc("modrow", [2, 3 * D])
            mb = alloc("modb", [2, 3 * D])
            gp = alloc("gp2", [2, 2, D])
            wrot = Rot(alloc, "modw", [128, 8, 512], 2)
            S.dma(lambda e: e.dma_start(out=mb[:], in_=modb_d[i]), writes=["modb"])
            S.dma(lambda e: e.dma_start(out=gp[:, 0, :], in_=gpre_d[i]), writes=["gp0"])
            S.dma(lambda e: e.dma_start(out=gp[:, 1, :], in_=gpost_d[i]), writes=["gp1"])
            for cb in range(6):
                wt, wk = wrot.get()
                S.dma(lambda e, wt=wt, cb=cb: e.dma_start(out=wt[:], in_=modw_d[i, :, :, cb * 512:(cb + 1) * 512]),
                      writes=[wk])
                pt, pk = psbank()

                def mm(e, wt=wt, pt=pt):
                    for dc in range(8):
                        ins = e.matmul(pt[0:2, :], lhsT=ssil[:, dc, :], rhs=wt[:, dc, :], start=(dc == 0), stop=(dc == 7))
                    return ins
                S.op("pe", mm, reads=[wk, "ssil"], writes=list(pk))
                S.op("dve", lambda e, pt=pt, cb=cb: e.tensor_tensor(out=rows[:, cb * 512:(cb + 1) * 512], in0=pt[0:2, :],
                                                                    in1=mb[:, cb * 512:(cb + 1) * 512], op=ALU.add),
                     reads=list(pk) + ["modb"], writes=[("rows", cb)])
            allrows = [("rows", cb) for cb in range(6)]
            S.op("dve", lambda e: e.scalar_tensor_tensor(out=rows[:, D:2 * D], in0=rows[:, D:2 * D], scalar=1.0,
                                                         in1=gp[:, 0, :], op0=ALU.add, op1=ALU.mult),
                 reads=allrows + ["gp0"], writes=allrows)
            S.op("dve", lambda e: e.tensor_tensor(out=rows[:, 2 * D:3 * D], in0=rows[:, 2 * D:3 * D], in1=gp[:, 1, :],
                                                  op=ALU.mult), reads=allrows + ["gp1"], writes=allrows)
            for wi, w in enumerate(("l", "c")):
                for k, slot in enumerate((1, 0, 2)):
                    for half in range(2):
                        pt, pk = psbank()
                        S.op("pe", lambda e, pt=pt, wi=wi, slot=slot, half=half: e.matmul(
                            pt[:, :], lhsT=sel[:, wi, :], rhs=rows[:, slot * D + half * 512: slot * D + half * 512 + 512],
                            start=True, stop=True), reads=allrows + ["sel"], writes=list(pk))
                        S.op("act", lambda e, pt=pt, w=w, k=k, half=half: e.activation(
                            out=bc[w][k][:, half * 512:(half + 1) * 512], in_=pt[:, :], func=AF.Copy),
                            reads=list(pk), writes=[("bc", w, k, half)])

        def bckeys(w, k):
            return [("bc", w, k, 0), ("bc", w, k, 1)]

        def prenorm_sub(src, seq, t0, sub, col, xt, xk, hdst, hk, tmp_rot, st_rot):
            w = seq
            for (pa, pb), ap in x_rows(src, seq, t0, sub, col):
                S.dma(lambda e, pa=pa, pb=pb, ap=ap: e.dma_start(out=xt[pa:pb, :], in_=ap), writes=[xk])
            junk, jk = tmp_rot.get()
            st, sk = st_rot.get()
            S.op("act", lambda e: e.activation(out=junk[:], in_=xt, func=AF.Square, accum_out=st[:, 0:1]),
                 reads=[xk], writes=[jk, sk])
            S.op("act", lambda e: e.activation(out=st[:, 1:2], in_=st[:, 0:1], func=AF.Sqrt, scale=1.0 / D, bias=1e-6),
                 reads=[sk], writes=[sk])
            S.op("dve", lambda e: e.reciprocal(out=st[:, 2:3], in_=st[:, 1:2]), reads=[sk], writes=[sk])
            S.op("dve", lambda e: e.scalar_tensor_tensor(out=junk[:], in0=xt, scalar=st[:, 2:3], in1=bc[w][0][:],
                                                         op0=ALU.mult, op1=ALU.mult),
                 reads=[xk, sk] + bckeys(w, 0), writes=[jk])
            S.op("dve", lambda e: e.tensor_tensor(out=junk[:], in0=junk[:], in1=bc[w][1][:], op=ALU.add),
                 reads=[jk] + bckeys(w, 1), writes=[jk])
            for half in range(2):
                pt, pk = psbank()

                def tr(e, pt=pt, half=half):
                    for q in range(4):
                        dc = half * 4 + q
                        ins = e.transpose(pt[:, q * 128:(q + 1) * 128], junk[:, dc * 128:(dc + 1) * 128], ident)
                    return ins
                S.op("pe", tr, reads=[jk, "masks"], writes=list(pk))
                for q in range(4):
                    dc = half * 4 + q
                    eng = "act" if q % 2 == 0 else "dve"
                    if eng == "act":
                        S.op("act", lambda e, pt=pt, q=q, dc=dc: e.activation(out=RR(hdst(dc)), in_=pt[:, q * 128:(q + 1) * 128],
                                                                              func=AF.Copy), reads=list(pk), writes=[hk])
                    else:
                        S.op("dve", lambda e, pt=pt, q=q, dc=dc: e.tensor_copy(out=RR(hdst(dc)), in_=pt[:, q * 128:(q + 1) * 128]),
                             reads=list(pk), writes=[hk])

        def outproj_post(G, gkeys, wo, wok, nk, seq, t0, col, xt_tiles, dst, tmp_rot, st_rot):
            w = seq
            for sub in range(NT // 128):
                xt, xk = xt_tiles[sub]
                pts = [psbank(), psbank()]
                for half in range(2):
                    pt, pk = pts[half]

                    def mm(e, pt=pt, half=half, sub=sub):
                        for kc in range(nk):
                            ins = e.matmul(pt[:, :], lhsT=RR(G[:, kc, sub * 128:(sub + 1) * 128]),
                                           rhs=RR(wo[:, kc, half * 512:(half + 1) * 512]), start=(kc == 0), stop=(kc == nk - 1))
                        return ins
                    S.op("pe", mm, reads=list(gkeys) + [wok], writes=list(pk))
                st, sk = st_rot.get()
                junk, jk = tmp_rot.get()
                for half in range(2):
                    pt, pk = pts[half]
                    S.op("act", lambda e, pt=pt, half=half: e.activation(out=junk[:, half * 512:(half + 1) * 512], in_=pt[:, :],
                                                                        func=AF.Square, accum_out=st[:, half:half + 1]),
                         reads=list(pk), writes=[jk, sk])
                S.op("dve", lambda e: e.tensor_tensor(out=st[:, 2:3], in0=st[:, 0:1], in1=st[:, 1:2], op=ALU.add),
                     reads=[sk], writes=[sk])
                S.op("act", lambda e: e.activation(out=st[:, 3:4], in_=st[:, 2:3], func=AF.Sqrt, scale=1.0 / D, bias=1e-6),
                     reads=[sk], writes=[sk])
                S.op("dve", lambda e: e.reciprocal(out=st[:, 4:5], in_=st[:, 3:4]), reads=[sk], writes=[sk])
                for half in range(2):
                    pt, pk = pts[half]
                    S.op("dve", lambda e, pt=pt, half=half: e.scalar_tensor_tensor(
                        out=junk[:, half * 512:(half + 1) * 512], in0=pt[:, :], scalar=st[:, 4:5],
                        in1=bc[w][2][:, half * 512:(half + 1) * 512], op0=ALU.mult, op1=ALU.mult),
                        reads=list(pk) + [sk, ("bc", w, 2, half)], writes=[jk])
                S.op("dve", lambda e, xt=xt: e.tensor_tensor(out=xt, in0=xt, in1=junk[:], op=ALU.add),
                     reads=[xk, jk], writes=[xk])
                for (pa, pb), ap in x_rows(dst, seq, t0, sub, col):
                    S.dma(lambda e, pa=pa, pb=pb, ap=ap, xt=xt: e.dma_start(out=ap, in_=xt[pa:pb, :]), reads=[xk],
                          writes=[("dst", seq, t0, sub, pa)])

        def odd_layer(i, xsrc, csrc, xdst, cdst, last):
            j = i // 2
            col = (j % 2 == 1)
            with contextlib.ExitStack() as ph:
                alloc = mk_alloc(ph)
                modulation(i, alloc)
                S.end_phase()
            with contextlib.ExitStack() as ph:
                alloc = mk_alloc(ph)
                wo = alloc("wo", [128, 16, D])
                dww = alloc("dww", [128, 16, 31])
                ov = alloc("ov", [128, 16, 3])
                for q in range(4):
                    S.dma(lambda e, q=q: e.dma_start(out=RR(wo[:, q * 4:(q + 1) * 4, :]), in_=owout_d[j, :, q * 4:(q + 1) * 4, :]),
                          writes=[("wo", q)], q=WQ)
                wok = ("wo", 0)
                wokeys = [("wo", q) for q in range(4)]
                S.dma(lambda e: e.dma_start(out=dww[:], in_=odww_d[j]), writes=["dww"])
                S.dma(lambda e: e.dma_start(out=ov[:], in_=ovec_d[j]), writes=["ov"])
                hT = alloc("hT", [128, 8, NT])
                A = alloc("A", [128, 16, NT])
                SZ = alloc("SZ", [128, 16, NT])
                xrot = Rot(alloc, "xt", [128, 2, D], 1)
                wrot = Rot(alloc, "wi", [128, 8, 3, 128], 3)
                tmp_rot = Rot(alloc, "tmpw", [128, D], 1)
                st_rot = Rot(alloc, "st", [128, 8], 4)
                sm_rot = Rot(alloc, "sm", [128, NT], 5)
                apad = Rot(alloc, "apad", [128, 384], 3)
                dgp = Rot(alloc, "dg", [128, 128], 18)
                ps_state["nrot"] = 4
                lay = {"cur": None}
                stat = alloc("stat", [128, 3, NT])
                for (seq, t0) in seq_tiles(not last):
                    src = csrc if seq == "c" else xsrc
                    dst = cdst if seq == "c" else xdst
                    seg = NCTX if seq == "c" else 64
                    nseg = NT // seg if seg <= NT else 1
                    seg = min(seg, NT)
                    if lay["cur"] != (seg, nseg):
                        lay["cur"] = (seg, nseg)
                        for t_, k_ in zip(apad.tiles, apad.keys):
                            S.op("act", lambda e, t_=t_: e.activation(out=RR(t_[:]), in_=bc["l"][0][:, 0:384], func=AF.Copy, scale=0.0),
                                 reads=bckeys("l", 0), writes=[k_])
                    xt3, xk3 = xrot.get()
                    xt_tiles = [(xt3[:, sub, :], (xk3, sub)) for sub in range(2)]
                    for sub in range(2):
                        prenorm_sub(src, seq, t0, sub, col, xt_tiles[sub][0], xt_tiles[sub][1],
                                    lambda dc, sub=sub: hT[:, dc, sub * 128:(sub + 1) * 128], "hT", tmp_rot, st_rot)
                    pmean, pmk = accbank(0)
                    pex2, pek = accbank(1)
                    statcnt = {"n": 0}

                    def cc_body(cc, seg=seg, nseg=nseg):
                        wi, wk = wrot.get()
                        S.dma(lambda e: e.dma_start(out=RR(wi[:]), in_=owin_d[j, cc]), writes=[wk], q=WQ)
                        yield

                        def mmg(pt, blk):
                            def mm(e):
                                for dc in range(8):
                                    ins = e.matmul(pt, lhsT=RR(wi[:, dc, blk, :]), rhs=RR(hT[:, dc, :]), start=(dc == 0), stop=(dc == 7))
                                return ins
                            return mm
                        pg, pgk = ps()
                        S.op("pe", mmg(pg, 1), reads=[wk, "hT"], writes=[pgk])
                        sg, sgk = sm_rot.get()
                        S.op("act", lambda e: e.activation(out=sg[:], in_=pg, func=AF.Sigmoid), reads=[pgk], writes=[sgk])
                        yield
                        pu, puk = ps()
                        S.op("pe", mmg(pu, 0), reads=[wk, "hT"], writes=[puk])
                        apt, ak = apad.get()
                        av = apt[:, 0:nseg * (seg + 30)].rearrange("p (s t) -> p s t", s=nseg)
                        S.op("dve", lambda e: e.tensor_tensor(out=RR(av[:, :, 15:15 + seg]), in0=pu.rearrange("p (s t) -> p s t", s=nseg),
                                                              in1=sg[:].rearrange("p (s t) -> p s t", s=nseg), op=ALU.mult),
                             reads=[puk, sgk, ak], writes=[ak])
                        yield
                        pz, pzk = ps()
                        S.op("pe", mmg(pz, 2), reads=[wk, "hT"], writes=[pzk])
                        S.op("act", lambda e: e.activation(out=RR(SZ[:, cc, :]), in_=pz, func=AF.Silu), reads=[pzk], writes=[("SZ", cc)])
                        yield
                        pcv, pcvk = pst[4 + (cc % 2)][:, 0:256], "cvbank%d" % (cc % 2)
                        pcv3 = pcv.rearrange("p (s t) -> p s t", s=nseg)
                        for g0 in range(0, 31, 8):
                            grp = list(range(g0, min(g0 + 8, 31)))
                            dts = []
                            for tap in grp:
                                dgt, dgk = dgp.get()
                                S.op("act", lambda e: e.activation(out=RR(dgt[:]), in_=ident, func=AF.Copy, scale=dww[:, cc, tap:tap + 1]),
                                     reads=["masks", "dww"], writes=[dgk])
                                dts.append((dgt, dgk))

                            def cmm(e):
                                for (dgt_, _), tap in zip(dts, grp):
                                    ins = e.matmul(pcv3, lhsT=RR(dgt_[:]), rhs=RR(av[:, :, tap:tap + seg]), start=(tap == 0), stop=(tap == 30))
                                return ins
                            S.op("pe", cmm, reads=[ak] + [k for _, k in dts], writes=[pcvk])
                            yield
                        S.op("act", lambda e: e.activation(out=A[:, cc, :], in_=pcv, func=AF.Identity, bias=ov[:, cc, 0:1], scale=1.0),
                             reads=[pcvk, "ov"], writes=[("A", cc)])
                        yield
                        sq, sqk = sm_rot.get()
                        S.op("act", lambda e: e.activation(out=sq[:], in_=A[:, cc, :], func=AF.Square), reads=[("A", cc)], writes=[sqk])
                        k_ = statcnt["n"]
                        statcnt["n"] += 1
                        S.op("pe", lambda e: e.matmul(pmean, lhsT=bones[:, 1, :], rhs=A[:, cc, :], start=(k_ == 0), stop=(k_ == 15)),
                             reads=[("A", cc), "bones"], writes=[pmk])
                        S.op("pe", lambda e: e.matmul(pex2, lhsT=bones[:, 1, :], rhs=sq[:], start=(k_ == 0), stop=(k_ == 15)),
                             reads=[sqk, "bones"], writes=[pek])

                    NSTR = 2
                    pending = [cc_body(cc) for cc in range(16)]
                    live = []
                    while pending or live:
                        while pending and len(live) < NSTR:
                            live.append(pending.pop(0))
                        for g_ in list(live):
                            try:
                                next(g_)
                            except StopIteration:
                                live.remove(g_)
                    S.op("act", lambda e: e.activation(out=stat[:, 0, :], in_=pmean, func=AF.Copy), reads=[pmk], writes=["stat"])
                    S.op("dve", lambda e: e.tensor_tensor(out=stat[:, 1, :], in0=stat[:, 0, :], in1=stat[:, 0, :], op=ALU.mult),
                         reads=["stat"], writes=["stat"])
                    S.op("dve", lambda e: e.tensor_tensor(out=stat[:, 1, :], in0=pex2, in1=stat[:, 1, :], op=ALU.subtract),
                         reads=["stat", pek], writes=["stat"])
                    S.op("act", lambda e: e.activation(out=stat[:, 2, :], in_=stat[:, 1, :], func=AF.Sqrt, bias=1e-5, scale=1.0),
                         reads=["stat"], writes=["stat"])
                    S.op("dve", lambda e: e.reciprocal(out=stat[:, 2, :], in_=stat[:, 2, :]), reads=["stat"], writes=["stat"])
                    for cc in range(16):
                        S.op("dve", lambda e, cc=cc: e.tensor_tensor(out=A[:, cc, :], in0=A[:, cc, :], in1=stat[:, 0, :],
                                                                     op=ALU.subtract), reads=[("A", cc), "stat"], writes=[("A", cc)])
                        S.op("dve", lambda e, cc=cc: e.tensor_tensor(out=A[:, cc, :], in0=A[:, cc, :], in1=stat[:, 2, :],
                                                                     op=ALU.mult), reads=[("A", cc), "stat"], writes=[("A", cc)])
                        S.op("act", lambda e, cc=cc: e.activation(out=A[:, cc, :], in_=A[:, cc, :], func=AF.Silu,
                                                                  scale=ov[:, cc, 1:2], bias=ov[:, cc, 2:3]),
                             reads=[("A", cc), "ov"], writes=[("A", cc)])
                        S.op("dve", lambda e, cc=cc: e.tensor_tensor(out=RR(SZ[:, cc, :]), in0=A[:, cc, :], in1=SZ[:, cc, :],
                                                                     op=ALU.mult), reads=[("A", cc), ("SZ", cc)], writes=[("SZ", cc)])
                    outproj_post(SZ, [("SZ", cc) for cc in range(16)] + wokeys[1:], wo, wok, 16, seq, t0, col, xt_tiles, dst,
                                 tmp_rot, st_rot)
                ps_state["nrot"] = 6
                S.end_phase()

        def even_layer(i, xsrc, csrc, xdst, cdst):
            j = i // 2
            col = (j % 2 == 1)
            tiles = seq_tiles(True)
            with contextlib.ExitStack() as ph:
                alloc = mk_alloc(ph)
                modulation(i, alloc)
                S.end_phase()
            with contextlib.ExitStack() as ph:
                alloc = mk_alloc(ph)
                xrot = Rot(alloc, "xt", [128, D], 3)
                hrot = Rot(alloc, "hTs", [128, 8, NT], 2)
                tmp_rot = Rot(alloc, "tmpw", [128, D], 2)
                st_rot = Rot(alloc, "st", [128, 8], 4)
                for (seq, t0) in tiles:
                    src = csrc if seq == "c" else xsrc
                    hT, hk = hrot.get()
                    for sub in range(2):
                        xt, xk = xrot.get()
                        prenorm_sub(src, seq, t0, sub, col, xt[:], xk, lambda dc, sub=sub, hT=hT: hT[:, dc, sub * 128:(sub + 1) * 128],
                                    hk, tmp_rot, st_rot)
                    po = poff(seq, t0)
                    S.dma(lambda e, hT=hT, po=po: e.dma_start(out=hT_d[:, :, po:po + NT], in_=hT[:]), reads=[hk],
                          writes=[("hTd", seq, t0)])
                S.end_phase()
            with contextlib.ExitStack() as ph:
                alloc = mk_alloc(ph)
                ev = alloc("ev", [128, 8, 17])
                muwa = alloc("muwa", [128, 8, 2])
                w1s = alloc("w1s", [128, 2, 8, 128])
                w2s = alloc("w2s", [128, 2, D])
                omka = alloc("omka", [128, 8])
                S.dma(lambda e: e.dma_start(out=ev[:], in_=evec_d[j]), writes=["ev"])
                S.dma(lambda e: e.dma_start(out=muwa[:], in_=emuwa_d[j]), writes=["muwa"])
                for q in range(2):
                    S.dma(lambda e, q=q: e.dma_start(out=RR(w1s[:, q]), in_=ew1_d[j, q]), writes=["w1s"], q=WQ)
                    S.dma(lambda e, q=q: e.dma_start(out=RR(w2s[:, q]), in_=ew2_d[j, q]), writes=["w2s"], q=WQ)
                S.op("dve", lambda e: e.tensor_scalar(out=omka[:], in0=ev[:, :, 8], scalar1=-1.0, scalar2=1.0, op0=ALU.mult,
                                                      op1=ALU.add), reads=["ev"], writes=["omka"])
                hh = alloc("hh", [128, 8, NT + 2])
                dh = alloc("dh", [128, 8, NT])
                smr = Rot(alloc, "smr", [128, NT], 8)
                lo = alloc("lo", [128, 2, NT])
                wrot = Rot(alloc, "wi", [128, 8, 8, 128], 2)
                sm = Rot(alloc, "sm", [128, NT], 38)
                ll = Rot(alloc, "ll", [128, NT], 12)
                q6 = Rot(alloc, "q6", [128, 6, NT], 2)
                vt_rot = Rot(alloc, "vts", [128, 2, 128], 2)
                VEC = dict(mu_r=0, mu_k=1, mu_v=2, w0=3, a0=5, k_k=7, k_a=8, r_k=9, lnw=11, lnb=12, sc=13)
                for (seq, t0) in tiles:
                    po = poff(seq, t0)
                    go = goff(seq, t0)
                    seg = min(NCTX if seq == "c" else 64, NT)
                    nseg = NT // seg
                    S.dma(lambda e, po=po: e.dma_start(out=RR(hh[:]), in_=hT_d[:, :, po - 1:po + NT + 1]), q=WQ,
                          reads=[("hTd", seq, t0), ("hTd", seq, t0 - NT), ("hTd", seq, t0 + NT)] + [("hTdz", c) for c in
                                                                                                     (0, NCTX + 1, NCTX + 2, NCTX + 3 + T)],
                          writes=["hh"])
                    S.op("dve", lambda e: e.tensor_tensor(out=RR(dh[:]), in0=hh[:, :, 0:NT], in1=hh[:, :, 2:NT + 2], op=ALU.add),
                         reads=["hh"], writes=["dh"])
                    S.op("dve", lambda e: e.scalar_tensor_tensor(out=RR(dh[:]), in0=dh[:], scalar=0.5, in1=hh[:, :, 1:NT + 1],
                                                                 op0=ALU.mult, op1=ALU.subtract), reads=["dh", "hh"], writes=["dh"])
                    for q in range(2):
                        pl, plk = ps()
                        for dc in range(8):
                            xw, xwk = smr.get()
                            S.op("dve", lambda e, xw=xw, dc=dc, q=q: e.scalar_tensor_tensor(
                                out=RR(xw[:]), in0=dh[:, dc, :], scalar=muwa[:, dc, q:q + 1], in1=hh[:, dc, 1:NT + 1],
                                op0=ALU.mult, op1=ALU.add), reads=["dh", "hh", "muwa"], writes=[xwk])
                            S.op("pe", lambda e, xw=xw, dc=dc, q=q, pl=pl: e.matmul(pl, lhsT=RR(w1s[:, q, dc, :]), rhs=RR(xw[:]),
                                                                                   start=(dc == 0), stop=(dc == 7)),
                                 reads=[xwk, "w1s"], writes=[plk])
                        S.op("act", lambda e, q=q, pl=pl: e.activation(out=RR(lo[:, q, :]), in_=pl, func=(AF.Tanh if q == 0 else AF.Copy)),
                             reads=[plk], writes=[("lo", q)])
                    def cc_body(cc, go=go, seg=seg, nseg=nseg):
                        wi, wk = wrot.get()
                        for hf in range(2):
                            S.dma(lambda e, wi=wi, cc=cc, hf=hf: e.dma_start(out=RR(wi[:, hf * 4:(hf + 1) * 4]), in_=ewin_d[j, cc, :, hf * 4:(hf + 1) * 4]),
                                  writes=[(wk, hf)], q=WQ)
                        wks = [(wk, 0), (wk, 1)]
                        V = lambda name, k=0, cc=cc: ev[:, cc, VEC[name] + k:VEC[name] + k + 1]

                        def proj(blk, rhs_is_dh=False, wi=wi):
                            pt, pk = ps()

                            def mm(e, pt=pt):
                                for dc in range(8):
                                    ins = e.matmul(pt, lhsT=RR(wi[:, dc, blk, :]), rhs=RR(dh[:, dc, :] if rhs_is_dh else hh[:, dc, 1:NT + 1]),
                                                   start=(dc == 0), stop=(dc == 7))
                                return ins
                            S.op("pe", mm, reads=wks + ["hh", "dh"], writes=[pk])
                            return pt, pk

                        def tt(eng, out, outk, in0, in1, op, rd):
                            S.op(eng, lambda e: e.tensor_tensor(out=out, in0=in0, in1=in1, op=op), reads=rd, writes=[outk])

                        rkv = []
                        for bi in range(3):
                            p1, p1k = proj(bi)
                            p2, p2k = proj(bi, True)
                            t2, t2k = sm.get()
                            S.op("act", lambda e, t2=t2, p2=p2: e.activation(out=t2[:], in_=p2, func=AF.Copy), reads=[p2k], writes=[t2k])
                            o, ok = ll.get()
                            S.op("dve", lambda e, o=o, t2=t2, p1=p1, bi=bi, cc=cc: e.scalar_tensor_tensor(
                                out=o[:], in0=t2[:], scalar=ev[:, cc, bi:bi + 1], in1=p1, op0=ALU.mult, op1=ALU.add),
                                reads=[t2k, p1k, "ev"], writes=[ok])
                            rkv.append((o, ok))
                            yield "A"
                        (rp, rpk), (kp, kpk), (vp, vpk) = rkv
                        pza, pzak = proj(3)
                        sza, szak = sm.get()
                        S.op("act", lambda e, sza=sza, pza=pza: e.activation(out=sza[:], in_=pza, func=AF.Silu), reads=[pzak], writes=[szak])
                        S.dma(lambda e, sza=sza, cc=cc, go=go: e.dma_start(out=misc_d[1, cc, :, go:go + NT], in_=sza[:]), reads=[szak],
                              writes=[("misc", 1, cc, go)])
                        yield "A"
                        pb, pbk = proj(4)
                        pcg, pcgk = proj(5)
                        pxb, pxbk = proj(6)
                        pzb, pzbk = proj(7)
                        cgs, cgsk = sm.get()
                        S.op("act", lambda e, cgs=cgs, pcg=pcg: e.activation(out=cgs[:], in_=pcg, func=AF.Copy), reads=[pcgk], writes=[cgsk])
                        cx, cxk = sm.get()
                        tt("dve", cx[:], cxk, cgs[:], pxb, ALU.mult, [cgsk, pxbk])
                        acc, acck = sm.get()
                        S.op("act", lambda e, acc=acc, cx=cx, cc=cc: e.activation(out=acc[:], in_=cx[:], func=AF.Copy, scale=ev[:, cc, 14:15]),
                             reads=[cxk, "ev"], writes=[acck])

                        def sconv0(e, acc=acc, cx=cx, cc=cc, seg=seg, nseg=nseg):
                            av = cx[:].rearrange("p (s t) -> p s t", s=nseg)
                            ov_ = acc[:].rearrange("p (s t) -> p s t", s=nseg)
                            return e.scalar_tensor_tensor(out=ov_[:, :, 1:seg], in0=av[:, :, 0:seg - 1], scalar=ev[:, cc, 13:14],
                                                          in1=ov_[:, :, 1:seg], op0=ALU.mult, op1=ALU.add)

                        def sconv2(e, acc=acc, cx=cx, cc=cc, seg=seg, nseg=nseg):
                            av = cx[:].rearrange("p (s t) -> p s t", s=nseg)
                            ov_ = acc[:].rearrange("p (s t) -> p s t", s=nseg)
                            return e.scalar_tensor_tensor(out=ov_[:, :, 0:seg - 1], in0=av[:, :, 1:seg], scalar=ev[:, cc, 15:16],
                                                          in1=ov_[:, :, 0:seg - 1], op0=ALU.mult, op1=ALU.add)
                        S.op("dve", sconv0, reads=[cxk, acck, "ev"], writes=[acck])
                        S.op("dve", sconv2, reads=[cxk, acck, "ev"], writes=[acck])
                        tt("dve", acc[:], acck, acc[:], pb, ALU.mult, [acck, pbk])
                        szb, szbk = sm.get()
                        S.op("act", lambda e, szb=szb, pzb=pzb: e.activation(out=szb[:], in_=pzb, func=AF.Silu), reads=[pzbk], writes=[szbk])
                        tt("dve", acc[:], acck, acc[:], szb[:], ALU.mult, [acck, szbk])
                        S.dma(lambda e, acc=acc, cc=cc, go=go: e.dma_start(out=misc_d[2, cc, :, go:go + NT], in_=acc[:]), reads=[acck],
                              writes=[("misc", 2, cc, go)])
                        yield "A"
                        kx, kxk = sm.get()
                        S.op("act", lambda e, kx=kx, kp=kp, cc=cc: e.activation(out=kx[:], in_=kp[:], func=AF.Copy, scale=ev[:, cc, 7:8]),
                             reads=[kpk, "ev"], writes=[kxk])
                        ksq, ksqk = smr.get()
                        kn, knk = sm.get()
                        S.op("act", lambda e, ksq=ksq, kx=kx: e.activation(out=RR(ksq[:]), in_=kx[:], func=AF.Square), reads=[kxk], writes=[ksqk])
                        pss, pssk = ps()
                        S.op("pe", lambda e, pss=pss, ksq=ksq: e.matmul(pss, lhsT=RR(bonesr[:]), rhs=RR(ksq[:]), start=True, stop=True),
                             reads=[ksqk, "bonesr"], writes=[pssk])
                        S.op("dve", lambda e, kn=kn, pss=pss: e.tensor_scalar(out=kn[:], in0=pss, scalar1=1e-24, scalar2=None, op0=ALU.max),
                             reads=[pssk], writes=[knk])
                        S.op("act", lambda e, kn=kn: e.activation(out=kn[:], in_=kn[:], func=AF.Sqrt), reads=[knk], writes=[knk])
                        S.op("dve", lambda e, kn=kn: e.reciprocal(out=kn[:], in_=kn[:]), reads=[knk], writes=[knk])
                        kk, kkk = ll.get()
                        tt("dve", kk[:], kkk, kx[:], kn[:], ALU.mult, [kxk, knk])
                        yield "B"
                        pbn, pbnk = accbank(cc % 2)
                        for z in range(2):
                            Q, Qk = q6.get()
                            plw, plwk = ps()
                            S.op("pe", lambda e, plw=plw, z=z, cc=cc: e.matmul(plw, lhsT=RR(w2s[z * 64:(z + 1) * 64, 0, cc * 128:(cc + 1) * 128]),
                                                                               rhs=RR(lo[z * 64:(z + 1) * 64, 0, :]), start=True, stop=True),
                                 reads=[("lo", 0), "w2s"], writes=[plwk])
                            pla, plak = ps()
                            S.op("pe", lambda e, pla=pla, z=z, cc=cc: e.matmul(pla, lhsT=RR(w2s[z * 64:(z + 1) * 64, 1, cc * 128:(cc + 1) * 128]),
                                                                               rhs=RR(lo[z * 64:(z + 1) * 64, 1, :]), start=True, stop=True),
                                 reads=[("lo", 1), "w2s"], writes=[plak])
                            ld, ldk = sm.get()
                            S.op("act", lambda e, ld=ld, plw=plw, z=z, cc=cc: e.activation(out=ld[:], in_=plw, func=AF.Sigmoid,
                                                                                           bias=ev[:, cc, 3 + z:4 + z], scale=1.0),
                                 reads=[plwk, "ev"], writes=[ldk])
                            asg, asgk = sm.get()
                            S.op("act", lambda e, asg=asg, pla=pla, z=z, cc=cc: e.activation(out=asg[:], in_=pla, func=AF.Sigmoid,
                                                                                             bias=ev[:, cc, 5 + z:6 + z], scale=1.0),
                                 reads=[plak, "ev"], writes=[asgk])
                            yield "B"
                            kd, kdk = sm.get()
                            S.op("act", lambda e, kd=kd, asg=asg, cc=cc: e.activation(out=kd[:], in_=asg[:], func=AF.Identity, scale=ev[:, cc, 8:9],
                                                                                       bias=omka[:, cc:cc + 1]),
                                 reads=[asgk, "ev", "omka"], writes=[kdk])
                            tt("dve", kd[:], kdk, kd[:], kp[:], ALU.mult, [kdk, kpk])
                            b, bk = sm.get()
                            tt("dve", b[:], bk, kk[:], asg[:], ALU.mult, [kkk, asgk])
                            yield "B"
                            ci, cik = sm.get()
                            S.op("dve", lambda e, ci=ci, ld=ld: e.tensor_tensor_scan(out=ci[:], data0=rmask[:], data1=ld[:], initial=0.0,
                                                                                     op0=ALU.mult, op1=ALU.add), reads=[ldk, "rmask"], writes=[cik])
                            nchk = NT // CH
                            v3 = lambda t: t[:].rearrange("p (n c) -> p n c", c=CH)
                            tot, totk = sm.get()
                            S.op("dve", lambda e, tot=tot, ci=ci: e.tensor_copy(out=tot[:, 0:nchk], in_=v3(ci)[:, :, CH - 1]), reads=[cik], writes=[totk])
                            totb = lambda tot=tot: tot[:, 0:nchk].unsqueeze(2).broadcast_to([128, nchk, CH])
                            if z == 1:
                                S.op("dve", lambda e, ci=ci, totb=totb: e.tensor_tensor(out=v3(ci), in0=totb(), in1=v3(ci), op=ALU.subtract),
                                     reads=[cik, totk], writes=[cik])
                                tt("dve", ci[:], cik, ci[:], ld[:], ALU.add, [cik, ldk])
                            ce, cek = sm.get()
                            tt("dve", ce[:], cek, ci[:], ld[:], ALU.subtract, [cik, ldk])
                            chh, chk = sm.get()
                            S.op("dve", lambda e, chh=chh, ci=ci, totb=totb: e.tensor_tensor(out=v3(chh), in0=totb(), in1=v3(ci), op=ALU.subtract),
                                 reads=[cik, totk], writes=[chk])
                            yield "B"
                            epos, eposk = sm.get()
                            eneg, enegk = sm.get()
                            S.op("act", lambda e, epos=epos, ci=ci: e.activation(out=epos[:], in_=ci[:], func=AF.Exp, scale=DECAY_SCALE), reads=[cik], writes=[eposk])
                            S.op("act", lambda e, eneg=eneg, ci=ci: e.activation(out=eneg[:], in_=ci[:], func=AF.Exp, scale=-DECAY_SCALE), reads=[cik], writes=[enegk])
                            S.op("act", lambda e, ce=ce: e.activation(out=ce[:], in_=ce[:], func=AF.Exp, scale=DECAY_SCALE), reads=[cek], writes=[cek])
                            S.op("act", lambda e, chh=chh: e.activation(out=chh[:], in_=chh[:], func=AF.Exp, scale=DECAY_SCALE), reads=[chk], writes=[chk])
                            c0 = go // CH
                            S.op("act", lambda e, tot=tot, z=z, cc=cc, c0=c0: e.activation(out=PCt[0][:, z, cc, c0:c0 + nchk], in_=tot[:, 0:nchk], func=AF.Exp, scale=DECAY_SCALE),
                                 reads=[totk], writes=[("PC", z, cc, c0)])
                            yield "B"
                            S.op("dve", lambda e, Q=Q, kk=kk, ce=ce: e.scalar_tensor_tensor(out=Q[:, 0, :], in0=kk[:], scalar=-1.0, in1=ce[:],
                                                                                            op0=ALU.mult, op1=ALU.mult), reads=[kkk, cek], writes=[(Qk, 0)])
                            tt("dve", Q[:, 1, :], (Qk, 1), rp[:], epos[:], ALU.mult, [rpk, eposk])
                            tt("dve", Q[:, 2, :], (Qk, 2), b[:], eneg[:], ALU.mult, [bk, enegk])
                            tt("dve", Q[:, 3, :], (Qk, 3), kd[:], eneg[:], ALU.mult, [kdk, enegk])
                            tt("dve", Q[:, 4, :], (Qk, 4), b[:], chh[:], ALU.mult, [bk, chk])
                            tt("dve", Q[:, 5, :], (Qk, 5), kd[:], chh[:], ALU.mult, [kdk, chk])
                            S.dma(lambda e, Q=Q, z=z, cc=cc, go=go: e.dma_start(out=scn_d[z, cc, :, :, go:go + NT], in_=Q[:]),
                                  reads=[(Qk, q) for q in range(6)], writes=[("scn", z, cc, go)])
                            yield "B"
                            bz, bzk = smr.get()
                            S.op("dve", lambda e, bz=bz, kd=kd, rp=rp, z=z, cc=cc: e.scalar_tensor_tensor(
                                out=RR(bz[:]), in0=kd[:], scalar=ev[:, cc, 9 + z:10 + z], in1=rp[:], op0=ALU.mult, op1=ALU.mult),
                                reads=[kdk, rpk, "ev"], writes=[bzk])
                            S.op("pe", lambda e, bz=bz, z=z, pbn=pbn: e.matmul(pbn, lhsT=RR(bonesr[:]), rhs=RR(bz[:]), start=(z == 0), stop=(z == 1)),
                                 reads=[bzk, "bonesr"], writes=[pbnk])
                            yield "B"
                        bon, bonk = sm.get()
                        tt("dve", bon[:], bonk, vp[:], pbn, ALU.mult, [vpk, pbnk])
                        S.dma(lambda e, bon=bon, cc=cc, go=go: e.dma_start(out=misc_d[0, cc, :, go:go + NT], in_=bon[:]), reads=[bonk],
                              writes=[("misc", 0, cc, go)])
                        yield "B"
                        ptv, ptvk = ps()

                        def trv(e, ptv=ptv, vp=vp):
                            for n2 in range(NT // 128):
                                ins = e.transpose(ptv[:, n2 * 128:(n2 + 1) * 128], vp[:, n2 * 128:(n2 + 1) * 128], ident)
                            return ins
                        S.op("pe", trv, reads=[vpk, "masks"], writes=[ptvk])
                        vts, vtsk = vt_rot.get()
                        S.op("act", lambda e, vts=vts, ptv=ptv: e.activation(out=vts[:].rearrange("p n c -> p (n c)"), in_=ptv, func=AF.Copy),
                             reads=[ptvk], writes=[vtsk])
                        c0 = go // CH
                        for n in range(NT // CH):
                            S.dma(lambda e, vts=vts, cc=cc, n=n, c0=c0: e.dma_start(
                                out=vtok_d[cc, c0 + n].rearrange("(h s) v -> s h v", h=2),
                                in_=vts[(n % 2) * 64:(n % 2) * 64 + 64, n // 2, :].rearrange("s (h v) -> s h v", h=2)),
                                reads=[vtsk], writes=[("vtok", cc, c0 + n)])

                    pending = [cc_body(cc) for cc in range(8)]
                    live = []
                    while pending or live:
                        if pending and len(live) < 2 and not any(t == "A" for _, t in live):
                            live.append([pending.pop(0), "A"])
                        for ent in list(live):
                            try:
                                ent[1] = next(ent[0])
                            except StopIteration:
                                live.remove(ent)
                S.end_phase()
            for z in range(2):
                for half in range(2):
                    scan_pass(z, half)
            with contextlib.ExitStack() as ph:
                alloc = mk_alloc(ph)
                wo = alloc("wo", [128, 16, D])
                ev = alloc("ev", [128, 8, 17])
                for q in range(4):
                    S.dma(lambda e, q=q: e.dma_start(out=RR(wo[:, q * 4:(q + 1) * 4, :]), in_=ewout_d[j, :, q * 4:(q + 1) * 4, :]),
                          writes=[("wo", q)], q=WQ)
                wokeys = [("wo", q) for q in range(4)]
                S.dma(lambda e: e.dma_start(out=ev[:], in_=evec_d[j]), writes=["ev"])
                G = alloc("G", [128, 16, NT])
                xrot = Rot(alloc, "xt", [128, 2, D], 2)
                tmp_rot = Rot(alloc, "tmpw", [128, D], 2)
                st_rot = Rot(alloc, "st", [128, 8], 4)
                sm = Rot(alloc, "sm", [128, NT], 12)
                for (seq, t0) in tiles:
                    src = csrc if seq == "c" else xsrc
                    dst = cdst if seq == "c" else xdst
                    go = goff(seq, t0)
                    xt3, xk3 = xrot.get()
                    xt_tiles = [(xt3[:, sub, :], (xk3, sub)) for sub in range(2)]
                    for sub in range(2):
                        for (pa, pb), ap in x_rows(src, seq, t0, sub, col):
                            S.dma(lambda e, pa=pa, pb=pb, ap=ap, sub=sub, xt3=xt3: e.dma_start(out=xt3[pa:pb, sub, :], in_=ap),
                                  writes=[(xk3, sub)])
                    def cc_body(cc, go=go):
                        of, ofk = sm.get()
                        ob, obk = sm.get()
                        bn, bnk = sm.get()
                        sz, szk = sm.get()
                        S.dma(lambda e, of=of, cc=cc, go=go: e.dma_start(out=of[:], in_=o_d[0, cc, :, go:go + NT]), writes=[ofk])
                        S.dma(lambda e, ob=ob, cc=cc, go=go: e.dma_start(out=ob[:], in_=o_d[1, cc, :, go:go + NT]), writes=[obk])
                        S.dma(lambda e, bn=bn, cc=cc, go=go: e.dma_start(out=bn[:], in_=misc_d[0, cc, :, go:go + NT]), writes=[bnk])
                        S.dma(lambda e, sz=sz, cc=cc, go=go: e.dma_start(out=sz[:], in_=misc_d[1, cc, :, go:go + NT]), writes=[szk])
                        S.dma(lambda e, cc=cc, go=go: e.dma_start(out=RR(G[:, 8 + cc, :]), in_=misc_d[2, cc, :, go:go + NT]), writes=[("G", 8 + cc)], q=WQ)
                        yield
                        S.op("pool", lambda e, of=of, ob=ob: e.tensor_tensor(out=of[:], in0=of[:], in1=ob[:], op=ALU.add), reads=[ofk, obk], writes=[ofk])
                        pm, pmk = ps()
                        S.op("pe", lambda e, pm=pm, of=of: e.matmul(pm, lhsT=bones[:, 0, :], rhs=of[:], start=True, stop=True),
                             reads=[ofk, "bones"], writes=[pmk])
                        S.op("dve", lambda e, of=of, pm=pm: e.scalar_tensor_tensor(out=of[:], in0=pm, scalar=-1.0 / 64, in1=of[:], op0=ALU.mult,
                                                                                   op1=ALU.add), reads=[ofk, pmk], writes=[ofk])
                        yield
                        S.op("act", lambda e, ob=ob, of=of: e.activation(out=ob[:], in_=of[:], func=AF.Square), reads=[ofk], writes=[obk])
                        pv, pvk = ps()
                        S.op("pe", lambda e, pv=pv, ob=ob: e.matmul(pv, lhsT=bones[:, 0, :], rhs=ob[:], start=True, stop=True),
                             reads=[obk, "bones"], writes=[pvk])
                        S.op("act", lambda e, ob=ob, pv=pv: e.activation(out=ob[:], in_=pv, func=AF.Sqrt, scale=1.0 / 64, bias=64e-5),
                             reads=[pvk], writes=[obk])
                        yield
                        S.op("dve", lambda e, ob=ob: e.reciprocal(out=ob[:], in_=ob[:]), reads=[obk], writes=[obk])
                        S.op("dve", lambda e, of=of, ob=ob: e.tensor_tensor(out=of[:], in0=of[:], in1=ob[:], op=ALU.mult), reads=[ofk, obk], writes=[ofk])
                        S.op("act", lambda e, of=of, cc=cc: e.activation(out=of[:], in_=of[:], func=AF.Identity, scale=ev[:, cc, 11:12],
                                                                         bias=ev[:, cc, 12:13]), reads=[ofk, "ev"], writes=[ofk])
                        yield
                        S.op("pool", lambda e, of=of, bn=bn: e.tensor_tensor(out=of[:], in0=of[:], in1=bn[:], op=ALU.add), reads=[ofk, bnk], writes=[ofk])
                        S.op("dve", lambda e, of=of, sz=sz, cc=cc: e.tensor_tensor(out=RR(G[:, cc, :]), in0=of[:], in1=sz[:], op=ALU.mult),
                             reads=[ofk, szk], writes=[("G", cc)])

                    pending = [cc_body(cc) for cc in range(8)]
                    live = []
                    while pending or live:
                        while pending and len(live) < 2:
                            live.append(pending.pop(0))
                        for g_ in list(live):
                            try:
                                next(g_)
                            except StopIteration:
                                live.remove(g_)
                    outproj_post(G, [("G", k) for k in range(16)] + wokeys[1:], wo, wokeys[0], 16, seq, t0, col, xt_tiles, dst, tmp_rot, st_rot)
                S.end_phase()

        def scan_pass(z, half):
            ctx_t = [("c", t0) for t0 in range(0, NCTX, NT)]
            lat_t = [("l", t0) for t0 in range(0, T, NT)]
            order = ctx_t + lat_t if z == 0 else ctx_t[::-1] + lat_t[::-1]
            MS, MI, ML = (0, 1, 2) if z == 0 else (2, 3, 0)
            with contextlib.ExitStack() as ph:
                alloc = mk_alloc(ph)
                mk2 = alloc("mk2", [128, 2, 128])
                bd6 = alloc("bd6", [128, 6, 128])
                S.op("dve", lambda e: e.tensor_copy(out=mk2[:, 0, :], in_=masks[:, MS, :]), reads=["masks"], writes=["mk2"])
                S.op("dve", lambda e: e.tensor_copy(out=mk2[:, 1, :], in_=masks[:, MI, :]), reads=["masks"], writes=["mk2"])
                for q in range(6):
                    S.op("dve", lambda e, q=q: e.tensor_copy(out=bd6[:, q, :], in_=BD), reads=["masks"], writes=["bd6"])
                opr = Rot(alloc, "opnd", [128, 6, NT], 8)
                vtr = Rot(alloc, "vtk", [128, NT // CH, CH], 8)
                oacc = Rot(alloc, "oacc", [64, 2, NT], 8)
                Hs = [[alloc("H%d_%d" % (p, k), [128, CH]) for k in range(2)] for p in range(4)]
                hcur = [0] * 4
                for p in range(4):
                    S.op("dve", lambda e, p=p: e.tensor_scalar(out=RR(Hs[p][0][:]), in0=masks[:, 0, 0:CH], scalar1=0.0, scalar2=None, op0=ALU.mult),
                         reads=["masks"], writes=[("H", p, 0)])
                blk = Rot(alloc, "blk", [128, 6, 128], 8)
                sc = Rot(alloc, "sc", [128, 2, 128], 16)
                w128 = Rot(alloc, "w128", [128, 128], 44)
                ttf = Rot(alloc, "ttf", [128, 128], 8)
                w64 = Rot(alloc, "w64", [128, CH], 16)
                nchk = NT // CH
                for (seq, t0) in order:
                    go = goff(seq, t0)
                    loaded = []
                    for p in range(4):
                        cc = half * 4 + p
                        op_, opk = opr.get()
                        vt_, vtk_ = vtr.get()
                        oa_, oak = oacc.get()
                        S.dma(lambda e, op_=op_, cc=cc, go=go: e.dma_start(out=op_[:], in_=scn_d[z, cc, :, :, go:go + NT]),
                              reads=[("scn", z, cc, go)], writes=[opk])
                        c0 = go // CH
                        S.dma(lambda e, vt_=vt_, cc=cc, c0=c0: e.dma_start(out=RR(vt_[:]), in_=vtok_d[cc, c0:c0 + nchk].rearrange("n p v -> p n v")),
                              reads=[("vtok", cc, c0 + n) for n in range(nchk)], writes=[vtk_], q=WQ)
                        loaded.append((op_, opk, vt_, vtk_, oa_, oak))
                    chunks = list(range(nchk)) if z == 0 else list(range(nchk))[::-1]
                    for n in chunks:
                        cn = go // CH + n
                        def precompute(p, n=n):
                            cc = half * 4 + p
                            op_, opk, vt_, vtk_, oa_, oak = loaded[p]
                            B6, B6k = blk.get()
                            S.op("dve", lambda e: e.tensor_tensor(
                                out=RR(B6[:].rearrange("p q (h t) -> p q h t", h=2)),
                                in0=op_[:, :, n * CH:(n + 1) * CH].unsqueeze(2).broadcast_to([128, 6, 2, CH]),
                                in1=bd6[:].rearrange("p q (h t) -> p q h t", h=2), op=ALU.mult), reads=[opk, "bd6"], writes=[B6k])
                            yield
                            AR = RR(B6[:, 0:2, :].rearrange("p q t -> p (q t)"))
                            p1, p1k = ps()
                            S.op("pe", lambda e: e.matmul(p1, lhsT=RR(B6[:, 2, :]), rhs=AR, start=True, stop=True), reads=[B6k], writes=[p1k])
                            s1, s1k = sc.get()
                            S.op("dve", lambda e: e.tensor_tensor(out=RR(s1[:].rearrange("p q t -> p (q t)")), in0=p1,
                                                                  in1=mk2[:].rearrange("p q t -> p (q t)"), op=ALU.mult),
                                 reads=[p1k, "mk2"], writes=[s1k])
                            yield
                            p2, p2k = ps()
                            S.op("pe", lambda e: e.matmul(p2, lhsT=RR(B6[:, 3, :]), rhs=AR, start=True, stop=True), reads=[B6k], writes=[p2k])
                            s2, s2k = sc.get()
                            S.op("dve", lambda e: e.tensor_tensor(out=RR(s2[:].rearrange("p q t -> p (q t)")), in0=p2,
                                                                  in1=mk2[:].rearrange("p q t -> p (q t)"), op=ALU.mult),
                                 reads=[p2k, "mk2"], writes=[s2k])
                            yield
                            p3, p3k = ps()
                            S.op("pe", lambda e: e.matmul(p3[:, 0:128], lhsT=RR(B6[:, 0, :]), rhs=RR(B6[:, 2, :]), start=True, stop=True),
                                 reads=[B6k], writes=[p3k])
                            Lc, Lck = w128.get()
                            S.op("dve", lambda e: e.tensor_tensor(out=RR(Lc[:]), in0=p3[:, 0:128], in1=masks[:, ML, :], op=ALU.mult),
                                 reads=[p3k, "masks"], writes=[Lck])
                            Mc, Mck = s1[:, 0, :], s1k
                            acc, acck = w128.get()
                            S.op("pool", lambda e: e.tensor_tensor(out=RR(acc[:]), in0=Mc, in1=ident, op=ALU.add),
                                 reads=[Mck, "masks"], writes=[acck])
                            yield
                            for it in range(5):
                                pL, pLk = ps()
                                S.op("pe", lambda e: e.matmul(pL[:, 0:128], lhsT=RR(Mc), rhs=RR(Lc[:]), start=True, stop=True),
                                     reads=[Mck, Lck], writes=[pLk])
                                Ln, Lnk = w128.get()
                                S.op("act", lambda e: e.activation(out=RR(Ln[:]), in_=pL[:, 0:128], func=AF.Copy), reads=[pLk], writes=[Lnk])
                                yield
                                if it < 4:
                                    pM, pMk = ps()
                                    S.op("pe", lambda e: e.matmul(pM[:, 0:128], lhsT=RR(Lc[:]), rhs=RR(Mc), start=True, stop=True),
                                         reads=[Mck, Lck], writes=[pMk])
                                    Mn, Mnk = w128.get()
                                    S.op("act", lambda e: e.activation(out=RR(Mn[:]), in_=pM[:, 0:128], func=AF.Copy), reads=[pMk], writes=[Mnk])
                                    yield
                                pA, pAk = ps()
                                S.op("pe", lambda e: e.matmul(pA[:, 0:128], lhsT=RR(Ln[:]), rhs=RR(acc[:]), start=True, stop=True),
                                     reads=[Lnk, acck], writes=[pAk])
                                acc2, acc2k = (w128.get() if it < 4 else ttf.get())
                                S.op("dve", lambda e: e.tensor_tensor(out=RR(acc2[:]), in0=pA[:, 0:128], in1=acc[:], op=ALU.add),
                                     reads=[pAk, acck], writes=[acc2k])
                                yield
                                acc, acck = acc2, acc2k
                                Lc, Lck = Ln, Lnk
                                if it < 4:
                                    Mc, Mck = Mn[:], Mnk
                            ptb, ptbk = ps()

                            def trb(e):
                                e.transpose(ptb[:, 0:128], B6[:, 4, :], ident)
                                return e.transpose(ptb[:, 128:256], B6[:, 5, :], ident)
                            S.op("pe", trb, reads=[B6k, "masks"], writes=[ptbk])
                            bkt, bktk = sc.get()
                            S.op("dve", lambda e: e.tensor_tensor(out=RR(bkt[:].rearrange("p q t -> p (q t)")), in0=ptb,
                                                                  in1=bd6[:, 0:2, :].rearrange("p q t -> p (q t)"), op=ALU.mult),
                                 reads=[ptbk, "bd6"], writes=[bktk])
                            stg[p] = dict(B6=B6, B6k=B6k, s1=s1, s1k=s1k, s2=s2, s2k=s2k, TT_=acc, TTk=acck, bkt=bkt, bktk=bktk)

                        stg = [None] * 4
                        gens = [precompute(p) for p in range(4)]
                        live = list(gens)
                        while live:
                            for g_ in list(live):
                                try:
                                    next(g_)
                                except StopIteration:
                                    live.remove(g_)
                        for p in range(4):
                            g = stg[p]
                            op_, opk, vt_, vtk_, oa_, oak = loaded[p]
                            H, Hk = Hs[p][hcur[p]], ("H", p, hcur[p])
                            pX, pXk = ps()

                            def mm1(e, pX=pX, g=g, H=H, vt_=vt_, n=n):
                                e.matmul(pX[:, 0:CH], lhsT=RR(g["B6"][:, 0, :]), rhs=RR(H[:]), start=True, stop=False)
                                return e.matmul(pX[:, 0:CH], lhsT=RR(g["s2"][:, 0, :]), rhs=RR(vt_[:, n, :]), start=False, stop=True)
                            S.op("pe", mm1, reads=[g["B6k"], Hk, g["s2k"], vtk_], writes=[pXk])
                            X, Xk = w64.get()
                            S.op("act", lambda e, X=X, pX=pX: e.activation(out=RR(X[:]), in_=pX[:, 0:CH], func=AF.Copy), reads=[pXk], writes=[Xk])
                            g.update(X=X, Xk=Xk, H=H, Hk=Hk)
                        for p in range(4):
                            g = stg[p]
                            pU, pUk = ps()
                            S.op("pe", lambda e, pU=pU, g=g: e.matmul(pU[:, 0:CH], lhsT=RR(g["TT_"][:]), rhs=RR(g["X"][:]), start=True, stop=True),
                                 reads=[g["TTk"], g["Xk"]], writes=[pUk])
                            U, Uk = w64.get()
                            S.op("act", lambda e, U=U, pU=pU: e.activation(out=RR(U[:]), in_=pU[:, 0:CH], func=AF.Copy), reads=[pUk], writes=[Uk])
                            g.update(U=U, Uk=Uk)
                        for p in range(4):
                            g = stg[p]
                            cc = half * 4 + p
                            op_, opk, vt_, vtk_, oa_, oak = loaded[p]
                            pO, pOk = ps()

                            def mm3(e, pO=pO, g=g, vt_=vt_, n=n):
                                e.matmul(pO[0:64, 0:128], lhsT=RR(g["H"][:]), rhs=RR(g["B6"][:, 1, :]), start=True, stop=False)
                                e.matmul(pO[0:64, 0:128], lhsT=RR(g["U"][:]), rhs=RR(g["s1"][:, 1, :]), start=False, stop=False)
                                return e.matmul(pO[0:64, 0:128], lhsT=RR(vt_[:, n, :]), rhs=RR(g["s2"][:, 1, :]), start=False, stop=True)
                            S.op("pe", mm3, reads=[g["Hk"], g["B6k"], g["Uk"], g["s1k"], g["s2k"], vtk_], writes=[pOk])
                            S.op("act", lambda e, pO=pO, oa_=oa_, n=n: e.activation(
                                out=oa_[:, :, n * CH:(n + 1) * CH], in_=pO[0:64, 0:128].rearrange("p (h t) -> p h t", h=2), func=AF.Copy),
                                reads=[pOk], writes=[oak])
                            pH, pHk = ps()

                            def mm4(e, pH=pH, g=g, vt_=vt_, n=n):
                                e.matmul(pH[:, 0:CH], lhsT=RR(g["bkt"][:, 0, :]), rhs=RR(g["U"][:]), start=True, stop=False)
                                return e.matmul(pH[:, 0:CH], lhsT=RR(g["bkt"][:, 1, :]), rhs=RR(vt_[:, n, :]), start=False, stop=True)
                            S.op("pe", mm4, reads=[g["bktk"], g["Uk"], vtk_], writes=[pHk])
                            nk = 1 - hcur[p]
                            Hn, Hnk = Hs[p][nk], ("H", p, nk)
                            S.op("dve", lambda e, Hn=Hn, g=g, pH=pH, cc=cc, cn=cn: e.scalar_tensor_tensor(
                                out=RR(Hn[:]), in0=g["H"][:], scalar=PCt[0][:, z, cc, cn:cn + 1], in1=pH[:, 0:CH], op0=ALU.mult, op1=ALU.add),
                                reads=[g["Hk"], pHk], writes=[Hnk])
                            hcur[p] = nk
                    for p in range(4):
                        cc = half * 4 + p
                        op_, opk, vt_, vtk_, oa_, oak = loaded[p]
                        S.dma(lambda e, oa_=oa_, cc=cc, go=go: e.dma_start(out=o_d[z, cc, :, go:go + NT].rearrange("(h v) t -> v h t", h=2), in_=oa_[:]),
                              reads=[oak], writes=[("od", z, cc, go)])
                S.end_phase()

        xcur, ccur = x_in, c_in
        for i in range(depth):
            last = (i == depth - 1)
            xdst = y_out if last else xs[i % 2]
            cdst = cs[i % 2]
            if i % 2 == 0:
                with contextlib.ExitStack() as evs:
                    PCt[0] = mk_alloc(evs)("PC", [128, 2, 8, NCH])
                    even_layer(i, xcur, ccur, xdst, cdst)
            else:
                odd_layer(i, xcur, ccur, xdst, cdst, last)
            xcur, ccur = xdst, cdst
        print("sched ops:", S.nops, {e: S.cnt[e] for e in S.ENG})
    return nc


def host_consts():
    idx = np.arange(128)
    r, c = idx[:, None], idx[None, :]
    same = (r // 64) == (c // 64)
    m = np.zeros((128, 6, 128), np.float32)
    m[:, 0] = same & (r < c)
    m[:, 1] = same & (r <= c)
    m[:, 2] = same & (r > c)
    m[:, 3] = same & (r >= c)
    m[:, 4] = same
    m[:, 5] = (r == c)
    sel = np.zeros((2, 2, 128), np.float32)
    sel[0, 0] = 1.0
    sel[1, 1] = 1.0
    bones = np.zeros((128, 2, 128), np.float32)
    bones[:, 0] = same
    bones[:, 1] = 1.0 / 2048
    rmask = np.ones((128, NT), np.float32)
    rmask[:, ::CH] = 0.0
    return dict(masks=m, sel=sel, bones=bones, rmask=rmask)


def fm(v, nchunk):
    v = np.asarray(v)
    if v.ndim == 1:
        return np.ascontiguousarray(v.reshape(nchunk, 128).T)
    return np.ascontiguousarray(np.moveaxis(v.reshape(v.shape[0], nchunk, 128), 0, -1).transpose(1, 0, 2))


def host_weights(inp, depth):
    n_even = (depth + 1) // 2
    n_odd = depth // 2
    w = {}
    w["modw"] = np.ascontiguousarray(inp["mod_w"][:depth].reshape(depth, 8, 128, 3 * D).transpose(0, 2, 1, 3))
    w["modb"] = np.ascontiguousarray(np.repeat(inp["mod_b"][:depth, None, :], 2, axis=1))
    w["gpre"] = np.ascontiguousarray(np.repeat(inp["g_pre"][:depth, None, :], 2, axis=1))
    w["gpost"] = np.ascontiguousarray(np.repeat(inp["g_post"][:depth, None, :], 2, axis=1))
    if n_even:
        wi = inp["ev_w_in"][:n_even]
        wi = wi.reshape(n_even, 8, 128, 8, 8, 128)
        w["ewin"] = np.ascontiguousarray(wi.transpose(0, 4, 2, 1, 3, 5))
        w["ewout"] = np.ascontiguousarray(inp["ev_w_out"][:n_even].reshape(n_even, 16, 128, D).transpose(0, 2, 1, 3))
        w1 = inp["ev_w1"][:n_even].reshape(n_even, 2, 8, 128, 64)
        a1 = inp["ev_a1"][:n_even].reshape(n_even, 2, 8, 128, 64)
        st = np.stack([w1, a1], axis=1)
        w["ew1"] = np.ascontiguousarray(st.transpose(0, 1, 4, 3, 2, 5).reshape(n_even, 2, 128, 8, 128))
        w2 = inp["ev_w2"][:n_even].reshape(n_even, 128, D)
        a2 = inp["ev_a2"][:n_even].reshape(n_even, 128, D)
        w["ew2"] = np.ascontiguousarray(np.stack([w2, a2], axis=1))
        ev = np.zeros((n_even, 128, 8, 17), np.float32)
        for j in range(n_even):
            vecs = [inp["ev_mu_rkv"][j, 0], inp["ev_mu_rkv"][j, 1], inp["ev_mu_rkv"][j, 2], inp["ev_w0"][j, 0], inp["ev_w0"][j, 1],
                    inp["ev_a0"][j, 0], inp["ev_a0"][j, 1], inp["ev_k_k"][j], inp["ev_k_a"][j], inp["ev_r_k"][j, 0].reshape(-1),
                    inp["ev_r_k"][j, 1].reshape(-1), inp["ev_lnx_w"][j], inp["ev_lnx_b"][j], inp["ev_sc_w"][j, 0], inp["ev_sc_w"][j, 1],
                    inp["ev_sc_w"][j, 2]]
            for k, v in enumerate(vecs):
                ev[j, :, :, k] = v.reshape(8, 128).T
        w["evec"] = ev
        mw = inp["ev_mu_wa"][:n_even]
        w["emuwa"] = np.ascontiguousarray(mw.reshape(n_even, 2, 8, 128).transpose(0, 3, 2, 1))
    if n_odd:
        wi = inp["od_w_in"][:n_odd].reshape(n_odd, 8, 128, 3, 16, 128)
        w["owin"] = np.ascontiguousarray(wi.transpose(0, 4, 2, 1, 3, 5))
        w["owout"] = np.ascontiguousarray(inp["od_w_out"][:n_odd].reshape(n_odd, 16, 128, D).transpose(0, 2, 1, 3))
        dw = inp["od_dw_w"][:n_odd]
        w["odww"] = np.ascontiguousarray(dw.reshape(n_odd, 31, 16, 128).transpose(0, 3, 2, 1))
        ov = np.stack([inp["od_dw_b"][:n_odd], inp["od_ln_w"][:n_odd], inp["od_ln_b"][:n_odd]], axis=-1)
        w["ovec"] = np.ascontiguousarray(ov.reshape(n_odd, 16, 128, 3).transpose(0, 2, 1, 3))
    return w


_CACHE = {}


def run(inputs, depth, ncores, T, NCTX, trace=False):
    inp = {k: np.asarray(v, dtype=np.float32) for k, v in inputs.items()}
    key = (T, NCTX, depth)
    if key not in _CACHE:
        _CACHE[key] = build(T, NCTX, depth)
    nc = _CACHE[key]
    shared = host_consts()
    shared.update(host_weights(inp, depth))
    in_maps = []
    for b in range(ncores):
        m = dict(shared)
        m["x"] = np.ascontiguousarray(inp["x"][b])
        m["ctx"] = np.ascontiguousarray(inp["ctx"][b])
        sv = np.stack([inp["c"][b], inp["c_ctx"]], axis=-1)
        m["svec"] = np.ascontiguousarray(sv.reshape(8, 128, 2).transpose(1, 0, 2))
        in_maps.append(m)
    res = run_bass_kernel_spmd(nc, in_maps, core_ids=list(range(ncores)), trace=trace)
    out = np.stack([np.asarray(r["y"], dtype=np.float32) for r in res.results], axis=0)
    return out, res


def kernel(**inputs):
    out, _ = run(inputs, 4, 8, 4096, 256)
    return out
```

```python
import contextlib
import math
import numpy as np
import concourse.bass as bass
import concourse.mybir as mybir
from concourse.bass_utils import run_bass_kernel_spmd

F32 = mybir.dt.float32
F32R = mybir.dt.float32r
USE_R = True
AF = mybir.ActivationFunctionType
ALU = mybir.AluOpType

D = 1024
NT = 256
CH = 64
NDMASEM = 32
SAME_ENGINE_SYNC = True
DECAY_SCALE = -math.exp(-0.5)


class Rec:
    def __init__(self):
        self.calls = []

    def __getattr__(self, name):
        def f(*a, **k):
            self.calls.append((name, a, k))
            return self
        return f


def _replay(calls, engine):
    ins = None
    for name, a, k in calls:
        ins = getattr(engine, name)(*a, **k)
    return ins


class Sched:
    ENG = ("pe", "dve", "act", "pool", "sp")
    OBJ = {"pe": "tensor", "dve": "vector", "act": "scalar", "pool": "gpsimd", "sp": "sync"}

    def __init__(self, nc, stack):
        self.nc = nc
        self.prog = {e: [] for e in self.ENG}
        self.sem = {e: stack.enter_context(nc.semaphore("s_" + e)) for e in self.ENG if e != "sp"}
        self.cnt = {e: 0 for e in self.ENG}
        self.dsem = [stack.enter_context(nc.semaphore("d%d" % i)) for i in range(NDMASEM)]
        self.dcnt = [0] * NDMASEM
        self.dnext = 0
        self.dnext2 = 0
        self.waited = {e: {} for e in self.ENG}
        self.lastw = {}
        self.readers = {}
        self.nops = 0
        import os
        self.kstop = int(os.environ.get("KSTOP", "1000"))
        self.kmax = int(os.environ.get("KMAXOPS", "100000000"))
        self.phase = 0

    def _semobj(self, key):
        return self.sem[key] if isinstance(key, str) else self.dsem[key]

    def _deps(self, eng, reads, writes):
        deps = {}

        def add(k, v):
            if deps.get(k, 0) < v:
                deps[k] = v

        for b in reads:
            w = self.lastw.get(b)
            if w is not None:
                add(*w)
        for b in writes:
            w = self.lastw.get(b)
            if w is not None:
                add(*w)
            for r in self.readers.get(b, ()):
                add(*r)
        out = []
        for k, v in deps.items():
            if k == eng and (eng == "pe" or not SAME_ENGINE_SYNC):
                continue
            if self.waited[eng].get(k, 0) >= v:
                continue
            self.waited[eng][k] = v
            out.append((k, v))
        return out

    def _commit(self, token, reads, writes):
        for b in writes:
            self.lastw[b] = token
            self.readers[b] = []
        for b in reads:
            self.readers.setdefault(b, []).append(token)

    def op(self, eng, fn, reads=(), writes=()):
        if self.phase >= self.kstop or self.nops >= self.kmax:
            return
        waits = self._deps(eng, reads, writes)
        self.cnt[eng] += 1
        rec = Rec()
        fn(rec)
        calls = rec.calls
        self.prog[eng].append((waits, (lambda engine, calls=calls: _replay(calls, engine)), (eng, 1)))
        self._commit((eng, self.cnt[eng]), reads, writes)
        self.nops += 1

    def dma(self, fn, reads=(), writes=(), q="sp"):
        if self.phase >= self.kstop or self.nops >= self.kmax:
            return
        half = NDMASEM // 2
        if q == "sp":
            i = self.dnext
            self.dnext = (self.dnext + 1) % half
        else:
            i = half + self.dnext2
            self.dnext2 = (self.dnext2 + 1) % (NDMASEM - half)
        waits = self._deps(q, reads, writes)
        if self.dcnt[i] > 0 and self.waited[q].get(i, 0) < self.dcnt[i]:
            self.waited[q][i] = self.dcnt[i]
            waits.append((i, self.dcnt[i]))
        self.dcnt[i] += 16
        rec = Rec()
        fn(rec)
        calls = rec.calls
        self.prog[q].append((waits, (lambda engine, calls=calls: _replay(calls, engine)), (i, 16)))
        self._commit((i, self.dcnt[i]), reads, writes)
        self.nops += 1

    def end_phase(self):
        self.phase += 1
        if self.phase > self.kstop:
            return
        print("phase", self.phase, "nops", self.nops)
        waits = [(e, self.cnt[e]) for e in self.ENG if e != "sp" and self.cnt[e] > 0]
        waits += [(i, self.dcnt[i]) for i in range(NDMASEM) if self.dcnt[i] > 0]
        self.prog["sp"].append((waits, None, None))
        nc = self.nc
        with nc.Block() as block:
            for e in self.ENG:
                prog = self.prog[e]
                if not prog:
                    continue

                def body(engine, prog=prog):
                    for waits, fn, inc in prog:
                        for k, v in waits:
                            engine.wait_ge(self._semobj(k), v)
                        if fn is None:
                            continue
                        ins = fn(engine)
                        ins.then_inc(self._semobj(inc[0]), inc[1])

                getattr(block, self.OBJ[e])(body)
        self.prog = {e: [] for e in self.ENG}
        self.lastw = {}
        self.readers = {}


class Rot:
    def __init__(self, alloc, name, shape, n):
        self.tiles = [alloc("%s%d" % (name, i), shape) for i in range(n)]
        self.keys = ["%s%d" % (name, i) for i in range(n)]
        self.i = 0

    def get(self):
        t, k = self.tiles[self.i], self.keys[self.i]
        self.i = (self.i + 1) % len(self.tiles)
        return t, k


def build(T, NCTX, depth):
    TT = NCTX + T
    TTP = TT + 4
    NCH = TT // CH
    nc = bass.Bass("TRN2", target_bir_lowering=False)
    dt = lambda name, shape, kind="ExternalInput": nc.dram_tensor(name, shape, F32, kind=kind).ap()
    n_even = (depth + 1) // 2
    n_odd = depth // 2
    x_in = dt("x", [T, D])
    c_in = dt("ctx", [NCTX, D])
    svec_d = dt("svec", [128, 8, 2])
    masks_d = dt("masks", [128, 6, 128])
    sel_d = dt("sel", [2, 2, 128])
    bones_d = dt("bones", [128, 2, 128])
    rmask_d = dt("rmask", [128, NT])
    modw_d = dt("modw", [depth, 128, 8, 3 * D])
    modb_d = dt("modb", [depth, 2, 3 * D])
    gpre_d = dt("gpre", [depth, 2, D])
    gpost_d = dt("gpost", [depth, 2, D])
    if n_even:
        ewin_d = dt("ewin", [n_even, 8, 128, 8, 8, 128])
        ewout_d = dt("ewout", [n_even, 128, 16, D])
        ew1_d = dt("ew1", [n_even, 2, 128, 8, 128])
        ew2_d = dt("ew2", [n_even, 2, 128, D])
        evec_d = dt("evec", [n_even, 128, 8, 17])
        emuwa_d = dt("emuwa", [n_even, 128, 8, 2])
    if n_odd:
        owin_d = dt("owin", [n_odd, 16, 128, 8, 3, 128])
        owout_d = dt("owout", [n_odd, 128, 16, D])
        odww_d = dt("odww", [n_odd, 128, 16, 31])
        ovec_d = dt("ovec", [n_odd, 128, 16, 3])
    y_out = dt("y", [T, D], "ExternalOutput")
    xs = [dt("xsa", [T, D], "Internal"), dt("xsb", [T, D], "Internal")]
    cs = [dt("csa", [NCTX, D], "Internal"), dt("csb", [NCTX, D], "Internal")]
    if n_even:
        hT_d = dt("hTd", [128, 8, TTP], "Internal")
        scn_d = dt("scn", [2, 8, 128, 6, TT], "Internal")
        vtok_d = dt("vtok", [8, NCH, 128, CH], "Internal")
        o_d = dt("od", [2, 8, 128, TT], "Internal")
        misc_d = dt("miscd", [3, 8, 128, TT], "Internal")

    with contextlib.ExitStack() as gs:
        S = Sched(nc, gs)

        uid = {"n": 0}

        def mk_alloc(stack):
            def alloc(name, shape):
                uid["n"] += 1
                return stack.enter_context(nc.sbuf_tensor("sb%d_%s" % (uid["n"], name), shape, F32))
            return alloc

        galloc = mk_alloc(gs)
        RR = (lambda ap: ap.bitcast(F32R)) if USE_R else (lambda ap: ap)
        WQ = "pool" if USE_R else "sp"
        masks = galloc("masks", [128, 6, 128])
        sel = galloc("sel", [2, 2, 128])
        bones = galloc("bones", [128, 2, 128])
        rmask = galloc("rmask", [128, NT])
        bonesr = galloc("bonesr", [128, 128])
        ssil = galloc("ssil", [128, 8, 2])
        bc = {w: [galloc("bc%s%d" % (w, k), [128, D]) for k in range(3)] for w in ("l", "c")}
        PCt = [None]
        zero = galloc("zero", [128, 8, 2])
        pst = [gs.enter_context(nc.psum_tensor("ps%d" % i, [128, 512], F32)) for i in range(8)]
        ps_state = {"i": 0}

        ps_state["nrot"] = 6

        def ps():
            i = ps_state["i"] % ps_state["nrot"]
            ps_state["i"] = (i + 1) % ps_state["nrot"]
            return pst[i][:, 0:256], "ps%d" % i

        def psbank():
            i = ps_state["i"] % ps_state["nrot"]
            ps_state["i"] = (i + 1) % ps_state["nrot"]
            return pst[i], ("ps%d" % i,)

        def accbank(k):
            return pst[6 + k][:, 0:256], "accbank%d" % k

        S.dma(lambda e: e.dma_start(out=masks[:], in_=masks_d), writes=["masks"])
        S.dma(lambda e: e.dma_start(out=sel[:], in_=sel_d), writes=["sel"])
        S.dma(lambda e: e.dma_start(out=bones[:], in_=bones_d), writes=["bones"])
        S.dma(lambda e: e.dma_start(out=rmask[:], in_=rmask_d), writes=["rmask"])
        S.dma(lambda e: e.dma_start(out=RR(bonesr[:]), in_=bones_d[:, 0, :]), writes=["bonesr"], q=WQ)
        S.dma(lambda e: e.dma_start(out=ssil[:], in_=svec_d), writes=["ssil"])
        S.op("act", lambda e: e.activation(out=ssil[:], in_=ssil[:], func=AF.Silu), reads=["ssil"], writes=["ssil"])
        S.op("dve", lambda e: e.memset(zero[:], 0.0), writes=["zero"])
        if n_even:
            for col in (0, NCTX + 1, NCTX + 2, NCTX + 3 + T):
                S.dma(lambda e, col=col: e.dma_start(out=hT_d[:, :, col:col + 1], in_=zero[:, :, 0:1], allow_slow_non_contiguous=True),
                      reads=["zero"], writes=[("hTdz", col)])
        S.end_phase()

        ident = masks[:, 5, :]
        BD = masks[:, 4, :]

        def seq_tiles(with_ctx):
            tl = []
            if with_ctx:
                for t0 in range(0, NCTX, NT):
                    tl.append(("c", t0))
            for t0 in range(0, T, NT):
                tl.append(("l", t0))
            return tl

        def goff(seq, t0):
            return t0 if seq == "c" else NCTX + t0

        def poff(seq, t0):
            return 1 + t0 if seq == "c" else NCTX + 3 + t0

        def x_rows(src, seq, t0, sub, col):
            p0 = t0 + sub * 128
            if seq == "c" or not col:
                return [((0, 128), src[p0:p0 + 128, :])]
            v = src.rearrange("(r c) d -> c r d", c=64)
            c0 = p0 // 64
            return [((0, 64), v[c0]), ((64, 128), v[c0 + 1])]

        def modulation(i, alloc):
            rows = alloc("modrow", [2, 3 * D])
            mb = alloc("modb", [2, 3 * D])
            gp = alloc("gp2", [2, 2, D])
            wrot = Rot(alloc, "modw", [128, 8, 512], 2)
            S.dma(lambda e: e.dma_start(out=mb[:], in_=modb_d[i]), writes=["modb"])
            S.dma(lambda e: e.dma_start(out=gp[:, 0, :], in_=gpre_d[i]), writes=["gp0"])
            S.dma(lambda e: e.dma_start(out=gp[:, 1, :], in_=gpost_d[i]), writes=["gp1"])
            for cb in range(6):
                wt, wk = wrot.get()
                S.dma(lambda e, wt=wt, cb=cb: e.dma_start(out=wt[:], in_=modw_d[i, :, :, cb * 512:(cb + 1) * 512]),
                      writes=[wk])
                pt, pk = psbank()

                def mm(e, wt=wt, pt=pt):
                    for dc in range(8):
                        ins = e.matmul(pt[0:2, :], lhsT=ssil[:, dc, :], rhs=wt[:, dc, :], start=(dc == 0), stop=(dc == 7))
                    return ins
                S.op("pe", mm, reads=[wk, "ssil"], writes=list(pk))
                S.op("dve", lambda e, pt=pt, cb=cb: e.tensor_tensor(out=rows[:, cb * 512:(cb + 1) * 512], in0=pt[0:2, :],
                                                                    in1=mb[:, cb * 512:(cb + 1) * 512], op=ALU.add),
                     reads=list(pk) + ["modb"], writes=[("rows", cb)])
            allrows = [("rows", cb) for cb in range(6)]
            S.op("dve", lambda e: e.scalar_tensor_tensor(out=rows[:, D:2 * D], in0=rows[:, D:2 * D], scalar=1.0,
                                                         in1=gp[:, 0, :], op0=ALU.add, op1=ALU.mult),
                 reads=allrows + ["gp0"], writes=allrows)
            S.op("dve", lambda e: e.tensor_tensor(out=rows[:, 2 * D:3 * D], in0=rows[:, 2 * D:3 * D], in1=gp[:, 1, :],
                                                  op=ALU.mult), reads=allrows + ["gp1"], writes=allrows)
            for wi, w in enumerate(("l", "c")):
                for k, slot in enumerate((1, 0, 2)):
                    for half in range(2):
                        pt, pk = psbank()
                        S.op("pe", lambda e, pt=pt, wi=wi, slot=slot, half=half: e.matmul(
                            pt[:, :], lhsT=sel[:, wi, :], rhs=rows[:, slot * D + half * 512: slot * D + half * 512 + 512],
                            start=True, stop=True), reads=allrows + ["sel"], writes=list(pk))
                        S.op("act", lambda e, pt=pt, w=w, k=k, half=half: e.activation(
                            out=bc[w][k][:, half * 512:(half + 1) * 512], in_=pt[:, :], func=AF.Copy),
                            reads=list(pk), writes=[("bc", w, k, half)])

        def bckeys(w, k):
            return [("bc", w, k, 0), ("bc", w, k, 1)]

        def prenorm_sub(src, seq, t0, sub, col, xt, xk, hdst, hk, tmp_rot, st_rot):
            w = seq
            for (pa, pb), ap in x_rows(src, seq, t0, sub, col):
                S.dma(lambda e, pa=pa, pb=pb, ap=ap: e.dma_start(out=xt[pa:pb, :], in_=ap), writes=[xk])
            junk, jk = tmp_rot.get()
            st, sk = st_rot.get()
            S.op("act", lambda e: e.activation(out=junk[:], in_=xt, func=AF.Square, accum_out=st[:, 0:1]),
                 reads=[xk], writes=[jk, sk])
            S.op("act", lambda e: e.activation(out=st[:, 1:2], in_=st[:, 0:1], func=AF.Sqrt, scale=1.0 / D, bias=1e-6),
                 reads=[sk], writes=[sk])
            S.op("dve", lambda e: e.reciprocal(out=st[:, 2:3], in_=st[:, 1:2]), reads=[sk], writes=[sk])
            S.op("dve", lambda e: e.scalar_tensor_tensor(out=junk[:], in0=xt, scalar=st[:, 2:3], in1=bc[w][0][:],
                                                         op0=ALU.mult, op1=ALU.mult),
                 reads=[xk, sk] + bckeys(w, 0), writes=[jk])
            S.op("dve", lambda e: e.tensor_tensor(out=junk[:], in0=junk[:], in1=bc[w][1][:], op=ALU.add),
                 reads=[jk] + bckeys(w, 1), writes=[jk])
            for half in range(2):
                pt, pk = psbank()

                def tr(e, pt=pt, half=half):
                    for q in range(4):
                        dc = half * 4 + q
                        ins = e.transpose(pt[:, q * 128:(q + 1) * 128], junk[:, dc * 128:(dc + 1) * 128], ident)
                    return ins
                S.op("pe", tr, reads=[jk, "masks"], writes=list(pk))
                for q in range(4):
                    dc = half * 4 + q
                    eng = "act" if q % 2 == 0 else "dve"
                    if eng == "act":
                        S.op("act", lambda e, pt=pt, q=q, dc=dc: e.activation(out=RR(hdst(dc)), in_=pt[:, q * 128:(q + 1) * 128],
                                                                              func=AF.Copy), reads=list(pk), writes=[hk])
                    else:
                        S.op("dve", lambda e, pt=pt, q=q, dc=dc: e.tensor_copy(out=RR(hdst(dc)), in_=pt[:, q * 128:(q + 1) * 128]),
                             reads=list(pk), writes=[hk])

        def outproj_post(G, gkeys, wo, wok, nk, seq, t0, col, xt_tiles, dst, tmp_rot, st_rot):
            w = seq
            for sub in range(NT // 128):
                xt, xk = xt_tiles[sub]
                pts = [psbank(), psbank()]
                for half in range(2):
                    pt, pk = pts[half]

                    def mm(e, pt=pt, half=half, sub=sub):
                        for kc in range(nk):
                            ins = e.matmul(pt[:, :], lhsT=RR(G[:, kc, sub * 128:(sub + 1) * 128]),
                                           rhs=RR(wo[:, kc, half * 512:(half + 1) * 512]), start=(kc == 0), stop=(kc == nk - 1))
                        return ins
                    S.op("pe", mm, reads=list(gkeys) + [wok], writes=list(pk))
                st, sk = st_rot.get()
                junk, jk = tmp_rot.get()
                for half in range(2):
                    pt, pk = pts[half]
                    S.op("act", lambda e, pt=pt, half=half: e.activation(out=junk[:, half * 512:(half + 1) * 512], in_=pt[:, :],
                                                                        func=AF.Square, accum_out=st[:, half:half + 1]),
                         reads=list(pk), writes=[jk, sk])
                S.op("dve", lambda e: e.tensor_tensor(out=st[:, 2:3], in0=st[:, 0:1], in1=st[:, 1:2], op=ALU.add),
                     reads=[sk], writes=[sk])
                S.op("act", lambda e: e.activation(out=st[:, 3:4], in_=st[:, 2:3], func=AF.Sqrt, scale=1.0 / D, bias=1e-6),
                     reads=[sk], writes=[sk])
                S.op("dve", lambda e: e.reciprocal(out=st[:, 4:5], in_=st[:, 3:4]), reads=[sk], writes=[sk])
                for half in range(2):
                    pt, pk = pts[half]
                    S.op("dve", lambda e, pt=pt, half=half: e.scalar_tensor_tensor(
                        out=junk[:, half * 512:(half + 1) * 512], in0=pt[:, :], scalar=st[:, 4:5],
                        in1=bc[w][2][:, half * 512:(half + 1) * 512], op0=ALU.mult, op1=ALU.mult),
                        reads=list(pk) + [sk, ("bc", w, 2, half)], writes=[jk])
                S.op("dve", lambda e, xt=xt: e.tensor_tensor(out=xt, in0=xt, in1=junk[:], op=ALU.add),
                     reads=[xk, jk], writes=[xk])
                for (pa, pb), ap in x_rows(dst, seq, t0, sub, col):
                    S.dma(lambda e, pa=pa, pb=pb, ap=ap, xt=xt: e.dma_start(out=ap, in_=xt[pa:pb, :]), reads=[xk],
                          writes=[("dst", seq, t0, sub, pa)])

        def odd_layer(i, xsrc, csrc, xdst, cdst, last):
            j = i // 2
            col = (j % 2 == 1)
            with contextlib.ExitStack() as ph:
                alloc = mk_alloc(ph)
                modulation(i, alloc)
                S.end_phase()
            with contextlib.ExitStack() as ph:
                alloc = mk_alloc(ph)
                wo = alloc("wo", [128, 16, D])
                dww = alloc("dww", [128, 16, 31])
                ov = alloc("ov", [128, 16, 3])
                for q in range(4):
                    S.dma(lambda e, q=q: e.dma_start(out=RR(wo[:, q * 4:(q + 1) * 4, :]), in_=owout_d[j, :, q * 4:(q + 1) * 4, :]),
                          writes=[("wo", q)], q=WQ)
                wok = ("wo", 0)
                wokeys = [("wo", q) for q in range(4)]
                S.dma(lambda e: e.dma_start(out=dww[:], in_=odww_d[j]), writes=["dww"])
                S.dma(lambda e: e.dma_start(out=ov[:], in_=ovec_d[j]), writes=["ov"])
                hT = alloc("hT", [128, 8, NT])
                A = alloc("A", [128, 16, NT])
                SZ = alloc("SZ", [128, 16, NT])
                xrot = Rot(alloc, "xt", [128, 2, D], 1)
                wrot = Rot(alloc, "wi", [128, 8, 3, 128], 3)
                tmp_rot = Rot(alloc, "tmpw", [128, D], 1)
                st_rot = Rot(alloc, "st", [128, 8], 4)
                sm_rot = Rot(alloc, "sm", [128, NT], 5)
                apad = Rot(alloc, "apad", [128, 384], 3)
                dgp = Rot(alloc, "dg", [128, 128], 18)
                ps_state["nrot"] = 4
                lay = {"cur": None}
                stat = alloc("stat", [128, 3, NT])
                for (seq, t0) in seq_tiles(not last):
                    src = csrc if seq == "c" else xsrc
                    dst = cdst if seq == "c" else xdst
                    seg = NCTX if seq == "c" else 64
                    nseg = NT // seg if seg <= NT else 1
                    seg = min(seg, NT)
                    if lay["cur"] != (seg, nseg):
                        lay["cur"] = (seg, nseg)
                        for t_, k_ in zip(apad.tiles, apad.keys):
                            S.op("act", lambda e, t_=t_: e.activation(out=RR(t_[:]), in_=bc["l"][0][:, 0:384], func=AF.Copy, scale=0.0),
                                 reads=bckeys("l", 0), writes=[k_])
                    xt3, xk3 = xrot.get()
                    xt_tiles = [(xt3[:, sub, :], (xk3, sub)) for sub in range(2)]
                    for sub in range(2):
                        prenorm_sub(src, seq, t0, sub, col, xt_tiles[sub][0], xt_tiles[sub][1],
                                    lambda dc, sub=sub: hT[:, dc, sub * 128:(sub + 1) * 128], "hT", tmp_rot, st_rot)
                    pmean, pmk = accbank(0)
                    pex2, pek = accbank(1)
                    statcnt = {"n": 0}

                    def cc_body(cc, seg=seg, nseg=nseg):
                        wi, wk = wrot.get()
                        S.dma(lambda e: e.dma_start(out=RR(wi[:]), in_=owin_d[j, cc]), writes=[wk], q=WQ)
                        yield

                        def mmg(pt, blk):
                            def mm(e):
                                for dc in range(8):
                                    ins = e.matmul(pt, lhsT=RR(wi[:, dc, blk, :]), rhs=RR(hT[:, dc, :]), start=(dc == 0), stop=(dc == 7))
                                return ins
                            return mm
                        pg, pgk = ps()
                        S.op("pe", mmg(pg, 1), reads=[wk, "hT"], writes=[pgk])
                        sg, sgk = sm_rot.get()
                        S.op("act", lambda e: e.activation(out=sg[:], in_=pg, func=AF.Sigmoid), reads=[pgk], writes=[sgk])
                        yield
                        pu, puk = ps()
                        S.op("pe", mmg(pu, 0), reads=[wk, "hT"], writes=[puk])
                        apt, ak = apad.get()
                        av = apt[:, 0:nseg * (seg + 30)].rearrange("p (s t) -> p s t", s=nseg)
                        S.op("dve", lambda e: e.tensor_tensor(out=RR(av[:, :, 15:15 + seg]), in0=pu.rearrange("p (s t) -> p s t", s=nseg),
                                                              in1=sg[:].rearrange("p (s t) -> p s t", s=nseg), op=ALU.mult),
                             reads=[puk, sgk, ak], writes=[ak])
                        yield
                        pz, pzk = ps()
                        S.op("pe", mmg(pz, 2), reads=[wk, "hT"], writes=[pzk])
                        S.op("act", lambda e: e.activation(out=RR(SZ[:, cc, :]), in_=pz, func=AF.Silu), reads=[pzk], writes=[("SZ", cc)])
                        yield
                        pcv, pcvk = pst[4 + (cc % 2)][:, 0:256], "cvbank%d" % (cc % 2)
                        pcv3 = pcv.rearrange("p (s t) -> p s t", s=nseg)
                        for g0 in range(0, 31, 8):
                            grp = list(range(g0, min(g0 + 8, 31)))
                            dts = []
                            for tap in grp:
                                dgt, dgk = dgp.get()
                                if tap % 4 == 0:
                                    S.op("act", lambda e: e.activation(out=RR(dgt[:]), in_=ident, func=AF.Copy, scale=dww[:, cc, tap:tap + 1]),
                                         reads=["masks", "dww"], writes=[dgk])
                                else:
                                    S.op("dve", lambda e: e.tensor_scalar(out=RR(dgt[:]), in0=ident, scalar1=dww[:, cc, tap:tap + 1], scalar2=None,
                                                                          op0=ALU.mult), reads=["masks", "dww"], writes=[dgk])
                                dts.append((dgt, dgk))

                            def cmm(e):
                                for (dgt_, _), tap in zip(dts, grp):
                                    ins = e.matmul(pcv3, lhsT=RR(dgt_[:]), rhs=RR(av[:, :, tap:tap + seg]), start=(tap == 0), stop=(tap == 30))
                                return ins
                            S.op("pe", cmm, reads=[ak] + [k for _, k in dts], writes=[pcvk])
                            yield
                        S.op("act", lambda e: e.activation(out=A[:, cc, :], in_=pcv, func=AF.Identity, bias=ov[:, cc, 0:1], scale=1.0),
                             reads=[pcvk, "ov"], writes=[("A", cc)])
                        yield
                        sq, sqk = sm_rot.get()
                        S.op("act", lambda e: e.activation(out=sq[:], in_=A[:, cc, :], func=AF.Square), reads=[("A", cc)], writes=[sqk])
                        k_ = statcnt["n"]
                        statcnt["n"] += 1
                        S.op("pe", lambda e: e.matmul(pmean, lhsT=bones[:, 1, :], rhs=A[:, cc, :], start=(k_ == 0), stop=(k_ == 15)),
                             reads=[("A", cc), "bones"], writes=[pmk])
                        S.op("pe", lambda e: e.matmul(pex2, lhsT=bones[:, 1, :], rhs=sq[:], start=(k_ == 0), stop=(k_ == 15)),
                             reads=[sqk, "bones"], writes=[pek])

                    NSTR = 2
                    pending = [cc_body(cc) for cc in range(16)]
                    live = []
                    while pending or live:
                        while pending and len(live) < NSTR:
                            live.append(pending.pop(0))
                        for g_ in list(live):
                            try:
                                next(g_)
                            except StopIteration:
                                live.remove(g_)
                    S.op("act", lambda e: e.activation(out=stat[:, 0, :], in_=pmean, func=AF.Copy), reads=[pmk], writes=["stat"])
                    S.op("dve", lambda e: e.tensor_tensor(out=stat[:, 1, :], in0=stat[:, 0, :], in1=stat[:, 0, :], op=ALU.mult),
                         reads=["stat"], writes=["stat"])
                    S.op("dve", lambda e: e.tensor_tensor(out=stat[:, 1, :], in0=pex2, in1=stat[:, 1, :], op=ALU.subtract),
                         reads=["stat", pek], writes=["stat"])
                    S.op("act", lambda e: e.activation(out=stat[:, 2, :], in_=stat[:, 1, :], func=AF.Sqrt, bias=1e-5, scale=1.0),
                         reads=["stat"], writes=["stat"])
                    S.op("dve", lambda e: e.reciprocal(out=stat[:, 2, :], in_=stat[:, 2, :]), reads=["stat"], writes=["stat"])
                    for cc in range(16):
                        S.op("dve", lambda e, cc=cc: e.tensor_tensor(out=A[:, cc, :], in0=A[:, cc, :], in1=stat[:, 0, :],
                                                                     op=ALU.subtract), reads=[("A", cc), "stat"], writes=[("A", cc)])
                        S.op("dve", lambda e, cc=cc: e.tensor_tensor(out=A[:, cc, :], in0=A[:, cc, :], in1=stat[:, 2, :],
                                                                     op=ALU.mult), reads=[("A", cc), "stat"], writes=[("A", cc)])
                        S.op("act", lambda e, cc=cc: e.activation(out=A[:, cc, :], in_=A[:, cc, :], func=AF.Silu,
                                                                  scale=ov[:, cc, 1:2], bias=ov[:, cc, 2:3]),
                             reads=[("A", cc), "ov"], writes=[("A", cc)])
                        S.op("dve", lambda e, cc=cc: e.tensor_tensor(out=RR(SZ[:, cc, :]), in0=A[:, cc, :], in1=SZ[:, cc, :],
                                                                     op=ALU.mult), reads=[("A", cc), ("SZ", cc)], writes=[("SZ", cc)])
                    outproj_post(SZ, [("SZ", cc) for cc in range(16)] + wokeys[1:], wo, wok, 16, seq, t0, col, xt_tiles, dst,
                                 tmp_rot, st_rot)
                ps_state["nrot"] = 6
                S.end_phase()

        def even_layer(i, xsrc, csrc, xdst, cdst):
            j = i // 2
            col = (j % 2 == 1)
            tiles = seq_tiles(True)
            with contextlib.ExitStack() as ph:
                alloc = mk_alloc(ph)
                modulation(i, alloc)
                S.end_phase()
            with contextlib.ExitStack() as ph:
                alloc = mk_alloc(ph)
                xrot = Rot(alloc, "xt", [128, D], 3)
                hrot = Rot(alloc, "hTs", [128, 8, NT], 2)
                tmp_rot = Rot(alloc, "tmpw", [128, D], 2)
                st_rot = Rot(alloc, "st", [128, 8], 4)
                for (seq, t0) in tiles:
                    src = csrc if seq == "c" else xsrc
                    hT, hk = hrot.get()
                    for sub in range(2):
                        xt, xk = xrot.get()
                        prenorm_sub(src, seq, t0, sub, col, xt[:], xk, lambda dc, sub=sub, hT=hT: hT[:, dc, sub * 128:(sub + 1) * 128],
                                    hk, tmp_rot, st_rot)
                    po = poff(seq, t0)
                    S.dma(lambda e, hT=hT, po=po: e.dma_start(out=hT_d[:, :, po:po + NT], in_=hT[:]), reads=[hk],
                          writes=[("hTd", seq, t0)])
                S.end_phase()
            with contextlib.ExitStack() as ph:
                alloc = mk_alloc(ph)
                ev = alloc("ev", [128, 8, 17])
                muwa = alloc("muwa", [128, 8, 2])
                w1s = alloc("w1s", [128, 2, 8, 128])
                w2s = alloc("w2s", [128, 2, D])
                omka = alloc("omka", [128, 8])
                S.dma(lambda e: e.dma_start(out=ev[:], in_=evec_d[j]), writes=["ev"])
                S.dma(lambda e: e.dma_start(out=muwa[:], in_=emuwa_d[j]), writes=["muwa"])
                for q in range(2):
                    S.dma(lambda e, q=q: e.dma_start(out=RR(w1s[:, q]), in_=ew1_d[j, q]), writes=["w1s"], q=WQ)
                    S.dma(lambda e, q=q: e.dma_start(out=RR(w2s[:, q]), in_=ew2_d[j, q]), writes=["w2s"], q=WQ)
                S.op("dve", lambda e: e.tensor_scalar(out=omka[:], in0=ev[:, :, 8], scalar1=-1.0, scalar2=1.0, op0=ALU.mult,
                                                      op1=ALU.add), reads=["ev"], writes=["omka"])
                hh = alloc("hh", [128, 8, NT + 2])
                dh = alloc("dh", [128, 8, NT])
                smr = Rot(alloc, "smr", [128, NT], 8)
                lo = alloc("lo", [128, 2, NT])
                wrot = Rot(alloc, "wi", [128, 8, 8, 128], 2)
                sm = Rot(alloc, "sm", [128, NT], 38)
                ll = Rot(alloc, "ll", [128, NT], 12)
                q6 = Rot(alloc, "q6", [128, 6, NT], 2)
                vt_rot = Rot(alloc, "vts", [128, 2, 128], 2)
                VEC = dict(mu_r=0, mu_k=1, mu_v=2, w0=3, a0=5, k_k=7, k_a=8, r_k=9, lnw=11, lnb=12, sc=13)
                for (seq, t0) in tiles:
                    po = poff(seq, t0)
                    go = goff(seq, t0)
                    seg = min(NCTX if seq == "c" else 64, NT)
                    nseg = NT // seg
                    S.dma(lambda e, po=po: e.dma_start(out=RR(hh[:]), in_=hT_d[:, :, po - 1:po + NT + 1]), q=WQ,
                          reads=[("hTd", seq, t0), ("hTd", seq, t0 - NT), ("hTd", seq, t0 + NT)] + [("hTdz", c) for c in
                                                                                                     (0, NCTX + 1, NCTX + 2, NCTX + 3 + T)],
                          writes=["hh"])
                    S.op("dve", lambda e: e.tensor_tensor(out=RR(dh[:]), in0=hh[:, :, 0:NT], in1=hh[:, :, 2:NT + 2], op=ALU.add),
                         reads=["hh"], writes=["dh"])
                    S.op("dve", lambda e: e.scalar_tensor_tensor(out=RR(dh[:]), in0=dh[:], scalar=0.5, in1=hh[:, :, 1:NT + 1],
                                                                 op0=ALU.mult, op1=ALU.subtract), reads=["dh", "hh"], writes=["dh"])
                    for q in range(2):
                        pl, plk = ps()
                        for dc in range(8):
                            xw, xwk = smr.get()
                            S.op("dve", lambda e, xw=xw, dc=dc, q=q: e.scalar_tensor_tensor(
                                out=RR(xw[:]), in0=dh[:, dc, :], scalar=muwa[:, dc, q:q + 1], in1=hh[:, dc, 1:NT + 1],
                                op0=ALU.mult, op1=ALU.add), reads=["dh", "hh", "muwa"], writes=[xwk])
                            S.op("pe", lambda e, xw=xw, dc=dc, q=q, pl=pl: e.matmul(pl, lhsT=RR(w1s[:, q, dc, :]), rhs=RR(xw[:]),
                                                                                   start=(dc == 0), stop=(dc == 7)),
                                 reads=[xwk, "w1s"], writes=[plk])
                        S.op("act", lambda e, q=q, pl=pl: e.activation(out=RR(lo[:, q, :]), in_=pl, func=(AF.Tanh if q == 0 else AF.Copy)),
                             reads=[plk], writes=[("lo", q)])
                    def cc_body(cc, go=go, seg=seg, nseg=nseg):
                        wi, wk = wrot.get()
                        for hf in range(2):
                            S.dma(lambda e, wi=wi, cc=cc, hf=hf: e.dma_start(out=RR(wi[:, hf * 4:(hf + 1) * 4]), in_=ewin_d[j, cc, :, hf * 4:(hf + 1) * 4]),
                                  writes=[(wk, hf)], q=WQ)
                        wks = [(wk, 0), (wk, 1)]
                        V = lambda name, k=0, cc=cc: ev[:, cc, VEC[name] + k:VEC[name] + k + 1]

                        def proj(blk, rhs_is_dh=False, wi=wi):
                            pt, pk = ps()

                            def mm(e, pt=pt):
                                for dc in range(8):
                                    ins = e.matmul(pt, lhsT=RR(wi[:, dc, blk, :]), rhs=RR(dh[:, dc, :] if rhs_is_dh else hh[:, dc, 1:NT + 1]),
                                                   start=(dc == 0), stop=(dc == 7))
                                return ins
                            S.op("pe", mm, reads=wks + ["hh", "dh"], writes=[pk])
                            return pt, pk

                        def tt(eng, out, outk, in0, in1, op, rd):
                            S.op(eng, lambda e: e.tensor_tensor(out=out, in0=in0, in1=in1, op=op), reads=rd, writes=[outk])

                        rkv = []
                        for bi in range(3):
                            p1, p1k = proj(bi)
                            p2, p2k = proj(bi, True)
                            t2, t2k = sm.get()
                            S.op("act", lambda e, t2=t2, p2=p2: e.activation(out=t2[:], in_=p2, func=AF.Copy), reads=[p2k], writes=[t2k])
                            o, ok = ll.get()
                            S.op("dve", lambda e, o=o, t2=t2, p1=p1, bi=bi, cc=cc: e.scalar_tensor_tensor(
                                out=o[:], in0=t2[:], scalar=ev[:, cc, bi:bi + 1], in1=p1, op0=ALU.mult, op1=ALU.add),
                                reads=[t2k, p1k, "ev"], writes=[ok])
                            rkv.append((o, ok))
                            yield "A"
                        (rp, rpk), (kp, kpk), (vp, vpk) = rkv
                        pza, pzak = proj(3)
                        sza, szak = sm.get()
                        S.op("act", lambda e, sza=sza, pza=pza: e.activation(out=sza[:], in_=pza, func=AF.Silu), reads=[pzak], writes=[szak])
                        S.dma(lambda e, sza=sza, cc=cc, go=go: e.dma_start(out=misc_d[1, cc, :, go:go + NT], in_=sza[:]), reads=[szak],
                              writes=[("misc", 1, cc, go)])
                        yield "A"
                        pb, pbk = proj(4)
                        pcg, pcgk = proj(5)
                        pxb, pxbk = proj(6)
                        pzb, pzbk = proj(7)
                        cgs, cgsk = sm.get()
                        S.op("act", lambda e, cgs=cgs, pcg=pcg: e.activation(out=cgs[:], in_=pcg, func=AF.Copy), reads=[pcgk], writes=[cgsk])
                        cx, cxk = sm.get()
                        tt("dve", cx[:], cxk, cgs[:], pxb, ALU.mult, [cgsk, pxbk])
                        acc, acck = sm.get()
                        S.op("act", lambda e, acc=acc, cx=cx, cc=cc: e.activation(out=acc[:], in_=cx[:], func=AF.Copy, scale=ev[:, cc, 14:15]),
                             reads=[cxk, "ev"], writes=[acck])

                        def sconv0(e, acc=acc, cx=cx, cc=cc, seg=seg, nseg=nseg):
                            av = cx[:].rearrange("p (s t) -> p s t", s=nseg)
                            ov_ = acc[:].rearrange("p (s t) -> p s t", s=nseg)
                            return e.scalar_tensor_tensor(out=ov_[:, :, 1:seg], in0=av[:, :, 0:seg - 1], scalar=ev[:, cc, 13:14],
                                                          in1=ov_[:, :, 1:seg], op0=ALU.mult, op1=ALU.add)

                        def sconv2(e, acc=acc, cx=cx, cc=cc, seg=seg, nseg=nseg):
                            av = cx[:].rearrange("p (s t) -> p s t", s=nseg)
                            ov_ = acc[:].rearrange("p (s t) -> p s t", s=nseg)
                            return e.scalar_tensor_tensor(out=ov_[:, :, 0:seg - 1], in0=av[:, :, 1:seg], scalar=ev[:, cc, 15:16],
                                                          in1=ov_[:, :, 0:seg - 1], op0=ALU.mult, op1=ALU.add)
                        S.op("dve", sconv0, reads=[cxk, acck, "ev"], writes=[acck])
                        S.op("dve", sconv2, reads=[cxk, acck, "ev"], writes=[acck])
                        tt("dve", acc[:], acck, acc[:], pb, ALU.mult, [acck, pbk])
                        szb, szbk = sm.get()
                        S.op("act", lambda e, szb=szb, pzb=pzb: e.activation(out=szb[:], in_=pzb, func=AF.Silu), reads=[pzbk], writes=[szbk])
                        tt("dve", acc[:], acck, acc[:], szb[:], ALU.mult, [acck, szbk])
                        S.dma(lambda e, acc=acc, cc=cc, go=go: e.dma_start(out=misc_d[2, cc, :, go:go + NT], in_=acc[:]), reads=[acck],
                              writes=[("misc", 2, cc, go)])
                        yield "A"
                        kx, kxk = sm.get()
                        S.op("act", lambda e, kx=kx, kp=kp, cc=cc: e.activation(out=kx[:], in_=kp[:], func=AF.Copy, scale=ev[:, cc, 7:8]),
                             reads=[kpk, "ev"], writes=[kxk])
                        ksq, ksqk = smr.get()
                        kn, knk = sm.get()
                        S.op("act", lambda e, ksq=ksq, kx=kx: e.activation(out=RR(ksq[:]), in_=kx[:], func=AF.Square), reads=[kxk], writes=[ksqk])
                        pss, pssk = ps()
                        S.op("pe", lambda e, pss=pss, ksq=ksq: e.matmul(pss, lhsT=RR(bonesr[:]), rhs=RR(ksq[:]), start=True, stop=True),
                             reads=[ksqk, "bonesr"], writes=[pssk])
                        S.op("dve", lambda e, kn=kn, pss=pss: e.tensor_scalar(out=kn[:], in0=pss, scalar1=1e-24, scalar2=None, op0=ALU.max),
                             reads=[pssk], writes=[knk])
                        S.op("act", lambda e, kn=kn: e.activation(out=kn[:], in_=kn[:], func=AF.Sqrt), reads=[knk], writes=[knk])
                        S.op("dve", lambda e, kn=kn: e.reciprocal(out=kn[:], in_=kn[:]), reads=[knk], writes=[knk])
                        kk, kkk = ll.get()
                        tt("dve", kk[:], kkk, kx[:], kn[:], ALU.mult, [kxk, knk])
                        yield "B"
                        pbn, pbnk = accbank(cc % 2)
                        for z in range(2):
                            Q, Qk = q6.get()
                            plw, plwk = ps()
                            S.op("pe", lambda e, plw=plw, z=z, cc=cc: e.matmul(plw, lhsT=RR(w2s[z * 64:(z + 1) * 64, 0, cc * 128:(cc + 1) * 128]),
                                                                               rhs=RR(lo[z * 64:(z + 1) * 64, 0, :]), start=True, stop=True),
                                 reads=[("lo", 0), "w2s"], writes=[plwk])
                            pla, plak = ps()
                            S.op("pe", lambda e, pla=pla, z=z, cc=cc: e.matmul(pla, lhsT=RR(w2s[z * 64:(z + 1) * 64, 1, cc * 128:(cc + 1) * 128]),
                                                                               rhs=RR(lo[z * 64:(z + 1) * 64, 1, :]), start=True, stop=True),
                                 reads=[("lo", 1), "w2s"], writes=[plak])
                            ld, ldk = sm.get()
                            S.op("act", lambda e, ld=ld, plw=plw, z=z, cc=cc: e.activation(out=ld[:], in_=plw, func=AF.Sigmoid,
                                                                                           bias=ev[:, cc, 3 + z:4 + z], scale=1.0),
                                 reads=[plwk, "ev"], writes=[ldk])
                            asg, asgk = sm.get()
                            S.op("act", lambda e, asg=asg, pla=pla, z=z, cc=cc: e.activation(out=asg[:], in_=pla, func=AF.Sigmoid,
                                                                                             bias=ev[:, cc, 5 + z:6 + z], scale=1.0),
                                 reads=[plak, "ev"], writes=[asgk])
                            yield "B"
                            kd, kdk = sm.get()
                            S.op("act", lambda e, kd=kd, asg=asg, cc=cc: e.activation(out=kd[:], in_=asg[:], func=AF.Identity, scale=ev[:, cc, 8:9],
                                                                                       bias=omka[:, cc:cc + 1]),
                                 reads=[asgk, "ev", "omka"], writes=[kdk])
                            tt("dve", kd[:], kdk, kd[:], kp[:], ALU.mult, [kdk, kpk])
                            b, bk = sm.get()
                            tt("dve", b[:], bk, kk[:], asg[:], ALU.mult, [kkk, asgk])
                            yield "B"
                            ci, cik = sm.get()
                            S.op("dve", lambda e, ci=ci, ld=ld: e.tensor_tensor_scan(out=ci[:], data0=rmask[:], data1=ld[:], initial=0.0,
                                                                                     op0=ALU.mult, op1=ALU.add), reads=[ldk, "rmask"], writes=[cik])
                            nchk = NT // CH
                            v3 = lambda t: t[:].rearrange("p (n c) -> p n c", c=CH)
                            tot, totk = sm.get()
                            S.op("dve", lambda e, tot=tot, ci=ci: e.tensor_copy(out=tot[:, 0:nchk], in_=v3(ci)[:, :, CH - 1]), reads=[cik], writes=[totk])
                            totb = lambda tot=tot: tot[:, 0:nchk].unsqueeze(2).broadcast_to([128, nchk, CH])
                            if z == 1:
                                S.op("dve", lambda e, ci=ci, totb=totb: e.tensor_tensor(out=v3(ci), in0=totb(), in1=v3(ci), op=ALU.subtract),
                                     reads=[cik, totk], writes=[cik])
                                tt("dve", ci[:], cik, ci[:], ld[:], ALU.add, [cik, ldk])
                            ce, cek = sm.get()
                            tt("dve", ce[:], cek, ci[:], ld[:], ALU.subtract, [cik, ldk])
                            chh, chk = sm.get()
                            S.op("dve", lambda e, chh=chh, ci=ci, totb=totb: e.tensor_tensor(out=v3(chh), in0=totb(), in1=v3(ci), op=ALU.subtract),
                                 reads=[cik, totk], writes=[chk])
                            yield "B"
                            epos, eposk = sm.get()
                            eneg, enegk = sm.get()
                            S.op("act", lambda e, epos=epos, ci=ci: e.activation(out=epos[:], in_=ci[:], func=AF.Exp, scale=DECAY_SCALE), reads=[cik], writes=[eposk])
                            S.op("act", lambda e, eneg=eneg, ci=ci: e.activation(out=eneg[:], in_=ci[:], func=AF.Exp, scale=-DECAY_SCALE), reads=[cik], writes=[enegk])
                            S.op("act", lambda e, ce=ce: e.activation(out=ce[:], in_=ce[:], func=AF.Exp, scale=DECAY_SCALE), reads=[cek], writes=[cek])
                            S.op("act", lambda e, chh=chh: e.activation(out=chh[:], in_=chh[:], func=AF.Exp, scale=DECAY_SCALE), reads=[chk], writes=[chk])
                            c0 = go // CH
                            S.op("act", lambda e, tot=tot, z=z, cc=cc, c0=c0: e.activation(out=PCt[0][:, z, cc, c0:c0 + nchk], in_=tot[:, 0:nchk], func=AF.Exp, scale=DECAY_SCALE),
                                 reads=[totk], writes=[("PC", z, cc, c0)])
                            yield "B"
                            S.op("dve", lambda e, Q=Q, kk=kk, ce=ce: e.scalar_tensor_tensor(out=Q[:, 0, :], in0=kk[:], scalar=-1.0, in1=ce[:],
                                                                                            op0=ALU.mult, op1=ALU.mult), reads=[kkk, cek], writes=[(Qk, 0)])
                            tt("dve", Q[:, 1, :], (Qk, 1), rp[:], epos[:], ALU.mult, [rpk, eposk])
                            tt("dve", Q[:, 2, :], (Qk, 2), b[:], eneg[:], ALU.mult, [bk, enegk])
                            tt("dve", Q[:, 3, :], (Qk, 3), kd[:], eneg[:], ALU.mult, [kdk, enegk])
                            tt("dve", Q[:, 4, :], (Qk, 4), b[:], chh[:], ALU.mult, [bk, chk])
                            tt("dve", Q[:, 5, :], (Qk, 5), kd[:], chh[:], ALU.mult, [kdk, chk])
                            S.dma(lambda e, Q=Q, z=z, cc=cc, go=go: e.dma_start(out=scn_d[z, cc, :, :, go:go + NT], in_=Q[:]),
                                  reads=[(Qk, q) for q in range(6)], writes=[("scn", z, cc, go)])
                            yield "B"
                            bz, bzk = smr.get()
                            S.op("dve", lambda e, bz=bz, kd=kd, rp=rp, z=z, cc=cc: e.scalar_tensor_tensor(
                                out=RR(bz[:]), in0=kd[:], scalar=ev[:, cc, 9 + z:10 + z], in1=rp[:], op0=ALU.mult, op1=ALU.mult),
                                reads=[kdk, rpk, "ev"], writes=[bzk])
                            S.op("pe", lambda e, bz=bz, z=z, pbn=pbn: e.matmul(pbn, lhsT=RR(bonesr[:]), rhs=RR(bz[:]), start=(z == 0), stop=(z == 1)),
                                 reads=[bzk, "bonesr"], writes=[pbnk])
                            yield "B"
                        bon, bonk = sm.get()
                        tt("dve", bon[:], bonk, vp[:], pbn, ALU.mult, [vpk, pbnk])
                        S.dma(lambda e, bon=bon, cc=cc, go=go: e.dma_start(out=misc_d[0, cc, :, go:go + NT], in_=bon[:]), reads=[bonk],
                              writes=[("misc", 0, cc, go)])
                        yield "B"
                        ptv, ptvk = ps()

                        def trv(e, ptv=ptv, vp=vp):
                            for n2 in range(NT // 128):
                                ins = e.transpose(ptv[:, n2 * 128:(n2 + 1) * 128], vp[:, n2 * 128:(n2 + 1) * 128], ident)
                            return ins
                        S.op("pe", trv, reads=[vpk, "masks"], writes=[ptvk])
                        vts, vtsk = vt_rot.get()
                        S.op("act", lambda e, vts=vts, ptv=ptv: e.activation(out=vts[:].rearrange("p n c -> p (n c)"), in_=ptv, func=AF.Copy),
                             reads=[ptvk], writes=[vtsk])
                        c0 = go // CH
                        for n in range(NT // CH):
                            S.dma(lambda e, vts=vts, cc=cc, n=n, c0=c0: e.dma_start(
                                out=vtok_d[cc, c0 + n].rearrange("(h s) v -> s h v", h=2),
                                in_=vts[(n % 2) * 64:(n % 2) * 64 + 64, n // 2, :].rearrange("s (h v) -> s h v", h=2)),
                                reads=[vtsk], writes=[("vtok", cc, c0 + n)])

                    pending = [cc_body(cc) for cc in range(8)]
                    live = []
                    while pending or live:
                        if pending and len(live) < 2 and not any(t == "A" for _, t in live):
                            live.append([pending.pop(0), "A"])
                        for ent in list(live):
                            try:
                                ent[1] = next(ent[0])
                            except StopIteration:
                                live.remove(ent)
                S.end_phase()
            for z in range(2):
                for half in range(2):
                    scan_pass(z, half)
            with contextlib.ExitStack() as ph:
                alloc = mk_alloc(ph)
                wo = alloc("wo", [128, 16, D])
                ev = alloc("ev", [128, 8, 17])
                for q in range(4):
                    S.dma(lambda e, q=q: e.dma_start(out=RR(wo[:, q * 4:(q + 1) * 4, :]), in_=ewout_d[j, :, q * 4:(q + 1) * 4, :]),
                          writes=[("wo", q)], q=WQ)
                wokeys = [("wo", q) for q in range(4)]
                S.dma(lambda e: e.dma_start(out=ev[:], in_=evec_d[j]), writes=["ev"])
                G = alloc("G", [128, 16, NT])
                xrot = Rot(alloc, "xt", [128, 2, D], 2)
                tmp_rot = Rot(alloc, "tmpw", [128, D], 2)
                st_rot = Rot(alloc, "st", [128, 8], 4)
                sm = Rot(alloc, "sm", [128, NT], 12)
                for (seq, t0) in tiles:
                    src = csrc if seq == "c" else xsrc
                    dst = cdst if seq == "c" else xdst
                    go = goff(seq, t0)
                    xt3, xk3 = xrot.get()
                    xt_tiles = [(xt3[:, sub, :], (xk3, sub)) for sub in range(2)]
                    for sub in range(2):
                        for (pa, pb), ap in x_rows(src, seq, t0, sub, col):
                            S.dma(lambda e, pa=pa, pb=pb, ap=ap, sub=sub, xt3=xt3: e.dma_start(out=xt3[pa:pb, sub, :], in_=ap),
                                  writes=[(xk3, sub)])
                    def cc_body(cc, go=go):
                        of, ofk = sm.get()
                        ob, obk = sm.get()
                        bn, bnk = sm.get()
                        sz, szk = sm.get()
                        S.dma(lambda e, of=of, cc=cc, go=go: e.dma_start(out=of[:], in_=o_d[0, cc, :, go:go + NT]), writes=[ofk])
                        S.dma(lambda e, ob=ob, cc=cc, go=go: e.dma_start(out=ob[:], in_=o_d[1, cc, :, go:go + NT]), writes=[obk])
                        S.dma(lambda e, bn=bn, cc=cc, go=go: e.dma_start(out=bn[:], in_=misc_d[0, cc, :, go:go + NT]), writes=[bnk])
                        S.dma(lambda e, sz=sz, cc=cc, go=go: e.dma_start(out=sz[:], in_=misc_d[1, cc, :, go:go + NT]), writes=[szk])
                        S.dma(lambda e, cc=cc, go=go: e.dma_start(out=RR(G[:, 8 + cc, :]), in_=misc_d[2, cc, :, go:go + NT]), writes=[("G", 8 + cc)], q=WQ)
                        yield
                        S.op("pool", lambda e, of=of, ob=ob: e.tensor_tensor(out=of[:], in0=of[:], in1=ob[:], op=ALU.add), reads=[ofk, obk], writes=[ofk])
                        pm, pmk = ps()
                        S.op("pe", lambda e, pm=pm, of=of: e.matmul(pm, lhsT=bones[:, 0, :], rhs=of[:], start=True, stop=True),
                             reads=[ofk, "bones"], writes=[pmk])
                        S.op("dve", lambda e, of=of, pm=pm: e.scalar_tensor_tensor(out=of[:], in0=pm, scalar=-1.0 / 64, in1=of[:], op0=ALU.mult,
                                                                                   op1=ALU.add), reads=[ofk, pmk], writes=[ofk])
                        yield
                        S.op("act", lambda e, ob=ob, of=of: e.activation(out=ob[:], in_=of[:], func=AF.Square), reads=[ofk], writes=[obk])
                        pv, pvk = ps()
                        S.op("pe", lambda e, pv=pv, ob=ob: e.matmul(pv, lhsT=bones[:, 0, :], rhs=ob[:], start=True, stop=True),
                             reads=[obk, "bones"], writes=[pvk])
                        S.op("act", lambda e, ob=ob, pv=pv: e.activation(out=ob[:], in_=pv, func=AF.Sqrt, scale=1.0 / 64, bias=64e-5),
                             reads=[pvk], writes=[obk])
                        yield
                        S.op("dve", lambda e, ob=ob: e.reciprocal(out=ob[:], in_=ob[:]), reads=[obk], writes=[obk])
                        S.op("dve", lambda e, of=of, ob=ob: e.tensor_tensor(out=of[:], in0=of[:], in1=ob[:], op=ALU.mult), reads=[ofk, obk], writes=[ofk])
                        S.op("act", lambda e, of=of, cc=cc: e.activation(out=of[:], in_=of[:], func=AF.Identity, scale=ev[:, cc, 11:12],
                                                                         bias=ev[:, cc, 12:13]), reads=[ofk, "ev"], writes=[ofk])
                        yield
                        S.op("pool", lambda e, of=of, bn=bn: e.tensor_tensor(out=of[:], in0=of[:], in1=bn[:], op=ALU.add), reads=[ofk, bnk], writes=[ofk])
                        S.op("dve", lambda e, of=of, sz=sz, cc=cc: e.tensor_tensor(out=RR(G[:, cc, :]), in0=of[:], in1=sz[:], op=ALU.mult),
                             reads=[ofk, szk], writes=[("G", cc)])

                    pending = [cc_body(cc) for cc in range(8)]
                    live = []
                    while pending or live:
                        while pending and len(live) < 2:
                            live.append(pending.pop(0))
                        for g_ in list(live):
                            try:
                                next(g_)
                            except StopIteration:
                                live.remove(g_)
                    outproj_post(G, [("G", k) for k in range(16)] + wokeys[1:], wo, wokeys[0], 16, seq, t0, col, xt_tiles, dst, tmp_rot, st_rot)
                S.end_phase()

        def scan_pass(z, half):
            ctx_t = [("c", t0) for t0 in range(0, NCTX, NT)]
            lat_t = [("l", t0) for t0 in range(0, T, NT)]
            order = ctx_t + lat_t if z == 0 else ctx_t[::-1] + lat_t[::-1]
            MS, MI, ML = (0, 1, 2) if z == 0 else (2, 3, 0)
            with contextlib.ExitStack() as ph:
                alloc = mk_alloc(ph)
                mk2 = alloc("mk2", [128, 2, 128])
                bd6 = alloc("bd6", [128, 6, 128])
                S.op("dve", lambda e: e.tensor_copy(out=mk2[:, 0, :], in_=masks[:, MS, :]), reads=["masks"], writes=["mk2"])
                S.op("dve", lambda e: e.tensor_copy(out=mk2[:, 1, :], in_=masks[:, MI, :]), reads=["masks"], writes=["mk2"])
                for q in range(6):
                    S.op("dve", lambda e, q=q: e.tensor_copy(out=bd6[:, q, :], in_=BD), reads=["masks"], writes=["bd6"])
                opr = Rot(alloc, "opnd", [128, 6, NT], 8)
                vtr = Rot(alloc, "vtk", [128, NT // CH, CH], 8)
                oacc = Rot(alloc, "oacc", [64, 2, NT], 8)
                Hs = [[alloc("H%d_%d" % (p, k), [128, CH]) for k in range(2)] for p in range(4)]
                hcur = [0] * 4
                for p in range(4):
                    S.op("dve", lambda e, p=p: e.tensor_scalar(out=RR(Hs[p][0][:]), in0=masks[:, 0, 0:CH], scalar1=0.0, scalar2=None, op0=ALU.mult),
                         reads=["masks"], writes=[("H", p, 0)])
                blk = Rot(alloc, "blk", [128, 6, 128], 8)
                sc = Rot(alloc, "sc", [128, 2, 128], 16)
                w128 = Rot(alloc, "w128", [128, 128], 44)
                ttf = Rot(alloc, "ttf", [128, 128], 8)
                w64 = Rot(alloc, "w64", [128, CH], 16)
                nchk = NT // CH
                for (seq, t0) in order:
                    go = goff(seq, t0)
                    loaded = []
                    for p in range(4):
                        cc = half * 4 + p
                        op_, opk = opr.get()
                        vt_, vtk_ = vtr.get()
                        oa_, oak = oacc.get()
                        S.dma(lambda e, op_=op_, cc=cc, go=go: e.dma_start(out=op_[:], in_=scn_d[z, cc, :, :, go:go + NT]),
                              reads=[("scn", z, cc, go)], writes=[opk])
                        c0 = go // CH
                        S.dma(lambda e, vt_=vt_, cc=cc, c0=c0: e.dma_start(out=RR(vt_[:]), in_=vtok_d[cc, c0:c0 + nchk].rearrange("n p v -> p n v")),
                              reads=[("vtok", cc, c0 + n) for n in range(nchk)], writes=[vtk_], q=WQ)
                        loaded.append((op_, opk, vt_, vtk_, oa_, oak))
                    chunks = list(range(nchk)) if z == 0 else list(range(nchk))[::-1]
                    for n in chunks:
                        cn = go // CH + n
                        def precompute(p, n=n):
                            cc = half * 4 + p
                            op_, opk, vt_, vtk_, oa_, oak = loaded[p]
                            B6, B6k = blk.get()
                            S.op("dve", lambda e: e.tensor_tensor(
                                out=RR(B6[:].rearrange("p q (h t) -> p q h t", h=2)),
                                in0=op_[:, :, n * CH:(n + 1) * CH].unsqueeze(2).broadcast_to([128, 6, 2, CH]),
                                in1=bd6[:].rearrange("p q (h t) -> p q h t", h=2), op=ALU.mult), reads=[opk, "bd6"], writes=[B6k])
                            yield
                            AR = RR(B6[:, 0:2, :].rearrange("p q t -> p (q t)"))
                            p1, p1k = ps()
                            S.op("pe", lambda e: e.matmul(p1, lhsT=RR(B6[:, 2, :]), rhs=AR, start=True, stop=True), reads=[B6k], writes=[p1k])
                            s1, s1k = sc.get()
                            S.op("dve", lambda e: e.tensor_tensor(out=RR(s1[:].rearrange("p q t -> p (q t)")), in0=p1,
                                                                  in1=mk2[:].rearrange("p q t -> p (q t)"), op=ALU.mult),
                                 reads=[p1k, "mk2"], writes=[s1k])
                            yield
                            p2, p2k = ps()
                            S.op("pe", lambda e: e.matmul(p2, lhsT=RR(B6[:, 3, :]), rhs=AR, start=True, stop=True), reads=[B6k], writes=[p2k])
                            s2, s2k = sc.get()
                            S.op("dve", lambda e: e.tensor_tensor(out=RR(s2[:].rearrange("p q t -> p (q t)")), in0=p2,
                                                                  in1=mk2[:].rearrange("p q t -> p (q t)"), op=ALU.mult),
                                 reads=[p2k, "mk2"], writes=[s2k])
                            yield
                            p3, p3k = ps()
                            S.op("pe", lambda e: e.matmul(p3[:, 0:128], lhsT=RR(B6[:, 0, :]), rhs=RR(B6[:, 2, :]), start=True, stop=True),
                                 reads=[B6k], writes=[p3k])
                            Lc, Lck = w128.get()
                            S.op("dve", lambda e: e.tensor_tensor(out=RR(Lc[:]), in0=p3[:, 0:128], in1=masks[:, ML, :], op=ALU.mult),
                                 reads=[p3k, "masks"], writes=[Lck])
                            Mc, Mck = s1[:, 0, :], s1k
                            acc, acck = w128.get()
                            S.op("pool", lambda e: e.tensor_tensor(out=RR(acc[:]), in0=Mc, in1=ident, op=ALU.add),
                                 reads=[Mck, "masks"], writes=[acck])
                            yield
                            for it in range(5):
                                pL, pLk = ps()
                                S.op("pe", lambda e: e.matmul(pL[:, 0:128], lhsT=RR(Mc), rhs=RR(Lc[:]), start=True, stop=True),
                                     reads=[Mck, Lck], writes=[pLk])
                                Ln, Lnk = w128.get()
                                S.op("act", lambda e: e.activation(out=RR(Ln[:]), in_=pL[:, 0:128], func=AF.Copy), reads=[pLk], writes=[Lnk])
                                yield
                                if it < 4:
                                    pM, pMk = ps()
                                    S.op("pe", lambda e: e.matmul(pM[:, 0:128], lhsT=RR(Lc[:]), rhs=RR(Mc), start=True, stop=True),
                                         reads=[Mck, Lck], writes=[pMk])
                                    Mn, Mnk = w128.get()
                                    S.op("act", lambda e: e.activation(out=RR(Mn[:]), in_=pM[:, 0:128], func=AF.Copy), reads=[pMk], writes=[Mnk])
                                    yield
                                pA, pAk = ps()
                                S.op("pe", lambda e: e.matmul(pA[:, 0:128], lhsT=RR(Ln[:]), rhs=RR(acc[:]), start=True, stop=True),
                                     reads=[Lnk, acck], writes=[pAk])
                                acc2, acc2k = (w128.get() if it < 4 else ttf.get())
                                S.op("dve", lambda e: e.tensor_tensor(out=RR(acc2[:]), in0=pA[:, 0:128], in1=acc[:], op=ALU.add),
                                     reads=[pAk, acck], writes=[acc2k])
                                yield
                                acc, acck = acc2, acc2k
                                Lc, Lck = Ln, Lnk
                                if it < 4:
                                    Mc, Mck = Mn[:], Mnk
                            ptb, ptbk = ps()

                            def trb(e):
                                e.transpose(ptb[:, 0:128], B6[:, 4, :], ident)
                                return e.transpose(ptb[:, 128:256], B6[:, 5, :], ident)
                            S.op("pe", trb, reads=[B6k, "masks"], writes=[ptbk])
                            bkt, bktk = sc.get()
                            S.op("dve", lambda e: e.tensor_tensor(out=RR(bkt[:].rearrange("p q t -> p (q t)")), in0=ptb,
                                                                  in1=bd6[:, 0:2, :].rearrange("p q t -> p (q t)"), op=ALU.mult),
                                 reads=[ptbk, "bd6"], writes=[bktk])
                            stg[p] = dict(B6=B6, B6k=B6k, s1=s1, s1k=s1k, s2=s2, s2k=s2k, TT_=acc, TTk=acck, bkt=bkt, bktk=bktk)

                        stg = [None] * 4
                        gens = [precompute(p) for p in range(4)]
                        live = list(gens)
                        while live:
                            for g_ in list(live):
                                try:
                                    next(g_)
                                except StopIteration:
                                    live.remove(g_)
                        for p in range(4):
                            g = stg[p]
                            op_, opk, vt_, vtk_, oa_, oak = loaded[p]
                            H, Hk = Hs[p][hcur[p]], ("H", p, hcur[p])
                            pX, pXk = ps()

                            def mm1(e, pX=pX, g=g, H=H, vt_=vt_, n=n):
                                e.matmul(pX[:, 0:CH], lhsT=RR(g["B6"][:, 0, :]), rhs=RR(H[:]), start=True, stop=False)
                                return e.matmul(pX[:, 0:CH], lhsT=RR(g["s2"][:, 0, :]), rhs=RR(vt_[:, n, :]), start=False, stop=True)
                            S.op("pe", mm1, reads=[g["B6k"], Hk, g["s2k"], vtk_], writes=[pXk])
                            X, Xk = w64.get()
                            S.op("act", lambda e, X=X, pX=pX: e.activation(out=RR(X[:]), in_=pX[:, 0:CH], func=AF.Copy), reads=[pXk], writes=[Xk])
                            g.update(X=X, Xk=Xk, H=H, Hk=Hk)
                        for p in range(4):
                            g = stg[p]
                            pU, pUk = ps()
                            S.op("pe", lambda e, pU=pU, g=g: e.matmul(pU[:, 0:CH], lhsT=RR(g["TT_"][:]), rhs=RR(g["X"][:]), start=True, stop=True),
                                 reads=[g["TTk"], g["Xk"]], writes=[pUk])
                            U, Uk = w64.get()
                            S.op("act", lambda e, U=U, pU=pU: e.activation(out=RR(U[:]), in_=pU[:, 0:CH], func=AF.Copy), reads=[pUk], writes=[Uk])
                            g.update(U=U, Uk=Uk)
                        for p in range(4):
                            g = stg[p]
                            cc = half * 4 + p
                            op_, opk, vt_, vtk_, oa_, oak = loaded[p]
                            pO, pOk = ps()

                            def mm3(e, pO=pO, g=g, vt_=vt_, n=n):
                                e.matmul(pO[0:64, 0:128], lhsT=RR(g["H"][:]), rhs=RR(g["B6"][:, 1, :]), start=True, stop=False)
                                e.matmul(pO[0:64, 0:128], lhsT=RR(g["U"][:]), rhs=RR(g["s1"][:, 1, :]), start=False, stop=False)
                                return e.matmul(pO[0:64, 0:128], lhsT=RR(vt_[:, n, :]), rhs=RR(g["s2"][:, 1, :]), start=False, stop=True)
                            S.op("pe", mm3, reads=[g["Hk"], g["B6k"], g["Uk"], g["s1k"], g["s2k"], vtk_], writes=[pOk])
                            S.op("act", lambda e, pO=pO, oa_=oa_, n=n: e.activation(
                                out=oa_[:, :, n * CH:(n + 1) * CH], in_=pO[0:64, 0:128].rearrange("p (h t) -> p h t", h=2), func=AF.Copy),
                                reads=[pOk], writes=[oak])
                            pH, pHk = ps()

                            def mm4(e, pH=pH, g=g, vt_=vt_, n=n):
                                e.matmul(pH[:, 0:CH], lhsT=RR(g["bkt"][:, 0, :]), rhs=RR(g["U"][:]), start=True, stop=False)
                                return e.matmul(pH[:, 0:CH], lhsT=RR(g["bkt"][:, 1, :]), rhs=RR(vt_[:, n, :]), start=False, stop=True)
                            S.op("pe", mm4, reads=[g["bktk"], g["Uk"], vtk_], writes=[pHk])
                            nk = 1 - hcur[p]
                            Hn, Hnk = Hs[p][nk], ("H", p, nk)
                            S.op("dve", lambda e, Hn=Hn, g=g, pH=pH, cc=cc, cn=cn: e.scalar_tensor_tensor(
                                out=RR(Hn[:]), in0=g["H"][:], scalar=PCt[0][:, z, cc, cn:cn + 1], in1=pH[:, 0:CH], op0=ALU.mult, op1=ALU.add),
                                reads=[g["Hk"], pHk], writes=[Hnk])
                            hcur[p] = nk
                    for p in range(4):
                        cc = half * 4 + p
                        op_, opk, vt_, vtk_, oa_, oak = loaded[p]
                        S.dma(lambda e, oa_=oa_, cc=cc, go=go: e.dma_start(out=o_d[z, cc, :, go:go + NT].rearrange("(h v) t -> v h t", h=2), in_=oa_[:]),
                              reads=[oak], writes=[("od", z, cc, go)])
                S.end_phase()

        xcur, ccur = x_in, c_in
        for i in range(depth):
            last = (i == depth - 1)
            xdst = y_out if last else xs[i % 2]
            cdst = cs[i % 2]
            if i % 2 == 0:
                with contextlib.ExitStack() as evs:
                    PCt[0] = mk_alloc(evs)("PC", [128, 2, 8, NCH])
                    even_layer(i, xcur, ccur, xdst, cdst)
            else:
                odd_layer(i, xcur, ccur, xdst, cdst, last)
            xcur, ccur = xdst, cdst
        print("sched ops:", S.nops, {e: S.cnt[e] for e in S.ENG})
    return nc


def host_consts():
    idx = np.arange(128)
    r, c = idx[:, None], idx[None, :]
    same = (r // 64) == (c // 64)
    m = np.zeros((128, 6, 128), np.float32)
    m[:, 0] = same & (r < c)
    m[:, 1] = same & (r <= c)
    m[:, 2] = same & (r > c)
    m[:, 3] = same & (r >= c)
    m[:, 4] = same
    m[:, 5] = (r == c)
    sel = np.zeros((2, 2, 128), np.float32)
    sel[0, 0] = 1.0
    sel[1, 1] = 1.0
    bones = np.zeros((128, 2, 128), np.float32)
    bones[:, 0] = same
    bones[:, 1] = 1.0 / 2048
    rmask = np.ones((128, NT), np.float32)
    rmask[:, ::CH] = 0.0
    return dict(masks=m, sel=sel, bones=bones, rmask=rmask)


def fm(v, nchunk):
    v = np.asarray(v)
    if v.ndim == 1:
        return np.ascontiguousarray(v.reshape(nchunk, 128).T)
    return np.ascontiguousarray(np.moveaxis(v.reshape(v.shape[0], nchunk, 128), 0, -1).transpose(1, 0, 2))


def host_weights(inp, depth):
    n_even = (depth + 1) // 2
    n_odd = depth // 2
    w = {}
    w["modw"] = np.ascontiguousarray(inp["mod_w"][:depth].reshape(depth, 8, 128, 3 * D).transpose(0, 2, 1, 3))
    w["modb"] = np.ascontiguousarray(np.repeat(inp["mod_b"][:depth, None, :], 2, axis=1))
    w["gpre"] = np.ascontiguousarray(np.repeat(inp["g_pre"][:depth, None, :], 2, axis=1))
    w["gpost"] = np.ascontiguousarray(np.repeat(inp["g_post"][:depth, None, :], 2, axis=1))
    if n_even:
        wi = inp["ev_w_in"][:n_even]
        wi = wi.reshape(n_even, 8, 128, 8, 8, 128)
        w["ewin"] = np.ascontiguousarray(wi.transpose(0, 4, 2, 1, 3, 5))
        w["ewout"] = np.ascontiguousarray(inp["ev_w_out"][:n_even].reshape(n_even, 16, 128, D).transpose(0, 2, 1, 3))
        w1 = inp["ev_w1"][:n_even].reshape(n_even, 2, 8, 128, 64)
        a1 = inp["ev_a1"][:n_even].reshape(n_even, 2, 8, 128, 64)
        st = np.stack([w1, a1], axis=1)
        w["ew1"] = np.ascontiguousarray(st.transpose(0, 1, 4, 3, 2, 5).reshape(n_even, 2, 128, 8, 128))
        w2 = inp["ev_w2"][:n_even].reshape(n_even, 128, D)
        a2 = inp["ev_a2"][:n_even].reshape(n_even, 128, D)
        w["ew2"] = np.ascontiguousarray(np.stack([w2, a2], axis=1))
        ev = np.zeros((n_even, 128, 8, 17), np.float32)
        for j in range(n_even):
            vecs = [inp["ev_mu_rkv"][j, 0], inp["ev_mu_rkv"][j, 1], inp["ev_mu_rkv"][j, 2], inp["ev_w0"][j, 0], inp["ev_w0"][j, 1],
                    inp["ev_a0"][j, 0], inp["ev_a0"][j, 1], inp["ev_k_k"][j], inp["ev_k_a"][j], inp["ev_r_k"][j, 0].reshape(-1),
                    inp["ev_r_k"][j, 1].reshape(-1), inp["ev_lnx_w"][j], inp["ev_lnx_b"][j], inp["ev_sc_w"][j, 0], inp["ev_sc_w"][j, 1],
                    inp["ev_sc_w"][j, 2]]
            for k, v in enumerate(vecs):
                ev[j, :, :, k] = v.reshape(8, 128).T
        w["evec"] = ev
        mw = inp["ev_mu_wa"][:n_even]
        w["emuwa"] = np.ascontiguousarray(mw.reshape(n_even, 2, 8, 128).transpose(0, 3, 2, 1))
    if n_odd:
        wi = inp["od_w_in"][:n_odd].reshape(n_odd, 8, 128, 3, 16, 128)
        w["owin"] = np.ascontiguousarray(wi.transpose(0, 4, 2, 1, 3, 5))
        w["owout"] = np.ascontiguousarray(inp["od_w_out"][:n_odd].reshape(n_odd, 16, 128, D).transpose(0, 2, 1, 3))
        dw = inp["od_dw_w"][:n_odd]
        w["odww"] = np.ascontiguousarray(dw.reshape(n_odd, 31, 16, 128).transpose(0, 3, 2, 1))
        ov = np.stack([inp["od_dw_b"][:n_odd], inp["od_ln_w"][:n_odd], inp["od_ln_b"][:n_odd]], axis=-1)
        w["ovec"] = np.ascontiguousarray(ov.reshape(n_odd, 16, 128, 3).transpose(0, 2, 1, 3))
    return w


_CACHE = {}


def run(inputs, depth, ncores, T, NCTX, trace=False):
    inp = {k: np.asarray(v, dtype=np.float32) for k, v in inputs.items()}
    key = (T, NCTX, depth)
    if key not in _CACHE:
        _CACHE[key] = build(T, NCTX, depth)
    nc = _CACHE[key]
    shared = host_consts()
    shared.update(host_weights(inp, depth))
    in_maps = []
    for b in range(ncores):
        m = dict(shared)
        m["x"] = np.ascontiguousarray(inp["x"][b])
        m["ctx"] = np.ascontiguousarray(inp["ctx"][b])
        sv = np.stack([inp["c"][b], inp["c_ctx"]], axis=-1)
        m["svec"] = np.ascontiguousarray(sv.reshape(8, 128, 2).transpose(1, 0, 2))
        in_maps.append(m)
    res = run_bass_kernel_spmd(nc, in_maps, core_ids=list(range(ncores)), trace=trace)
    out = np.stack([np.asarray(r["y"], dtype=np.float32) for r in res.results], axis=0)
    return out, res


def kernel(**inputs):
    out, _ = run(inputs, 4, 8, 4096, 256)
    return out
```

```python
import contextlib
import math
import numpy as np
import concourse.bass as bass
import concourse.mybir as mybir
from concourse.bass_utils import run_bass_kernel_spmd

F32 = mybir.dt.float32
F32R = mybir.dt.float32r
USE_R = True
AF = mybir.ActivationFunctionType
ALU = mybir.AluOpType

D = 1024
NT = 256
CH = 64
NDMASEM = 32
SAME_ENGINE_SYNC = True
DECAY_SCALE = -math.exp(-0.5)


class Rec:
    def __init__(self):
        self.calls = []

    def __getattr__(self, name):
        def f(*a, **k):
            self.calls.append((name, a, k))
            return self
        return f


def _replay(calls, engine):
    ins = None
    for name, a, k in calls:
        ins = getattr(engine, name)(*a, **k)
    return ins


class Sched:
    ENG = ("pe", "dve", "act", "pool", "sp")
    OBJ = {"pe": "tensor", "dve": "vector", "act": "scalar", "pool": "gpsimd", "sp": "sync"}

    def __init__(self, nc, stack):
        self.nc = nc
        self.prog = {e: [] for e in self.ENG}
        self.sem = {e: stack.enter_context(nc.semaphore("s_" + e)) for e in self.ENG if e != "sp"}
        self.cnt = {e: 0 for e in self.ENG}
        self.dsem = [stack.enter_context(nc.semaphore("d%d" % i)) for i in range(NDMASEM)]
        self.dcnt = [0] * NDMASEM
        self.dnext = 0
        self.dnext2 = 0
        self.waited = {e: {} for e in self.ENG}
        self.lastw = {}
        self.readers = {}
        self.nops = 0
        import os
        self.kstop = int(os.environ.get("KSTOP", "1000"))
        self.kmax = int(os.environ.get("KMAXOPS", "100000000"))
        self.phase = 0

    def _semobj(self, key):
        return self.sem[key] if isinstance(key, str) else self.dsem[key]

    def _deps(self, eng, reads, writes):
        deps = {}

        def add(k, v):
            if deps.get(k, 0) < v:
                deps[k] = v

        for b in reads:
            w = self.lastw.get(b)
            if w is not None:
                add(*w)
        for b in writes:
            w = self.lastw.get(b)
            if w is not None:
                add(*w)
            for r in self.readers.get(b, ()):
                add(*r)
        out = []
        for k, v in deps.items():
            if k == eng and (eng == "pe" or not SAME_ENGINE_SYNC):
                continue
            if self.waited[eng].get(k, 0) >= v:
                continue
            self.waited[eng][k] = v
            out.append((k, v))
        return out

    def _commit(self, token, reads, writes):
        for b in writes:
            self.lastw[b] = token
            self.readers[b] = []
        for b in reads:
            self.readers.setdefault(b, []).append(token)

    def op(self, eng, fn, reads=(), writes=()):
        if self.phase >= self.kstop or self.nops >= self.kmax:
            return
        waits = self._deps(eng, reads, writes)
        self.cnt[eng] += 1
        rec = Rec()
        fn(rec)
        calls = rec.calls
        self.prog[eng].append((waits, (lambda engine, calls=calls: _replay(calls, engine)), (eng, 1)))
        self._commit((eng, self.cnt[eng]), reads, writes)
        self.nops += 1

    def dma(self, fn, reads=(), writes=(), q="sp"):
        if self.phase >= self.kstop or self.nops >= self.kmax:
            return
        half = NDMASEM // 2
        if q == "sp":
            i = self.dnext
            self.dnext = (self.dnext + 1) % half
        else:
            i = half + self.dnext2
            self.dnext2 = (self.dnext2 + 1) % (NDMASEM - half)
        waits = self._deps(q, reads, writes)
        if self.dcnt[i] > 0 and self.waited[q].get(i, 0) < self.dcnt[i]:
            self.waited[q][i] = self.dcnt[i]
            waits.append((i, self.dcnt[i]))
        self.dcnt[i] += 16
        rec = Rec()
        fn(rec)
        calls = rec.calls
        self.prog[q].append((waits, (lambda engine, calls=calls: _replay(calls, engine)), (i, 16)))
        self._commit((i, self.dcnt[i]), reads, writes)
        self.nops += 1

    def end_phase(self):
        self.phase += 1
        if self.phase > self.kstop:
            return
        print("phase", self.phase, "nops", self.nops)
        waits = [(e, self.cnt[e]) for e in self.ENG if e != "sp" and self.cnt[e] > 0]
        waits += [(i, self.dcnt[i]) for i in range(NDMASEM) if self.dcnt[i] > 0]
        self.prog["sp"].append((waits, None, None))
        nc = self.nc
        with nc.Block() as block:
            for e in self.ENG:
                prog = self.prog[e]
                if not prog:
                    continue

                def body(engine, prog=prog):
                    for waits, fn, inc in prog:
                        for k, v in waits:
                            engine.wait_ge(self._semobj(k), v)
                        if fn is None:
                            continue
                        ins = fn(engine)
                        ins.then_inc(self._semobj(inc[0]), inc[1])

                getattr(block, self.OBJ[e])(body)
        self.prog = {e: [] for e in self.ENG}
        self.lastw = {}
        self.readers = {}


class Rot:
    def __init__(self, alloc, name, shape, n):
        self.tiles = [alloc("%s%d" % (name, i), shape) for i in range(n)]
        self.keys = ["%s%d" % (name, i) for i in range(n)]
        self.i = 0

    def get(self):
        t, k = self.tiles[self.i], self.keys[self.i]
        self.i = (self.i + 1) % len(self.tiles)
        return t, k


def build(T, NCTX, depth):
    TT = NCTX + T
    TTP = TT + 4
    NCH = TT // CH
    nc = bass.Bass("TRN2", target_bir_lowering=False)
    dt = lambda name, shape, kind="ExternalInput": nc.dram_tensor(name, shape, F32, kind=kind).ap()
    n_even = (depth + 1) // 2
    n_odd = depth // 2
    x_in = dt("x", [T, D])
    c_in = dt("ctx", [NCTX, D])
    svec_d = dt("svec", [128, 8, 2])
    masks_d = dt("masks", [128, 6, 128])
    sel_d = dt("sel", [2, 2, 128])
    bones_d = dt("bones", [128, 2, 128])
    rmask_d = dt("rmask", [128, NT])
    modw_d = dt("modw", [depth, 128, 8, 3 * D])
    modb_d = dt("modb", [depth, 2, 3 * D])
    gpre_d = dt("gpre", [depth, 2, D])
    gpost_d = dt("gpost", [depth, 2, D])
    if n_even:
        ewin_d = dt("ewin", [n_even, 8, 128, 8, 8, 128])
        ewout_d = dt("ewout", [n_even, 128, 16, D])
        ew1_d = dt("ew1", [n_even, 2, 128, 8, 128])
        ew2_d = dt("ew2", [n_even, 2, 128, D])
        evec_d = dt("evec", [n_even, 128, 8, 17])
        emuwa_d = dt("emuwa", [n_even, 128, 8, 2])
    if n_odd:
        owin_d = dt("owin", [n_odd, 16, 128, 8, 3, 128])
        owout_d = dt("owout", [n_odd, 128, 16, D])
        odww_d = dt("odww", [n_odd, 128, 16, 31])
        ovec_d = dt("ovec", [n_odd, 128, 16, 3])
    y_out = dt("y", [T, D], "ExternalOutput")
    xs = [dt("xsa", [T, D], "Internal"), dt("xsb", [T, D], "Internal")]
    cs = [dt("csa", [NCTX, D], "Internal"), dt("csb", [NCTX, D], "Internal")]
    if n_even:
        hT_d = dt("hTd", [128, 8, TTP], "Internal")
        scn_d = dt("scn", [2, 8, 128, 6, TT], "Internal")
        vtok_d = dt("vtok", [8, NCH, 128, CH], "Internal")
        o_d = dt("od", [2, 8, 128, TT], "Internal")
        misc_d = dt("miscd", [3, 8, 128, TT], "Internal")

    with contextlib.ExitStack() as gs:
        S = Sched(nc, gs)

        uid = {"n": 0}

        def mk_alloc(stack):
            def alloc(name, shape):
                uid["n"] += 1
                return stack.enter_context(nc.sbuf_tensor("sb%d_%s" % (uid["n"], name), shape, F32))
            return alloc

        galloc = mk_alloc(gs)
        RR = (lambda ap: ap.bitcast(F32R)) if USE_R else (lambda ap: ap)
        WQ = "pool" if USE_R else "sp"
        masks = galloc("masks", [128, 6, 128])
        sel = galloc("sel", [2, 2, 128])
        bones = galloc("bones", [128, 2, 128])
        rmask = galloc("rmask", [128, NT])
        bonesr = galloc("bonesr", [128, 128])
        ssil = galloc("ssil", [128, 8, 2])
        bc = {w: [galloc("bc%s%d" % (w, k), [128, D]) for k in range(3)] for w in ("l", "c")}
        PCt = [None]
        zero = galloc("zero", [128, 8, 2])
        pst = [gs.enter_context(nc.psum_tensor("ps%d" % i, [128, 512], F32)) for i in range(8)]
        ps_state = {"i": 0}

        ps_state["nrot"] = 6

        def ps():
            i = ps_state["i"] % ps_state["nrot"]
            ps_state["i"] = (i + 1) % ps_state["nrot"]
            return pst[i][:, 0:256], "ps%d" % i

        def psbank():
            i = ps_state["i"] % ps_state["nrot"]
            ps_state["i"] = (i + 1) % ps_state["nrot"]
            return pst[i], ("ps%d" % i,)

        def accbank(k):
            return pst[6 + k][:, 0:256], "accbank%d" % k

        S.dma(lambda e: e.dma_start(out=masks[:], in_=masks_d), writes=["masks"])
        S.dma(lambda e: e.dma_start(out=sel[:], in_=sel_d), writes=["sel"])
        S.dma(lambda e: e.dma_start(out=bones[:], in_=bones_d), writes=["bones"])
        S.dma(lambda e: e.dma_start(out=rmask[:], in_=rmask_d), writes=["rmask"])
        S.dma(lambda e: e.dma_start(out=RR(bonesr[:]), in_=bones_d[:, 0, :]), writes=["bonesr"], q=WQ)
        S.dma(lambda e: e.dma_start(out=ssil[:], in_=svec_d), writes=["ssil"])
        S.op("act", lambda e: e.activation(out=ssil[:], in_=ssil[:], func=AF.Silu), reads=["ssil"], writes=["ssil"])
        S.op("dve", lambda e: e.memset(zero[:], 0.0), writes=["zero"])
        if n_even:
            for col in (0, NCTX + 1, NCTX + 2, NCTX + 3 + T):
                S.dma(lambda e, col=col: e.dma_start(out=hT_d[:, :, col:col + 1], in_=zero[:, :, 0:1], allow_slow_non_contiguous=True),
                      reads=["zero"], writes=[("hTdz", col)])
        S.end_phase()

        ident = masks[:, 5, :]
        BD = masks[:, 4, :]

        def seq_tiles(with_ctx):
            tl = []
            if with_ctx:
                for t0 in range(0, NCTX, NT):
                    tl.append(("c", t0))
            for t0 in range(0, T, NT):
                tl.append(("l", t0))
            return tl

        def goff(seq, t0):
            return t0 if seq == "c" else NCTX + t0

        def poff(seq, t0):
            return 1 + t0 if seq == "c" else NCTX + 3 + t0

        def x_rows(src, seq, t0, sub, col):
            p0 = t0 + sub * 128
            if seq == "c" or not col:
                return [((0, 128), src[p0:p0 + 128, :])]
            v = src.rearrange("(r c) d -> c r d", c=64)
            c0 = p0 // 64
            return [((0, 64), v[c0]), ((64, 128), v[c0 + 1])]

        def modulation(i, alloc):
            rows = alloc("modrow", [2, 3 * D])
            mb = alloc("modb", [2, 3 * D])
            gp = alloc("gp2", [2, 2, D])
            wrot = Rot(alloc, "modw", [128, 8, 512], 2)
            S.dma(lambda e: e.dma_start(out=mb[:], in_=modb_d[i]), writes=["modb"])
            S.dma(lambda e: e.dma_start(out=gp[:, 0, :], in_=gpre_d[i]), writes=["gp0"])
            S.dma(lambda e: e.dma_start(out=gp[:, 1, :], in_=gpost_d[i]), writes=["gp1"])
            for cb in range(6):
                wt, wk = wrot.get()
                S.dma(lambda e, wt=wt, cb=cb: e.dma_start(out=wt[:], in_=modw_d[i, :, :, cb * 512:(cb + 1) * 512]),
                      writes=[wk])
                pt, pk = psbank()

                def mm(e, wt=wt, pt=pt):
                    for dc in range(8):
                        ins = e.matmul(pt[0:2, :], lhsT=ssil[:, dc, :], rhs=wt[:, dc, :], start=(dc == 0), stop=(dc == 7))
                    return ins
                S.op("pe", mm, reads=[wk, "ssil"], writes=list(pk))
                S.op("dve", lambda e, pt=pt, cb=cb: e.tensor_tensor(out=rows[:, cb * 512:(cb + 1) * 512], in0=pt[0:2, :],
                                                                    in1=mb[:, cb * 512:(cb + 1) * 512], op=ALU.add),
                     reads=list(pk) + ["modb"], writes=[("rows", cb)])
            allrows = [("rows", cb) for cb in range(6)]
            S.op("dve", lambda e: e.scalar_tensor_tensor(out=rows[:, D:2 * D], in0=rows[:, D:2 * D], scalar=1.0,
                                                         in1=gp[:, 0, :], op0=ALU.add, op1=ALU.mult),
                 reads=allrows + ["gp0"], writes=allrows)
            S.op("dve", lambda e: e.tensor_tensor(out=rows[:, 2 * D:3 * D], in0=rows[:, 2 * D:3 * D], in1=gp[:, 1, :],
                                                  op=ALU.mult), reads=allrows + ["gp1"], writes=allrows)
            for wi, w in enumerate(("l", "c")):
                for k, slot in enumerate((1, 0, 2)):
                    for half in range(2):
                        pt, pk = psbank()
                        S.op("pe", lambda e, pt=pt, wi=wi, slot=slot, half=half: e.matmul(
                            pt[:, :], lhsT=sel[:, wi, :], rhs=rows[:, slot * D + half * 512: slot * D + half * 512 + 512],
                            start=True, stop=True), reads=allrows + ["sel"], writes=list(pk))
                        S.op("act", lambda e, pt=pt, w=w, k=k, half=half: e.activation(
                            out=bc[w][k][:, half * 512:(half + 1) * 512], in_=pt[:, :], func=AF.Copy),
                            reads=list(pk), writes=[("bc", w, k, half)])

        def bckeys(w, k):
            return [("bc", w, k, 0), ("bc", w, k, 1)]

        def prenorm_sub(src, seq, t0, sub, col, xt, xk, hdst, hk, tmp_rot, st_rot):
            w = seq
            for (pa, pb), ap in x_rows(src, seq, t0, sub, col):
                S.dma(lambda e, pa=pa, pb=pb, ap=ap: e.dma_start(out=xt[pa:pb, :], in_=ap), writes=[xk])
            junk, jk = tmp_rot.get()
            st, sk = st_rot.get()
            S.op("act", lambda e: e.activation(out=junk[:], in_=xt, func=AF.Square, accum_out=st[:, 0:1]),
                 reads=[xk], writes=[jk, sk])
            S.op("act", lambda e: e.activation(out=st[:, 1:2], in_=st[:, 0:1], func=AF.Sqrt, scale=1.0 / D, bias=1e-6),
                 reads=[sk], writes=[sk])
            S.op("dve", lambda e: e.reciprocal(out=st[:, 2:3], in_=st[:, 1:2]), reads=[sk], writes=[sk])
            S.op("dve", lambda e: e.scalar_tensor_tensor(out=junk[:], in0=xt, scalar=st[:, 2:3], in1=bc[w][0][:],
                                                         op0=ALU.mult, op1=ALU.mult),
                 reads=[xk, sk] + bckeys(w, 0), writes=[jk])
            S.op("dve", lambda e: e.tensor_tensor(out=junk[:], in0=junk[:], in1=bc[w][1][:], op=ALU.add),
                 reads=[jk] + bckeys(w, 1), writes=[jk])
            for half in range(2):
                pt, pk = psbank()

                def tr(e, pt=pt, half=half):
                    for q in range(4):
                        dc = half * 4 + q
                        ins = e.transpose(pt[:, q * 128:(q + 1) * 128], junk[:, dc * 128:(dc + 1) * 128], ident)
                    return ins
                S.op("pe", tr, reads=[jk, "masks"], writes=list(pk))
                for q in range(4):
                    dc = half * 4 + q
                    eng = "act" if q % 2 == 0 else "dve"
                    if eng == "act":
                        S.op("act", lambda e, pt=pt, q=q, dc=dc: e.activation(out=RR(hdst(dc)), in_=pt[:, q * 128:(q + 1) * 128],
                                                                              func=AF.Copy), reads=list(pk), writes=[hk])
                    else:
                        S.op("dve", lambda e, pt=pt, q=q, dc=dc: e.tensor_copy(out=RR(hdst(dc)), in_=pt[:, q * 128:(q + 1) * 128]),
                             reads=list(pk), writes=[hk])

        def outproj_post(G, gkeys, wo, wok, nk, seq, t0, col, xt_tiles, dst, tmp_rot, st_rot):
            w = seq
            for sub in range(NT // 128):
                xt, xk = xt_tiles[sub]
                pts = [psbank(), psbank()]
                for half in range(2):
                    pt, pk = pts[half]

                    def mm(e, pt=pt, half=half, sub=sub):
                        for kc in range(nk):
                            ins = e.matmul(pt[:, :], lhsT=RR(G[:, kc, sub * 128:(sub + 1) * 128]),
                                           rhs=RR(wo[:, kc, half * 512:(half + 1) * 512]), start=(kc == 0), stop=(kc == nk - 1))
                        return ins
                    S.op("pe", mm, reads=list(gkeys) + [wok], writes=list(pk))
                st, sk = st_rot.get()
                junk, jk = tmp_rot.get()
                for half in range(2):
                    pt, pk = pts[half]
                    S.op("act", lambda e, pt=pt, half=half: e.activation(out=junk[:, half * 512:(half + 1) * 512], in_=pt[:, :],
                                                                        func=AF.Square, accum_out=st[:, half:half + 1]),
                         reads=list(pk), writes=[jk, sk])
                S.op("dve", lambda e: e.tensor_tensor(out=st[:, 2:3], in0=st[:, 0:1], in1=st[:, 1:2], op=ALU.add),
                     reads=[sk], writes=[sk])
                S.op("act", lambda e: e.activation(out=st[:, 3:4], in_=st[:, 2:3], func=AF.Sqrt, scale=1.0 / D, bias=1e-6),
                     reads=[sk], writes=[sk])
                S.op("dve", lambda e: e.reciprocal(out=st[:, 4:5], in_=st[:, 3:4]), reads=[sk], writes=[sk])
                for half in range(2):
                    pt, pk = pts[half]
                    S.op("dve", lambda e, pt=pt, half=half: e.scalar_tensor_tensor(
                        out=junk[:, half * 512:(half + 1) * 512], in0=pt[:, :], scalar=st[:, 4:5],
                        in1=bc[w][2][:, half * 512:(half + 1) * 512], op0=ALU.mult, op1=ALU.mult),
                        reads=list(pk) + [sk, ("bc", w, 2, half)], writes=[jk])
                S.op("dve", lambda e, xt=xt: e.tensor_tensor(out=xt, in0=xt, in1=junk[:], op=ALU.add),
                     reads=[xk, jk], writes=[xk])
                for (pa, pb), ap in x_rows(dst, seq, t0, sub, col):
                    S.dma(lambda e, pa=pa, pb=pb, ap=ap, xt=xt: e.dma_start(out=ap, in_=xt[pa:pb, :]), reads=[xk],
                          writes=[("dst", seq, t0, sub, pa)])

        def odd_layer(i, xsrc, csrc, xdst, cdst, last):
            j = i // 2
            col = (j % 2 == 1)
            with contextlib.ExitStack() as ph:
                alloc = mk_alloc(ph)
                modulation(i, alloc)
                S.end_phase()
            with contextlib.ExitStack() as ph:
                alloc = mk_alloc(ph)
                wo = alloc("wo", [128, 16, D])
                dww = alloc("dww", [128, 16, 31])
                ov = alloc("ov", [128, 16, 3])
                for q in range(4):
                    S.dma(lambda e, q=q: e.dma_start(out=RR(wo[:, q * 4:(q + 1) * 4, :]), in_=owout_d[j, :, q * 4:(q + 1) * 4, :]),
                          writes=[("wo", q)], q=WQ)
                wok = ("wo", 0)
                wokeys = [("wo", q) for q in range(4)]
                S.dma(lambda e: e.dma_start(out=dww[:], in_=odww_d[j]), writes=["dww"])
                S.dma(lambda e: e.dma_start(out=ov[:], in_=ovec_d[j]), writes=["ov"])
                hT = alloc("hT", [128, 8, NT])
                A = alloc("A", [128, 16, NT])
                SZ = alloc("SZ", [128, 16, NT])
                xrot = Rot(alloc, "xt", [128, 2, D], 1)
                wrot = Rot(alloc, "wi", [128, 8, 3, 128], 3)
                tmp_rot = Rot(alloc, "tmpw", [128, D], 1)
                st_rot = Rot(alloc, "st", [128, 8], 4)
                sm_rot = Rot(alloc, "sm", [128, NT], 5)
                apad = Rot(alloc, "apad", [128, 384], 3)
                dgp = Rot(alloc, "dg", [128, 128], 18)
                ps_state["nrot"] = 4
                lay = {"cur": None}
                stat = alloc("stat", [128, 3, NT])
                for (seq, t0) in seq_tiles(not last):
                    src = csrc if seq == "c" else xsrc
                    dst = cdst if seq == "c" else xdst
                    seg = NCTX if seq == "c" else 64
                    nseg = NT // seg if seg <= NT else 1
                    seg = min(seg, NT)
                    if lay["cur"] != (seg, nseg):
                        lay["cur"] = (seg, nseg)
                        for t_, k_ in zip(apad.tiles, apad.keys):
                            S.op("act", lambda e, t_=t_: e.activation(out=RR(t_[:]), in_=bc["l"][0][:, 0:384], func=AF.Copy, scale=0.0),
                                 reads=bckeys("l", 0), writes=[k_])
                    xt3, xk3 = xrot.get()
                    xt_tiles = [(xt3[:, sub, :], (xk3, sub)) for sub in range(2)]
                    for sub in range(2):
                        prenorm_sub(src, seq, t0, sub, col, xt_tiles[sub][0], xt_tiles[sub][1],
                                    lambda dc, sub=sub: hT[:, dc, sub * 128:(sub + 1) * 128], "hT", tmp_rot, st_rot)
                    pmean, pmk = accbank(0)
                    pex2, pek = accbank(1)
                    statcnt = {"n": 0}

                    def cc_body(cc, seg=seg, nseg=nseg):
                        wi, wk = wrot.get()
                        S.dma(lambda e: e.dma_start(out=RR(wi[:]), in_=owin_d[j, cc]), writes=[wk], q=WQ)
                        yield

                        def mmg(pt, blk):
                            def mm(e):
                                for dc in range(8):
                                    ins = e.matmul(pt, lhsT=RR(wi[:, dc, blk, :]), rhs=RR(hT[:, dc, :]), start=(dc == 0), stop=(dc == 7))
                                return ins
                            return mm
                        pg, pgk = ps()
                        S.op("pe", mmg(pg, 1), reads=[wk, "hT"], writes=[pgk])
                        sg, sgk = sm_rot.get()
                        S.op("act", lambda e: e.activation(out=sg[:], in_=pg, func=AF.Sigmoid), reads=[pgk], writes=[sgk])
                        yield
                        pu, puk = ps()
                        S.op("pe", mmg(pu, 0), reads=[wk, "hT"], writes=[puk])
                        apt, ak = apad.get()
                        av = apt[:, 0:nseg * (seg + 30)].rearrange("p (s t) -> p s t", s=nseg)
                        S.op("dve", lambda e: e.tensor_tensor(out=RR(av[:, :, 15:15 + seg]), in0=pu.rearrange("p (s t) -> p s t", s=nseg),
                                                              in1=sg[:].rearrange("p (s t) -> p s t", s=nseg), op=ALU.mult),
                             reads=[puk, sgk, ak], writes=[ak])
                        yield
                        pz, pzk = ps()
                        S.op("pe", mmg(pz, 2), reads=[wk, "hT"], writes=[pzk])
                        S.op("act", lambda e: e.activation(out=RR(SZ[:, cc, :]), in_=pz, func=AF.Silu), reads=[pzk], writes=[("SZ", cc)])
                        yield
                        pcv, pcvk = pst[4 + (cc % 2)][:, 0:256], "cvbank%d" % (cc % 2)
                        pcv3 = pcv.rearrange("p (s t) -> p s t", s=nseg)
                        for g0 in range(0, 31, 8):
                            grp = list(range(g0, min(g0 + 8, 31)))
                            dts = []
                            for tap in grp:
                                dgt, dgk = dgp.get()
                                if tap % 4 == 0:
                                    S.op("act", lambda e: e.activation(out=RR(dgt[:]), in_=ident, func=AF.Copy, scale=dww[:, cc, tap:tap + 1]),
                                         reads=["masks", "dww"], writes=[dgk])
                                else:
                                    S.op("dve", lambda e: e.tensor_scalar(out=RR(dgt[:]), in0=ident, scalar1=dww[:, cc, tap:tap + 1], scalar2=None,
                                                                          op0=ALU.mult), reads=["masks", "dww"], writes=[dgk])
                                dts.append((dgt, dgk))

                            def cmm(e):
                                for (dgt_, _), tap in zip(dts, grp):
                                    ins = e.matmul(pcv3, lhsT=RR(dgt_[:]), rhs=RR(av[:, :, tap:tap + seg]), start=(tap == 0), stop=(tap == 30))
                                return ins
                            S.op("pe", cmm, reads=[ak] + [k for _, k in dts], writes=[pcvk])
                            yield
                        S.op("act", lambda e: e.activation(out=A[:, cc, :], in_=pcv, func=AF.Identity, bias=ov[:, cc, 0:1], scale=1.0),
                             reads=[pcvk, "ov"], writes=[("A", cc)])
                        yield
                        sq, sqk = sm_rot.get()
                        S.op("act", lambda e: e.activation(out=sq[:], in_=A[:, cc, :], func=AF.Square), reads=[("A", cc)], writes=[sqk])
                        k_ = statcnt["n"]
                        statcnt["n"] += 1
                        S.op("pe", lambda e: e.matmul(pmean, lhsT=bones[:, 1, :], rhs=A[:, cc, :], start=(k_ == 0), stop=(k_ == 15)),
                             reads=[("A", cc), "bones"], writes=[pmk])
                        S.op("pe", lambda e: e.matmul(pex2, lhsT=bones[:, 1, :], rhs=sq[:], start=(k_ == 0), stop=(k_ == 15)),
                             reads=[sqk, "bones"], writes=[pek])

                    NSTR = 2
                    pending = [cc_body(cc) for cc in range(16)]
                    live = []
                    while pending or live:
                        while pending and len(live) < NSTR:
                            live.append(pending.pop(0))
                        for g_ in list(live):
                            try:
                                next(g_)
                            except StopIteration:
                                live.remove(g_)
                    S.op("act", lambda e: e.activation(out=stat[:, 0, :], in_=pmean, func=AF.Copy), reads=[pmk], writes=["stat"])
                    S.op("dve", lambda e: e.tensor_tensor(out=stat[:, 1, :], in0=stat[:, 0, :], in1=stat[:, 0, :], op=ALU.mult),
                         reads=["stat"], writes=["stat"])
                    S.op("dve", lambda e: e.tensor_tensor(out=stat[:, 1, :], in0=pex2, in1=stat[:, 1, :], op=ALU.subtract),
                         reads=["stat", pek], writes=["stat"])
                    S.op("act", lambda e: e.activation(out=stat[:, 2, :], in_=stat[:, 1, :], func=AF.Sqrt, bias=1e-5, scale=1.0),
                         reads=["stat"], writes=["stat"])
                    S.op("dve", lambda e: e.reciprocal(out=stat[:, 2, :], in_=stat[:, 2, :]), reads=["stat"], writes=["stat"])
                    for cc in range(16):
                        S.op("dve", lambda e, cc=cc: e.tensor_tensor(out=A[:, cc, :], in0=A[:, cc, :], in1=stat[:, 0, :],
                                                                     op=ALU.subtract), reads=[("A", cc), "stat"], writes=[("A", cc)])
                        S.op("dve", lambda e, cc=cc: e.tensor_tensor(out=A[:, cc, :], in0=A[:, cc, :], in1=stat[:, 2, :],
                                                                     op=ALU.mult), reads=[("A", cc), "stat"], writes=[("A", cc)])
                        S.op("act", lambda e, cc=cc: e.activation(out=A[:, cc, :], in_=A[:, cc, :], func=AF.Silu,
                                                                  scale=ov[:, cc, 1:2], bias=ov[:, cc, 2:3]),
                             reads=[("A", cc), "ov"], writes=[("A", cc)])
                        S.op("dve", lambda e, cc=cc: e.tensor_tensor(out=RR(SZ[:, cc, :]), in0=A[:, cc, :], in1=SZ[:, cc, :],
                                                                     op=ALU.mult), reads=[("A", cc), ("SZ", cc)], writes=[("SZ", cc)])
                    outproj_post(SZ, [("SZ", cc) for cc in range(16)] + wokeys[1:], wo, wok, 16, seq, t0, col, xt_tiles, dst,
                                 tmp_rot, st_rot)
                ps_state["nrot"] = 6
                S.end_phase()

        def even_layer(i, xsrc, csrc, xdst, cdst):
            j = i // 2
            col = (j % 2 == 1)
            tiles = seq_tiles(True)
            with contextlib.ExitStack() as ph:
                alloc = mk_alloc(ph)
                modulation(i, alloc)
                S.end_phase()
            with contextlib.ExitStack() as ph:
                alloc = mk_alloc(ph)
                xrot = Rot(alloc, "xt", [128, D], 3)
                hrot = Rot(alloc, "hTs", [128, 8, NT], 2)
                tmp_rot = Rot(alloc, "tmpw", [128, D], 2)
                st_rot = Rot(alloc, "st", [128, 8], 4)
                for (seq, t0) in tiles:
                    src = csrc if seq == "c" else xsrc
                    hT, hk = hrot.get()
                    for sub in range(2):
                        xt, xk = xrot.get()
                        prenorm_sub(src, seq, t0, sub, col, xt[:], xk, lambda dc, sub=sub, hT=hT: hT[:, dc, sub * 128:(sub + 1) * 128],
                                    hk, tmp_rot, st_rot)
                    po = poff(seq, t0)
                    S.dma(lambda e, hT=hT, po=po: e.dma_start(out=hT_d[:, :, po:po + NT], in_=hT[:]), reads=[hk],
                          writes=[("hTd", seq, t0)])
                S.end_phase()
            with contextlib.ExitStack() as ph:
                alloc = mk_alloc(ph)
                ev = alloc("ev", [128, 8, 17])
                muwa = alloc("muwa", [128, 8, 2])
                w1s = alloc("w1s", [128, 2, 8, 128])
                w2s = alloc("w2s", [128, 2, D])
                omka = alloc("omka", [128, 8])
                S.dma(lambda e: e.dma_start(out=ev[:], in_=evec_d[j]), writes=["ev"])
                S.dma(lambda e: e.dma_start(out=muwa[:], in_=emuwa_d[j]), writes=["muwa"])
                for q in range(2):
                    S.dma(lambda e, q=q: e.dma_start(out=RR(w1s[:, q]), in_=ew1_d[j, q]), writes=["w1s"], q=WQ)
                    S.dma(lambda e, q=q: e.dma_start(out=RR(w2s[:, q]), in_=ew2_d[j, q]), writes=["w2s"], q=WQ)
                S.op("dve", lambda e: e.tensor_scalar(out=omka[:], in0=ev[:, :, 8], scalar1=-1.0, scalar2=1.0, op0=ALU.mult,
                                                      op1=ALU.add), reads=["ev"], writes=["omka"])
                hh = alloc("hh", [128, 8, NT + 2])
                dh = alloc("dh", [128, 8, NT])
                smr = Rot(alloc, "smr", [128, NT], 8)
                lo = alloc("lo", [128, 2, NT])
                wrot = Rot(alloc, "wi", [128, 8, 8, 128], 2)
                sm = Rot(alloc, "sm", [128, NT], 38)
                ll = Rot(alloc, "ll", [128, NT], 12)
                q6 = Rot(alloc, "q6", [128, 6, NT], 2)
                vt_rot = Rot(alloc, "vts", [128, 2, 128], 2)
                VEC = dict(mu_r=0, mu_k=1, mu_v=2, w0=3, a0=5, k_k=7, k_a=8, r_k=9, lnw=11, lnb=12, sc=13)
                for (seq, t0) in tiles:
                    po = poff(seq, t0)
                    go = goff(seq, t0)
                    seg = min(NCTX if seq == "c" else 64, NT)
                    nseg = NT // seg
                    S.dma(lambda e, po=po: e.dma_start(out=RR(hh[:]), in_=hT_d[:, :, po - 1:po + NT + 1]), q=WQ,
                          reads=[("hTd", seq, t0), ("hTd", seq, t0 - NT), ("hTd", seq, t0 + NT)] + [("hTdz", c) for c in
                                                                                                     (0, NCTX + 1, NCTX + 2, NCTX + 3 + T)],
                          writes=["hh"])
                    S.op("dve", lambda e: e.tensor_tensor(out=RR(dh[:]), in0=hh[:, :, 0:NT], in1=hh[:, :, 2:NT + 2], op=ALU.add),
                         reads=["hh"], writes=["dh"])
                    S.op("dve", lambda e: e.scalar_tensor_tensor(out=RR(dh[:]), in0=dh[:], scalar=0.5, in1=hh[:, :, 1:NT + 1],
                                                                 op0=ALU.mult, op1=ALU.subtract), reads=["dh", "hh"], writes=["dh"])
                    for q in range(2):
                        pl, plk = ps()
                        for dc in range(8):
                            xw, xwk = smr.get()
                            S.op("dve", lambda e, xw=xw, dc=dc, q=q: e.scalar_tensor_tensor(
                                out=RR(xw[:]), in0=dh[:, dc, :], scalar=muwa[:, dc, q:q + 1], in1=hh[:, dc, 1:NT + 1],
                                op0=ALU.mult, op1=ALU.add), reads=["dh", "hh", "muwa"], writes=[xwk])
                            S.op("pe", lambda e, xw=xw, dc=dc, q=q, pl=pl: e.matmul(pl, lhsT=RR(w1s[:, q, dc, :]), rhs=RR(xw[:]),
                                                                                   start=(dc == 0), stop=(dc == 7)),
                                 reads=[xwk, "w1s"], writes=[plk])
                        S.op("act", lambda e, q=q, pl=pl: e.activation(out=RR(lo[:, q, :]), in_=pl, func=(AF.Tanh if q == 0 else AF.Copy)),
                             reads=[plk], writes=[("lo", q)])
                    def cc_body(cc, go=go, seg=seg, nseg=nseg):
                        wi, wk = wrot.get()
                        for hf in range(2):
                            S.dma(lambda e, wi=wi, cc=cc, hf=hf: e.dma_start(out=RR(wi[:, hf * 4:(hf + 1) * 4]), in_=ewin_d[j, cc, :, hf * 4:(hf + 1) * 4]),
                                  writes=[(wk, hf)], q=WQ)
                        wks = [(wk, 0), (wk, 1)]
                        V = lambda name, k=0, cc=cc: ev[:, cc, VEC[name] + k:VEC[name] + k + 1]

                        def proj(blk, rhs_is_dh=False, wi=wi):
                            pt, pk = ps()

                            def mm(e, pt=pt):
                                for dc in range(8):
                                    ins = e.matmul(pt, lhsT=RR(wi[:, dc, blk, :]), rhs=RR(dh[:, dc, :] if rhs_is_dh else hh[:, dc, 1:NT + 1]),
                                                   start=(dc == 0), stop=(dc == 7))
                                return ins
                            S.op("pe", mm, reads=wks + ["hh", "dh"], writes=[pk])
                            return pt, pk

                        def tt(eng, out, outk, in0, in1, op, rd):
                            S.op(eng, lambda e: e.tensor_tensor(out=out, in0=in0, in1=in1, op=op), reads=rd, writes=[outk])

                        rkv = []
                        for bi in range(3):
                            p1, p1k = proj(bi)
                            p2, p2k = proj(bi, True)
                            t2, t2k = sm.get()
                            S.op("act", lambda e, t2=t2, p2=p2: e.activation(out=t2[:], in_=p2, func=AF.Copy), reads=[p2k], writes=[t2k])
                            o, ok = ll.get()
                            S.op("dve", lambda e, o=o, t2=t2, p1=p1, bi=bi, cc=cc: e.scalar_tensor_tensor(
                                out=o[:], in0=t2[:], scalar=ev[:, cc, bi:bi + 1], in1=p1, op0=ALU.mult, op1=ALU.add),
                                reads=[t2k, p1k, "ev"], writes=[ok])
                            rkv.append((o, ok))
                            yield "A"
                        (rp, rpk), (kp, kpk), (vp, vpk) = rkv
                        pza, pzak = proj(3)
                        sza, szak = sm.get()
                        S.op("act", lambda e, sza=sza, pza=pza: e.activation(out=sza[:], in_=pza, func=AF.Silu), reads=[pzak], writes=[szak])
                        S.dma(lambda e, sza=sza, cc=cc, go=go: e.dma_start(out=misc_d[1, cc, :, go:go + NT], in_=sza[:]), reads=[szak],
                              writes=[("misc", 1, cc, go)])
                        yield "A"
                        pb, pbk = proj(4)
                        pcg, pcgk = proj(5)
                        pxb, pxbk = proj(6)
                        pzb, pzbk = proj(7)
                        cgs, cgsk = sm.get()
                        S.op("act", lambda e, cgs=cgs, pcg=pcg: e.activation(out=cgs[:], in_=pcg, func=AF.Copy), reads=[pcgk], writes=[cgsk])
                        cx, cxk = sm.get()
                        tt("dve", cx[:], cxk, cgs[:], pxb, ALU.mult, [cgsk, pxbk])
                        acc, acck = sm.get()
                        S.op("act", lambda e, acc=acc, cx=cx, cc=cc: e.activation(out=acc[:], in_=cx[:], func=AF.Copy, scale=ev[:, cc, 14:15]),
                             reads=[cxk, "ev"], writes=[acck])

                        def sconv0(e, acc=acc, cx=cx, cc=cc, seg=seg, nseg=nseg):
                            av = cx[:].rearrange("p (s t) -> p s t", s=nseg)
                            ov_ = acc[:].rearrange("p (s t) -> p s t", s=nseg)
                            return e.scalar_tensor_tensor(out=ov_[:, :, 1:seg], in0=av[:, :, 0:seg - 1], scalar=ev[:, cc, 13:14],
                                                          in1=ov_[:, :, 1:seg], op0=ALU.mult, op1=ALU.add)

                        def sconv2(e, acc=acc, cx=cx, cc=cc, seg=seg, nseg=nseg):
                            av = cx[:].rearrange("p (s t) -> p s t", s=nseg)
                            ov_ = acc[:].rearrange("p (s t) -> p s t", s=nseg)
                            return e.scalar_tensor_tensor(out=ov_[:, :, 0:seg - 1], in0=av[:, :, 1:seg], scalar=ev[:, cc, 15:16],
                                                          in1=ov_[:, :, 0:seg - 1], op0=ALU.mult, op1=ALU.add)
                        S.op("dve", sconv0, reads=[cxk, acck, "ev"], writes=[acck])
                        S.op("dve", sconv2, reads=[cxk, acck, "ev"], writes=[acck])
                        tt("dve", acc[:], acck, acc[:], pb, ALU.mult, [acck, pbk])
                        szb, szbk = sm.get()
                        S.op("act", lambda e, szb=szb, pzb=pzb: e.activation(out=szb[:], in_=pzb, func=AF.Silu), reads=[pzbk], writes=[szbk])
                        tt("dve", acc[:], acck, acc[:], szb[:], ALU.mult, [acck, szbk])
                        S.dma(lambda e, acc=acc, cc=cc, go=go: e.dma_start(out=misc_d[2, cc, :, go:go + NT], in_=acc[:]), reads=[acck],
                              writes=[("misc", 2, cc, go)])
                        yield "A"
                        kx, kxk = sm.get()
                        S.op("act", lambda e, kx=kx, kp=kp, cc=cc: e.activation(out=kx[:], in_=kp[:], func=AF.Copy, scale=ev[:, cc, 7:8]),
                             reads=[kpk, "ev"], writes=[kxk])
                        ksq, ksqk = smr.get()
                        kn, knk = sm.get()
                        S.op("act", lambda e, ksq=ksq, kx=kx: e.activation(out=RR(ksq[:]), in_=kx[:], func=AF.Square), reads=[kxk], writes=[ksqk])
                        pss, pssk = ps()
                        S.op("pe", lambda e, pss=pss, ksq=ksq: e.matmul(pss, lhsT=RR(bonesr[:]), rhs=RR(ksq[:]), start=True, stop=True),
                             reads=[ksqk, "bonesr"], writes=[pssk])
                        S.op("dve", lambda e, kn=kn, pss=pss: e.tensor_scalar(out=kn[:], in0=pss, scalar1=1e-24, scalar2=None, op0=ALU.max),
                             reads=[pssk], writes=[knk])
                        S.op("act", lambda e, kn=kn: e.activation(out=kn[:], in_=kn[:], func=AF.Sqrt), reads=[knk], writes=[knk])
                        S.op("dve", lambda e, kn=kn: e.reciprocal(out=kn[:], in_=kn[:]), reads=[knk], writes=[knk])
                        kk, kkk = ll.get()
                        tt("dve", kk[:], kkk, kx[:], kn[:], ALU.mult, [kxk, knk])
                        yield "B"
                        pbn, pbnk = accbank(cc % 2)
                        for z in range(2):
                            Q, Qk = q6.get()
                            plw, plwk = ps()
                            S.op("pe", lambda e, plw=plw, z=z, cc=cc: e.matmul(plw, lhsT=RR(w2s[z * 64:(z + 1) * 64, 0, cc * 128:(cc + 1) * 128]),
                                                                               rhs=RR(lo[z * 64:(z + 1) * 64, 0, :]), start=True, stop=True),
                                 reads=[("lo", 0), "w2s"], writes=[plwk])
                            pla, plak = ps()
                            S.op("pe", lambda e, pla=pla, z=z, cc=cc: e.matmul(pla, lhsT=RR(w2s[z * 64:(z + 1) * 64, 1, cc * 128:(cc + 1) * 128]),
                                                                               rhs=RR(lo[z * 64:(z + 1) * 64, 1, :]), start=True, stop=True),
                                 reads=[("lo", 1), "w2s"], writes=[plak])
                            ld, ldk = sm.get()
                            S.op("act", lambda e, ld=ld, plw=plw, z=z, cc=cc: e.activation(out=ld[:], in_=plw, func=AF.Sigmoid,
                                                                                           bias=ev[:, cc, 3 + z:4 + z], scale=1.0),
                                 reads=[plwk, "ev"], writes=[ldk])
                            asg, asgk = sm.get()
                            S.op("act", lambda e, asg=asg, pla=pla, z=z, cc=cc: e.activation(out=asg[:], in_=pla, func=AF.Sigmoid,
                                                                                             bias=ev[:, cc, 5 + z:6 + z], scale=1.0),
                                 reads=[plak, "ev"], writes=[asgk])
                            yield "B"
                            kd, kdk = sm.get()
                            S.op("act", lambda e, kd=kd, asg=asg, cc=cc: e.activation(out=kd[:], in_=asg[:], func=AF.Identity, scale=ev[:, cc, 8:9],
                                                                                       bias=omka[:, cc:cc + 1]),
                                 reads=[asgk, "ev", "omka"], writes=[kdk])
                            tt("dve", kd[:], kdk, kd[:], kp[:], ALU.mult, [kdk, kpk])
                            b, bk = sm.get()
                            tt("dve", b[:], bk, kk[:], asg[:], ALU.mult, [kkk, asgk])
                            yield "B"
                            ci, cik = sm.get()
                            S.op("dve", lambda e, ci=ci, ld=ld: e.tensor_tensor_scan(out=ci[:], data0=rmask[:], data1=ld[:], initial=0.0,
                                                                                     op0=ALU.mult, op1=ALU.add), reads=[ldk, "rmask"], writes=[cik])
                            nchk = NT // CH
                            v3 = lambda t: t[:].rearrange("p (n c) -> p n c", c=CH)
                            tot, totk = sm.get()
                            S.op("dve", lambda e, tot=tot, ci=ci: e.tensor_copy(out=tot[:, 0:nchk], in_=v3(ci)[:, :, CH - 1]), reads=[cik], writes=[totk])
                            totb = lambda tot=tot: tot[:, 0:nchk].unsqueeze(2).broadcast_to([128, nchk, CH])
                            if z == 1:
                                S.op("dve", lambda e, ci=ci, totb=totb: e.tensor_tensor(out=v3(ci), in0=totb(), in1=v3(ci), op=ALU.subtract),
                                     reads=[cik, totk], writes=[cik])
                                tt("dve", ci[:], cik, ci[:], ld[:], ALU.add, [cik, ldk])
                            ce, cek = sm.get()
                            tt("dve", ce[:], cek, ci[:], ld[:], ALU.subtract, [cik, ldk])
                            chh, chk = sm.get()
                            S.op("dve", lambda e, chh=chh, ci=ci, totb=totb: e.tensor_tensor(out=v3(chh), in0=totb(), in1=v3(ci), op=ALU.subtract),
                                 reads=[cik, totk], writes=[chk])
                            yield "B"
                            epos, eposk = sm.get()
                            eneg, enegk = sm.get()
                            S.op("act", lambda e, epos=epos, ci=ci: e.activation(out=epos[:], in_=ci[:], func=AF.Exp, scale=DECAY_SCALE), reads=[cik], writes=[eposk])
                            S.op("act", lambda e, eneg=eneg, ci=ci: e.activation(out=eneg[:], in_=ci[:], func=AF.Exp, scale=-DECAY_SCALE), reads=[cik], writes=[enegk])
                            S.op("act", lambda e, ce=ce: e.activation(out=ce[:], in_=ce[:], func=AF.Exp, scale=DECAY_SCALE), reads=[cek], writes=[cek])
                            S.op("act", lambda e, chh=chh: e.activation(out=chh[:], in_=chh[:], func=AF.Exp, scale=DECAY_SCALE), reads=[chk], writes=[chk])
                            c0 = go // CH
                            S.op("act", lambda e, tot=tot, z=z, cc=cc, c0=c0: e.activation(out=PCt[0][:, z, cc, c0:c0 + nchk], in_=tot[:, 0:nchk], func=AF.Exp, scale=DECAY_SCALE),
                                 reads=[totk], writes=[("PC", z, cc, c0)])
                            yield "B"
                            S.op("dve", lambda e, Q=Q, kk=kk, ce=ce: e.scalar_tensor_tensor(out=Q[:, 0, :], in0=kk[:], scalar=-1.0, in1=ce[:],
                                                                                            op0=ALU.mult, op1=ALU.mult), reads=[kkk, cek], writes=[(Qk, 0)])
                            tt("dve", Q[:, 1, :], (Qk, 1), rp[:], epos[:], ALU.mult, [rpk, eposk])
                            tt("dve", Q[:, 2, :], (Qk, 2), b[:], eneg[:], ALU.mult, [bk, enegk])
                            tt("dve", Q[:, 3, :], (Qk, 3), kd[:], eneg[:], ALU.mult, [kdk, enegk])
                            tt("dve", Q[:, 4, :], (Qk, 4), b[:], chh[:], ALU.mult, [bk, chk])
                            tt("dve", Q[:, 5, :], (Qk, 5), kd[:], chh[:], ALU.mult, [kdk, chk])
                            S.dma(lambda e, Q=Q, z=z, cc=cc, go=go: e.dma_start(out=scn_d[z, cc, :, :, go:go + NT], in_=Q[:]),
                                  reads=[(Qk, q) for q in range(6)], writes=[("scn", z, cc, go)])
                            yield "B"
                            bz, bzk = smr.get()
                            S.op("dve", lambda e, bz=bz, kd=kd, rp=rp, z=z, cc=cc: e.scalar_tensor_tensor(
                                out=RR(bz[:]), in0=kd[:], scalar=ev[:, cc, 9 + z:10 + z], in1=rp[:], op0=ALU.mult, op1=ALU.mult),
                                reads=[kdk, rpk, "ev"], writes=[bzk])
                            S.op("pe", lambda e, bz=bz, z=z, pbn=pbn: e.matmul(pbn, lhsT=RR(bonesr[:]), rhs=RR(bz[:]), start=(z == 0), stop=(z == 1)),
                                 reads=[bzk, "bonesr"], writes=[pbnk])
                            yield "B"
                        bon, bonk = sm.get()
                        tt("dve", bon[:], bonk, vp[:], pbn, ALU.mult, [vpk, pbnk])
                        S.dma(lambda e, bon=bon, cc=cc, go=go: e.dma_start(out=misc_d[0, cc, :, go:go + NT], in_=bon[:]), reads=[bonk],
                              writes=[("misc", 0, cc, go)])
                        yield "B"
                        ptv, ptvk = ps()

                        def trv(e, ptv=ptv, vp=vp):
                            for n2 in range(NT // 128):
                                ins = e.transpose(ptv[:, n2 * 128:(n2 + 1) * 128], vp[:, n2 * 128:(n2 + 1) * 128], ident)
                            return ins
                        S.op("pe", trv, reads=[vpk, "masks"], writes=[ptvk])
                        vts, vtsk = vt_rot.get()
                        S.op("act", lambda e, vts=vts, ptv=ptv: e.activation(out=vts[:].rearrange("p n c -> p (n c)"), in_=ptv, func=AF.Copy),
                             reads=[ptvk], writes=[vtsk])
                        c0 = go // CH
                        for n in range(NT // CH):
                            S.dma(lambda e, vts=vts, cc=cc, n=n, c0=c0: e.dma_start(
                                out=vtok_d[cc, c0 + n].rearrange("(h s) v -> s h v", h=2),
                                in_=vts[(n % 2) * 64:(n % 2) * 64 + 64, n // 2, :].rearrange("s (h v) -> s h v", h=2)),
                                reads=[vtsk], writes=[("vtok", cc, c0 + n)])

                    pending = [cc_body(cc) for cc in range(8)]
                    live = []
                    while pending or live:
                        if pending and len(live) < 2 and not any(t == "A" for _, t in live):
                            live.append([pending.pop(0), "A"])
                        for ent in list(live):
                            try:
                                ent[1] = next(ent[0])
                            except StopIteration:
                                live.remove(ent)
                S.end_phase()
            for z in range(2):
                for half in range(2):
                    scan_pass(z, half)
            with contextlib.ExitStack() as ph:
                alloc = mk_alloc(ph)
                wo = alloc("wo", [128, 16, D])
                ev = alloc("ev", [128, 8, 17])
                for q in range(4):
                    S.dma(lambda e, q=q: e.dma_start(out=RR(wo[:, q * 4:(q + 1) * 4, :]), in_=ewout_d[j, :, q * 4:(q + 1) * 4, :]),
                          writes=[("wo", q)], q=WQ)
                wokeys = [("wo", q) for q in range(4)]
                S.dma(lambda e: e.dma_start(out=ev[:], in_=evec_d[j]), writes=["ev"])
                G = alloc("G", [128, 16, NT])
                xrot = Rot(alloc, "xt", [128, 2, D], 2)
                tmp_rot = Rot(alloc, "tmpw", [128, D], 2)
                st_rot = Rot(alloc, "st", [128, 8], 4)
                sm = Rot(alloc, "sm", [128, NT], 12)
                for (seq, t0) in tiles:
                    src = csrc if seq == "c" else xsrc
                    dst = cdst if seq == "c" else xdst
                    go = goff(seq, t0)
                    xt3, xk3 = xrot.get()
                    xt_tiles = [(xt3[:, sub, :], (xk3, sub)) for sub in range(2)]
                    for sub in range(2):
                        for (pa, pb), ap in x_rows(src, seq, t0, sub, col):
                            S.dma(lambda e, pa=pa, pb=pb, ap=ap, sub=sub, xt3=xt3: e.dma_start(out=xt3[pa:pb, sub, :], in_=ap),
                                  writes=[(xk3, sub)])
                    def cc_body(cc, go=go):
                        of, ofk = sm.get()
                        ob, obk = sm.get()
                        bn, bnk = sm.get()
                        sz, szk = sm.get()
                        S.dma(lambda e, of=of, cc=cc, go=go: e.dma_start(out=of[:], in_=o_d[0, cc, :, go:go + NT]), writes=[ofk])
                        S.dma(lambda e, ob=ob, cc=cc, go=go: e.dma_start(out=ob[:], in_=o_d[1, cc, :, go:go + NT]), writes=[obk])
                        S.dma(lambda e, bn=bn, cc=cc, go=go: e.dma_start(out=bn[:], in_=misc_d[0, cc, :, go:go + NT]), writes=[bnk])
                        S.dma(lambda e, sz=sz, cc=cc, go=go: e.dma_start(out=sz[:], in_=misc_d[1, cc, :, go:go + NT]), writes=[szk])
                        S.dma(lambda e, cc=cc, go=go: e.dma_start(out=RR(G[:, 8 + cc, :]), in_=misc_d[2, cc, :, go:go + NT]), writes=[("G", 8 + cc)], q=WQ)
                        yield
                        S.op("pool", lambda e, of=of, ob=ob: e.tensor_tensor(out=of[:], in0=of[:], in1=ob[:], op=ALU.add), reads=[ofk, obk], writes=[ofk])
                        pm, pmk = ps()
                        S.op("pe", lambda e, pm=pm, of=of: e.matmul(pm, lhsT=bones[:, 0, :], rhs=of[:], start=True, stop=True),
                             reads=[ofk, "bones"], writes=[pmk])
                        S.op("dve", lambda e, of=of, pm=pm: e.scalar_tensor_tensor(out=of[:], in0=pm, scalar=-1.0 / 64, in1=of[:], op0=ALU.mult,
                                                                                   op1=ALU.add), reads=[ofk, pmk], writes=[ofk])
                        yield
                        S.op("act", lambda e, ob=ob, of=of: e.activation(out=ob[:], in_=of[:], func=AF.Square), reads=[ofk], writes=[obk])
                        pv, pvk = ps()
                        S.op("pe", lambda e, pv=pv, ob=ob: e.matmul(pv, lhsT=bones[:, 0, :], rhs=ob[:], start=True, stop=True),
                             reads=[obk, "bones"], writes=[pvk])
                        S.op("act", lambda e, ob=ob, pv=pv: e.activation(out=ob[:], in_=pv, func=AF.Sqrt, scale=1.0 / 64, bias=64e-5),
                             reads=[pvk], writes=[obk])
                        yield
                        S.op("dve", lambda e, ob=ob: e.reciprocal(out=ob[:], in_=ob[:]), reads=[obk], writes=[obk])
                        S.op("dve", lambda e, of=of, ob=ob: e.tensor_tensor(out=of[:], in0=of[:], in1=ob[:], op=ALU.mult), reads=[ofk, obk], writes=[ofk])
                        S.op("act", lambda e, of=of, cc=cc: e.activation(out=of[:], in_=of[:], func=AF.Identity, scale=ev[:, cc, 11:12],
                                                                         bias=ev[:, cc, 12:13]), reads=[ofk, "ev"], writes=[ofk])
                        yield
                        S.op("pool", lambda e, of=of, bn=bn: e.tensor_tensor(out=of[:], in0=of[:], in1=bn[:], op=ALU.add), reads=[ofk, bnk], writes=[ofk])
                        S.op("dve", lambda e, of=of, sz=sz, cc=cc: e.tensor_tensor(out=RR(G[:, cc, :]), in0=of[:], in1=sz[:], op=ALU.mult),
                             reads=[ofk, szk], writes=[("G", cc)])

                    pending = [cc_body(cc) for cc in range(8)]
                    live = []
                    while pending or live:
                        while pending and len(live) < 2:
                            live.append(pending.pop(0))
                        for g_ in list(live):
                            try:
                                next(g_)
                            except StopIteration:
                                live.remove(g_)
                    outproj_post(G, [("G", k) for k in range(16)] + wokeys[1:], wo, wokeys[0], 16, seq, t0, col, xt_tiles, dst, tmp_rot, st_rot)
                S.end_phase()

        def scan_pass(z, half):
            ctx_t = [("c", t0) for t0 in range(0, NCTX, NT)]
            lat_t = [("l", t0) for t0 in range(0, T, NT)]
            order = ctx_t + lat_t if z == 0 else ctx_t[::-1] + lat_t[::-1]
            MS, MI, ML = (0, 1, 2) if z == 0 else (2, 3, 0)
            with contextlib.ExitStack() as ph:
                alloc = mk_alloc(ph)
                mk2 = alloc("mk2", [128, 2, 128])
                bd6 = alloc("bd6", [128, 6, 128])
                S.op("dve", lambda e: e.tensor_copy(out=mk2[:, 0, :], in_=masks[:, MS, :]), reads=["masks"], writes=["mk2"])
                S.op("dve", lambda e: e.tensor_copy(out=mk2[:, 1, :], in_=masks[:, MI, :]), reads=["masks"], writes=["mk2"])
                for q in range(6):
                    S.op("dve", lambda e, q=q: e.tensor_copy(out=bd6[:, q, :], in_=BD), reads=["masks"], writes=["bd6"])
                opr = Rot(alloc, "opnd", [128, 6, NT], 8)
                vtr = Rot(alloc, "vtk", [128, NT // CH, CH], 8)
                oacc = Rot(alloc, "oacc", [64, 2, NT], 8)
                Hs = [[alloc("H%d_%d" % (p, k), [128, CH]) for k in range(2)] for p in range(4)]
                hcur = [0] * 4
                for p in range(4):
                    S.op("dve", lambda e, p=p: e.tensor_scalar(out=RR(Hs[p][0][:]), in0=masks[:, 0, 0:CH], scalar1=0.0, scalar2=None, op0=ALU.mult),
                         reads=["masks"], writes=[("H", p, 0)])
                blk = Rot(alloc, "blk", [128, 6, 128], 8)
                sc = Rot(alloc, "sc", [128, 2, 128], 16)
                w128 = Rot(alloc, "w128", [128, 128], 44)
                ttf = Rot(alloc, "ttf", [128, 128], 8)
                w64 = Rot(alloc, "w64", [128, CH], 16)
                nchk = NT // CH
                def load_tile(seq, t0):
                    go = goff(seq, t0)
                    loaded = []
                    for p in range(4):
                        cc = half * 4 + p
                        op_, opk = opr.get()
                        vt_, vtk_ = vtr.get()
                        oa_, oak = oacc.get()
                        S.dma(lambda e, op_=op_, cc=cc, go=go: e.dma_start(out=op_[:], in_=scn_d[z, cc, :, :, go:go + NT]),
                              reads=[("scn", z, cc, go)], writes=[opk])
                        c0 = go // CH
                        S.dma(lambda e, vt_=vt_, cc=cc, c0=c0: e.dma_start(out=RR(vt_[:]), in_=vtok_d[cc, c0:c0 + nchk].rearrange("n p v -> p n v")),
                              reads=[("vtok", cc, c0 + n) for n in range(nchk)], writes=[vtk_], q=WQ)
                        loaded.append((op_, opk, vt_, vtk_, oa_, oak))
                    return loaded

                nxt = load_tile(*order[0])
                for ti, (seq, t0) in enumerate(order):
                    go = goff(seq, t0)
                    loaded = nxt
                    if ti + 1 < len(order):
                        nxt = load_tile(*order[ti + 1])
                    chunks = list(range(nchk)) if z == 0 else list(range(nchk))[::-1]
                    for n in chunks:
                        cn = go // CH + n
                        def precompute(p, n=n):
                            cc = half * 4 + p
                            op_, opk, vt_, vtk_, oa_, oak = loaded[p]
                            B6, B6k = blk.get()
                            S.op("dve", lambda e: e.tensor_tensor(
                                out=RR(B6[:].rearrange("p q (h t) -> p q h t", h=2)),
                                in0=op_[:, :, n * CH:(n + 1) * CH].unsqueeze(2).broadcast_to([128, 6, 2, CH]),
                                in1=bd6[:].rearrange("p q (h t) -> p q h t", h=2), op=ALU.mult), reads=[opk, "bd6"], writes=[B6k])
                            yield
                            AR = RR(B6[:, 0:2, :].rearrange("p q t -> p (q t)"))
                            p1, p1k = ps()
                            S.op("pe", lambda e: e.matmul(p1, lhsT=RR(B6[:, 2, :]), rhs=AR, start=True, stop=True), reads=[B6k], writes=[p1k])
                            s1, s1k = sc.get()
                            S.op("dve", lambda e: e.tensor_tensor(out=RR(s1[:].rearrange("p q t -> p (q t)")), in0=p1,
                                                                  in1=mk2[:].rearrange("p q t -> p (q t)"), op=ALU.mult),
                                 reads=[p1k, "mk2"], writes=[s1k])
                            yield
                            p2, p2k = ps()
                            S.op("pe", lambda e: e.matmul(p2, lhsT=RR(B6[:, 3, :]), rhs=AR, start=True, stop=True), reads=[B6k], writes=[p2k])
                            s2, s2k = sc.get()
                            S.op("dve", lambda e: e.tensor_tensor(out=RR(s2[:].rearrange("p q t -> p (q t)")), in0=p2,
                                                                  in1=mk2[:].rearrange("p q t -> p (q t)"), op=ALU.mult),
                                 reads=[p2k, "mk2"], writes=[s2k])
                            yield
                            p3, p3k = ps()
                            S.op("pe", lambda e: e.matmul(p3[:, 0:128], lhsT=RR(B6[:, 0, :]), rhs=RR(B6[:, 2, :]), start=True, stop=True),
                                 reads=[B6k], writes=[p3k])
                            Lc, Lck = w128.get()
                            S.op("dve", lambda e: e.tensor_tensor(out=RR(Lc[:]), in0=p3[:, 0:128], in1=masks[:, ML, :], op=ALU.mult),
                                 reads=[p3k, "masks"], writes=[Lck])
                            Mc, Mck = s1[:, 0, :], s1k
                            acc, acck = w128.get()
                            S.op("pool", lambda e: e.tensor_tensor(out=RR(acc[:]), in0=Mc, in1=ident, op=ALU.add),
                                 reads=[Mck, "masks"], writes=[acck])
                            yield
                            for it in range(5):
                                pL, pLk = ps()
                                S.op("pe", lambda e: e.matmul(pL[:, 0:128], lhsT=RR(Mc), rhs=RR(Lc[:]), start=True, stop=True),
                                     reads=[Mck, Lck], writes=[pLk])
                                Ln, Lnk = w128.get()
                                S.op("act", lambda e: e.activation(out=RR(Ln[:]), in_=pL[:, 0:128], func=AF.Copy), reads=[pLk], writes=[Lnk])
                                yield
                                if it < 4:
                                    pM, pMk = ps()
                                    S.op("pe", lambda e: e.matmul(pM[:, 0:128], lhsT=RR(Lc[:]), rhs=RR(Mc), start=True, stop=True),
                                         reads=[Mck, Lck], writes=[pMk])
                                    Mn, Mnk = w128.get()
                                    S.op("act", lambda e: e.activation(out=RR(Mn[:]), in_=pM[:, 0:128], func=AF.Copy), reads=[pMk], writes=[Mnk])
                                    yield
                                pA, pAk = ps()
                                S.op("pe", lambda e: e.matmul(pA[:, 0:128], lhsT=RR(Ln[:]), rhs=RR(acc[:]), start=True, stop=True),
                                     reads=[Lnk, acck], writes=[pAk])
                                acc2, acc2k = (w128.get() if it < 4 else ttf.get())
                                S.op("dve", lambda e: e.tensor_tensor(out=RR(acc2[:]), in0=pA[:, 0:128], in1=acc[:], op=ALU.add),
                                     reads=[pAk, acck], writes=[acc2k])
                                yield
                                acc, acck = acc2, acc2k
                                Lc, Lck = Ln, Lnk
                                if it < 4:
                                    Mc, Mck = Mn[:], Mnk
                            ptb, ptbk = ps()

                            def trb(e):
                                e.transpose(ptb[:, 0:128], B6[:, 4, :], ident)
                                return e.transpose(ptb[:, 128:256], B6[:, 5, :], ident)
                            S.op("pe", trb, reads=[B6k, "masks"], writes=[ptbk])
                            bkt, bktk = sc.get()
                            S.op("dve", lambda e: e.tensor_tensor(out=RR(bkt[:].rearrange("p q t -> p (q t)")), in0=ptb,
                                                                  in1=bd6[:, 0:2, :].rearrange("p q t -> p (q t)"), op=ALU.mult),
                                 reads=[ptbk, "bd6"], writes=[bktk])
                            stg[p] = dict(B6=B6, B6k=B6k, s1=s1, s1k=s1k, s2=s2, s2k=s2k, TT_=acc, TTk=acck, bkt=bkt, bktk=bktk)

                        stg = [None] * 4
                        gens = [precompute(p) for p in range(4)]
                        live = list(gens)
                        while live:
                            for g_ in list(live):
                                try:
                                    next(g_)
                                except StopIteration:
                                    live.remove(g_)
                        for p in range(4):
                            g = stg[p]
                            op_, opk, vt_, vtk_, oa_, oak = loaded[p]
                            H, Hk = Hs[p][hcur[p]], ("H", p, hcur[p])
                            pX, pXk = ps()

                            def mm1(e, pX=pX, g=g, H=H, vt_=vt_, n=n):
                                e.matmul(pX[:, 0:CH], lhsT=RR(g["B6"][:, 0, :]), rhs=RR(H[:]), start=True, stop=False)
                                return e.matmul(pX[:, 0:CH], lhsT=RR(g["s2"][:, 0, :]), rhs=RR(vt_[:, n, :]), start=False, stop=True)
                            S.op("pe", mm1, reads=[g["B6k"], Hk, g["s2k"], vtk_], writes=[pXk])
                            X, Xk = w64.get()
                            S.op("act", lambda e, X=X, pX=pX: e.activation(out=RR(X[:]), in_=pX[:, 0:CH], func=AF.Copy), reads=[pXk], writes=[Xk])
                            g.update(X=X, Xk=Xk, H=H, Hk=Hk)
                        for p in range(4):
                            g = stg[p]
                            pU, pUk = ps()
                            S.op("pe", lambda e, pU=pU, g=g: e.matmul(pU[:, 0:CH], lhsT=RR(g["TT_"][:]), rhs=RR(g["X"][:]), start=True, stop=True),
                                 reads=[g["TTk"], g["Xk"]], writes=[pUk])
                            U, Uk = w64.get()
                            S.op("act", lambda e, U=U, pU=pU: e.activation(out=RR(U[:]), in_=pU[:, 0:CH], func=AF.Copy), reads=[pUk], writes=[Uk])
                            g.update(U=U, Uk=Uk)
                        for p in range(4):
                            g = stg[p]
                            cc = half * 4 + p
                            op_, opk, vt_, vtk_, oa_, oak = loaded[p]
                            pO, pOk = ps()

                            def mm3(e, pO=pO, g=g, vt_=vt_, n=n):
                                e.matmul(pO[0:64, 0:128], lhsT=RR(g["H"][:]), rhs=RR(g["B6"][:, 1, :]), start=True, stop=False)
                                e.matmul(pO[0:64, 0:128], lhsT=RR(g["U"][:]), rhs=RR(g["s1"][:, 1, :]), start=False, stop=False)
                                return e.matmul(pO[0:64, 0:128], lhsT=RR(vt_[:, n, :]), rhs=RR(g["s2"][:, 1, :]), start=False, stop=True)
                            S.op("pe", mm3, reads=[g["Hk"], g["B6k"], g["Uk"], g["s1k"], g["s2k"], vtk_], writes=[pOk])
                            S.op("act", lambda e, pO=pO, oa_=oa_, n=n: e.activation(
                                out=oa_[:, :, n * CH:(n + 1) * CH], in_=pO[0:64, 0:128].rearrange("p (h t) -> p h t", h=2), func=AF.Copy),
                                reads=[pOk], writes=[oak])
                            pH, pHk = ps()

                            def mm4(e, pH=pH, g=g, vt_=vt_, n=n):
                                e.matmul(pH[:, 0:CH], lhsT=RR(g["bkt"][:, 0, :]), rhs=RR(g["U"][:]), start=True, stop=False)
                                return e.matmul(pH[:, 0:CH], lhsT=RR(g["bkt"][:, 1, :]), rhs=RR(vt_[:, n, :]), start=False, stop=True)
                            S.op("pe", mm4, reads=[g["bktk"], g["Uk"], vtk_], writes=[pHk])
                            nk = 1 - hcur[p]
                            Hn, Hnk = Hs[p][nk], ("H", p, nk)
                            S.op("dve", lambda e, Hn=Hn, g=g, pH=pH, cc=cc, cn=cn: e.scalar_tensor_tensor(
                                out=RR(Hn[:]), in0=g["H"][:], scalar=PCt[0][:, z, cc, cn:cn + 1], in1=pH[:, 0:CH], op0=ALU.mult, op1=ALU.add),
                                reads=[g["Hk"], pHk], writes=[Hnk])
                            hcur[p] = nk
                    for p in range(4):
                        cc = half * 4 + p
                        op_, opk, vt_, vtk_, oa_, oak = loaded[p]
                        S.dma(lambda e, oa_=oa_, cc=cc, go=go: e.dma_start(out=o_d[z, cc, :, go:go + NT].rearrange("(h v) t -> v h t", h=2), in_=oa_[:]),
                              reads=[oak], writes=[("od", z, cc, go)])
                S.end_phase()

        xcur, ccur = x_in, c_in
        for i in range(depth):
            last = (i == depth - 1)
            xdst = y_out if last else xs[i % 2]
            cdst = cs[i % 2]
            if i % 2 == 0:
                with contextlib.ExitStack() as evs:
                    PCt[0] = mk_alloc(evs)("PC", [128, 2, 8, NCH])
                    even_layer(i, xcur, ccur, xdst, cdst)
            else:
                odd_layer(i, xcur, ccur, xdst, cdst, last)
            xcur, ccur = xdst, cdst
        print("sched ops:", S.nops, {e: S.cnt[e] for e in S.ENG})
    return nc


def host_consts():
    idx = np.arange(128)
    r, c = idx[:, None], idx[None, :]
    same = (r // 64) == (c // 64)
    m = np.zeros((128, 6, 128), np.float32)
    m[:, 0] = same & (r < c)
    m[:, 1] = same & (r <= c)
    m[:, 2] = same & (r > c)
    m[:, 3] = same & (r >= c)
    m[:, 4] = same
    m[:, 5] = (r == c)
    sel = np.zeros((2, 2, 128), np.float32)
    sel[0, 0] = 1.0
    sel[1, 1] = 1.0
    bones = np.zeros((128, 2, 128), np.float32)
    bones[:, 0] = same
    bones[:, 1] = 1.0 / 2048
    rmask = np.ones((128, NT), np.float32)
    rmask[:, ::CH] = 0.0
    return dict(masks=m, sel=sel, bones=bones, rmask=rmask)


def fm(v, nchunk):
    v = np.asarray(v)
    if v.ndim == 1:
        return np.ascontiguousarray(v.reshape(nchunk, 128).T)
    return np.ascontiguousarray(np.moveaxis(v.reshape(v.shape[0], nchunk, 128), 0, -1).transpose(1, 0, 2))


def host_weights(inp, depth):
    n_even = (depth + 1) // 2
    n_odd = depth // 2
    w = {}
    w["modw"] = np.ascontiguousarray(inp["mod_w"][:depth].reshape(depth, 8, 128, 3 * D).transpose(0, 2, 1, 3))
    w["modb"] = np.ascontiguousarray(np.repeat(inp["mod_b"][:depth, None, :], 2, axis=1))
    w["gpre"] = np.ascontiguousarray(np.repeat(inp["g_pre"][:depth, None, :], 2, axis=1))
    w["gpost"] = np.ascontiguousarray(np.repeat(inp["g_post"][:depth, None, :], 2, axis=1))
    if n_even:
        wi = inp["ev_w_in"][:n_even]
        wi = wi.reshape(n_even, 8, 128, 8, 8, 128)
        w["ewin"] = np.ascontiguousarray(wi.transpose(0, 4, 2, 1, 3, 5))
        w["ewout"] = np.ascontiguousarray(inp["ev_w_out"][:n_even].reshape(n_even, 16, 128, D).transpose(0, 2, 1, 3))
        w1 = inp["ev_w1"][:n_even].reshape(n_even, 2, 8, 128, 64)
        a1 = inp["ev_a1"][:n_even].reshape(n_even, 2, 8, 128, 64)
        st = np.stack([w1, a1], axis=1)
        w["ew1"] = np.ascontiguousarray(st.transpose(0, 1, 4, 3, 2, 5).reshape(n_even, 2, 128, 8, 128))
        w2 = inp["ev_w2"][:n_even].reshape(n_even, 128, D)
        a2 = inp["ev_a2"][:n_even].reshape(n_even, 128, D)
        w["ew2"] = np.ascontiguousarray(np.stack([w2, a2], axis=1))
        ev = np.zeros((n_even, 128, 8, 17), np.float32)
        for j in range(n_even):
            vecs = [inp["ev_mu_rkv"][j, 0], inp["ev_mu_rkv"][j, 1], inp["ev_mu_rkv"][j, 2], inp["ev_w0"][j, 0], inp["ev_w0"][j, 1],
                    inp["ev_a0"][j, 0], inp["ev_a0"][j, 1], inp["ev_k_k"][j], inp["ev_k_a"][j], inp["ev_r_k"][j, 0].reshape(-1),
                    inp["ev_r_k"][j, 1].reshape(-1), inp["ev_lnx_w"][j], inp["ev_lnx_b"][j], inp["ev_sc_w"][j, 0], inp["ev_sc_w"][j, 1],
                    inp["ev_sc_w"][j, 2]]
            for k, v in enumerate(vecs):
                ev[j, :, :, k] = v.reshape(8, 128).T
        w["evec"] = ev
        mw = inp["ev_mu_wa"][:n_even]
        w["emuwa"] = np.ascontiguousarray(mw.reshape(n_even, 2, 8, 128).transpose(0, 3, 2, 1))
    if n_odd:
        wi = inp["od_w_in"][:n_odd].reshape(n_odd, 8, 128, 3, 16, 128)
        w["owin"] = np.ascontiguousarray(wi.transpose(0, 4, 2, 1, 3, 5))
        w["owout"] = np.ascontiguousarray(inp["od_w_out"][:n_odd].reshape(n_odd, 16, 128, D).transpose(0, 2, 1, 3))
        dw = inp["od_dw_w"][:n_odd]
        w["odww"] = np.ascontiguousarray(dw.reshape(n_odd, 31, 16, 128).transpose(0, 3, 2, 1))
        ov = np.stack([inp["od_dw_b"][:n_odd], inp["od_ln_w"][:n_odd], inp["od_ln_b"][:n_odd]], axis=-1)
        w["ovec"] = np.ascontiguousarray(ov.reshape(n_odd, 16, 128, 3).transpose(0, 2, 1, 3))
    return w


_CACHE = {}


def run(inputs, depth, ncores, T, NCTX, trace=False):
    inp = {k: np.asarray(v, dtype=np.float32) for k, v in inputs.items()}
    key = (T, NCTX, depth)
    if key not in _CACHE:
        _CACHE[key] = build(T, NCTX, depth)
    nc = _CACHE[key]
    shared = host_consts()
    shared.update(host_weights(inp, depth))
    in_maps = []
    for b in range(ncores):
        m = dict(shared)
        m["x"] = np.ascontiguousarray(inp["x"][b])
        m["ctx"] = np.ascontiguousarray(inp["ctx"][b])
        sv = np.stack([inp["c"][b], inp["c_ctx"]], axis=-1)
        m["svec"] = np.ascontiguousarray(sv.reshape(8, 128, 2).transpose(1, 0, 2))
        in_maps.append(m)
    res = run_bass_kernel_spmd(nc, in_maps, core_ids=list(range(ncores)), trace=trace)
    out = np.stack([np.asarray(r["y"], dtype=np.float32) for r in res.results], axis=0)
    return out, res


def kernel(**inputs):
    out, _ = run(inputs, 4, 8, 4096, 256)
    return out
```
